# Optimizing a Trainium2 kernel written in Bass

```python
import math
import jax, jax.numpy as jnp
from jax import lax
import numpy as np

D_MODEL = 1024
BATCH = 4
SEQ = 8192
DEPTH = 4

MEM_TOKENS = 256
N_MIXERS = 2
MIX_WIDTH = 2 * D_MODEL
XATTN_WIDTH = MIX_WIDTH // 4
TOKEN_WIDTH = MIX_WIDTH - XATTN_WIDTH
XATTN_HEADS = 4
XATTN_HEAD_DIM = XATTN_WIDTH // XATTN_HEADS
S5_GROUP = 16
S5_STATE = 64
S5_GROUPS = TOKEN_WIDTH // S5_GROUP
GDN_HEAD_DIM = 128
GDN_V_HEADS = TOKEN_WIDTH // GDN_HEAD_DIM
GDN_QK_HEADS = GDN_V_HEADS // 2
GDN_QK_WIDTH = GDN_QK_HEADS * GDN_HEAD_DIM
CONV_WIDTH = 4
CHUNK = 64
NORM_EPS = 1e-6
S5_IN = 2 * TOKEN_WIDTH + 2 * XATTN_WIDTH
GDN_IN = 2 * GDN_QK_WIDTH + TOKEN_WIDTH + 2 * GDN_V_HEADS + TOKEN_WIDTH + 2 * XATTN_WIDTH

kernel_name = "hybrid_s5_gdn_xattn_trunk"


def rmsnorm(x, w):
    xf = x.astype(jnp.float32)
    xf = xf * lax.rsqrt(jnp.mean(xf * xf, axis=-1, keepdims=True) + NORM_EPS)
    return xf.astype(x.dtype) * w


def l2norm(x):
    xf = x.astype(jnp.float32)
    return xf * lax.rsqrt(jnp.sum(xf * xf, axis=-1, keepdims=True) + NORM_EPS)


def causal_conv(x, w):
    k = w.shape[0]
    length = x.shape[1]
    xp = jnp.pad(x, ((0, 0), (k - 1, 0), (0, 0)))
    return sum(xp[:, j:j + length] * w[j] for j in range(k))


def cross_attn(q, mem_h, w_kv):
    bsz, length, _ = q.shape
    kv = mem_h @ w_kv
    k, v = jnp.split(kv, 2, axis=-1)
    q = q.reshape(bsz, length, XATTN_HEADS, XATTN_HEAD_DIM)
    k = k.reshape(bsz, -1, XATTN_HEADS, XATTN_HEAD_DIM)
    v = v.reshape(bsz, -1, XATTN_HEADS, XATTN_HEAD_DIM)
    s = jnp.einsum('blhd,bmhd->bhlm', q, k).astype(jnp.float32) * (XATTN_HEAD_DIM ** -0.5)
    p = jax.nn.softmax(s, axis=-1).astype(v.dtype)
    o = jnp.einsum('bhlm,bmhd->blhd', p, v)
    return o.reshape(bsz, length, XATTN_WIDTH)


def _s5_combine(c1, c2):
    a1r, a1i, b1r, b1i = c1
    a2r, a2i, b2r, b2i = c2
    ar = a1r * a2r - a1i * a2i
    ai = a1r * a2i + a1i * a2r
    br = a2r * b1r - a2i * b1i + b2r
    bi = a2r * b1i + a2i * b1r + b2i
    return (ar, ai, br, bi)


def s5_mixer(u, lam_re, lam_im, log_step, b_re, b_im, c_re, c_im, d, w_glu, b_glu):
    bsz, length, _ = u.shape
    f32 = jnp.float32
    uf = u.astype(f32)
    ug = uf.reshape(bsz, length, S5_GROUPS, S5_GROUP)
    dt = jnp.exp(log_step.astype(f32))[:, None]
    lr, li = lam_re.astype(f32), lam_im.astype(f32)
    mag = jnp.exp(lr * dt)
    ar, ai = mag * jnp.cos(li * dt), mag * jnp.sin(li * dt)
    den = lr * lr + li * li
    nr, ni = ar - 1.0, ai
    cr, ci = (nr * lr + ni * li) / den, (ni * lr - nr * li) / den
    br, bi = b_re.astype(f32), b_im.astype(f32)
    bbar_re = cr[..., None] * br - ci[..., None] * bi
    bbar_im = cr[..., None] * bi + ci[..., None] * br
    bu_re = jnp.einsum('blgh,gph->blgp', ug, bbar_re)
    bu_im = jnp.einsum('blgh,gph->blgp', ug, bbar_im)
    a_re = jnp.broadcast_to(ar, (1, length, S5_GROUPS, S5_STATE))
    a_im = jnp.broadcast_to(ai, (1, length, S5_GROUPS, S5_STATE))
    _, _, x_re, x_im = lax.associative_scan(_s5_combine, (a_re, a_im, bu_re, bu_im), axis=1)
    y = (jnp.einsum('blgp,ghp->blgh', x_re, c_re.astype(f32))
         - jnp.einsum('blgp,ghp->blgh', x_im, c_im.astype(f32)))
    y = y.reshape(bsz, length, TOKEN_WIDTH) + d.astype(f32) * uf
    y = jax.nn.gelu(y)
    y = y * jax.nn.sigmoid(y @ w_glu.astype(f32) + b_glu.astype(f32))
    return y.astype(u.dtype)


def gated_delta_chunked(q, k, v, g, beta):
    f32 = jnp.float32
    bsz, length, heads, dk = k.shape
    dv = v.shape[-1]
    n = length // CHUNK

    def to_chunks(t):
        return t.astype(f32).reshape(bsz, n, CHUNK, heads, -1).transpose(1, 0, 3, 2, 4)

    q, k, v = to_chunks(q), to_chunks(k), to_chunks(v)
    g = g.astype(f32).reshape(bsz, n, CHUNK, heads).transpose(1, 0, 3, 2)
    beta = beta.astype(f32).reshape(bsz, n, CHUNK, heads).transpose(1, 0, 3, 2)
    g = jnp.cumsum(g, axis=-1)
    k_beta = k * beta[..., None]
    v_beta = v * beta[..., None]
    causal = jnp.tril(jnp.ones((CHUNK, CHUNK), dtype=bool))
    strict = jnp.tril(jnp.ones((CHUNK, CHUNK), dtype=bool), -1)
    decay = jnp.exp(jnp.where(causal, g[..., :, None] - g[..., None, :], -jnp.inf))
    l_mat = jnp.where(strict, jnp.einsum('nbhid,nbhjd->nbhij', k_beta, k) * decay, 0.0)
    eye = jnp.broadcast_to(jnp.eye(CHUNK, dtype=f32), l_mat.shape)
    t_mat = lax.linalg.triangular_solve(l_mat, eye, left_side=True, lower=True, unit_diagonal=True)
    u = jnp.einsum('nbhij,nbhjd->nbhid', t_mat, v_beta)
    w = jnp.einsum('nbhij,nbhjd->nbhid', t_mat, k_beta * jnp.exp(g)[..., None])
    intra = jnp.where(causal, jnp.einsum('nbhid,nbhjd->nbhij', q, k) * decay, 0.0)

    def step(state, xs):
        q_c, k_c, u_c, w_c, g_c, a_c = xs
        v_new = u_c - jnp.einsum('bhcd,bhde->bhce', w_c, state)
        o = (jnp.einsum('bhcd,bhde->bhce', q_c * jnp.exp(g_c)[..., None], state)
             + jnp.einsum('bhij,bhje->bhie', a_c, v_new))
        g_last = g_c[..., -1]
        k_dec = k_c * jnp.exp(g_last[..., None] - g_c)[..., None]
        state = state * jnp.exp(g_last)[..., None, None] + jnp.einsum('bhcd,bhce->bhde', k_dec, v_new)
        return state, o

    s0 = jnp.zeros((bsz, heads, dk, dv), dtype=f32)
    _, o = lax.scan(step, s0, (q, k, u, w, g, intra))
    return o.transpose(1, 0, 3, 2, 4).reshape(bsz, length, heads, dv)


def gdn_mixer(qkv, a, b, gate, conv_w, a_log, dt_bias, norm_w):
    bsz, length, _ = qkv.shape
    f32 = jnp.float32
    qkv = jax.nn.silu(causal_conv(qkv, conv_w))
    q, k, v = jnp.split(qkv, [GDN_QK_WIDTH, 2 * GDN_QK_WIDTH], axis=-1)
    rep = GDN_V_HEADS // GDN_QK_HEADS
    q = jnp.repeat(l2norm(q.reshape(bsz, length, GDN_QK_HEADS, GDN_HEAD_DIM)), rep, axis=2) * (GDN_HEAD_DIM ** -0.5)
    k = jnp.repeat(l2norm(k.reshape(bsz, length, GDN_QK_HEADS, GDN_HEAD_DIM)), rep, axis=2)
    v = v.reshape(bsz, length, GDN_V_HEADS, GDN_HEAD_DIM)
    g = -jnp.exp(a_log.astype(f32)) * jax.nn.softplus(a.astype(f32) + dt_bias.astype(f32))
    beta = jax.nn.sigmoid(b.astype(f32))
    o = gated_delta_chunked(q, k, v, g, beta).astype(qkv.dtype)
    o = rmsnorm(o, norm_w).reshape(bsz, length, TOKEN_WIDTH)
    return o * jax.nn.silu(gate)


def setup_inputs(seed: int = 0) -> dict:
    key = jax.random.key(seed)
    ks = jax.random.split(key, 24)
    n_s5 = (DEPTH + 1) // 2
    n_gdn = DEPTH // 2
    nrm = jax.random.normal
    f32 = jnp.float32
    x = nrm(ks[0], (BATCH, SEQ, D_MODEL), f32)
    mem = nrm(ks[1], (BATCH, MEM_TOKENS, D_MODEL), f32)
    norm_w = 1.0 + 0.02 * nrm(ks[2], (DEPTH, D_MODEL), f32)
    w_out = nrm(ks[3], (DEPTH, MIX_WIDTH, D_MODEL), f32) * MIX_WIDTH ** -0.5
    mem_norm_w = 1.0 + 0.02 * nrm(ks[4], (DEPTH, D_MODEL), f32)
    w_mem_kv = nrm(ks[5], (DEPTH, D_MODEL, 2 * XATTN_WIDTH), f32) * D_MODEL ** -0.5
    s5_w_in = nrm(ks[6], (n_s5, D_MODEL, S5_IN), f32) * D_MODEL ** -0.5
    s5_lambda_re = -0.5 + 0.01 * nrm(ks[7], (n_s5, S5_GROUPS, S5_STATE), f32)
    s5_lambda_im = (math.pi * jnp.arange(S5_STATE, dtype=f32)
                    + 0.01 * nrm(ks[8], (n_s5, S5_GROUPS, S5_STATE), f32))
    s5_log_step = jax.random.uniform(ks[9], (n_s5, S5_GROUPS), f32, math.log(1e-3), math.log(1e-1))
    s5_b_re = nrm(ks[10], (n_s5, S5_GROUPS, S5_STATE, S5_GROUP), f32) * (2 * S5_GROUP) ** -0.5
    s5_b_im = nrm(ks[11], (n_s5, S5_GROUPS, S5_STATE, S5_GROUP), f32) * (2 * S5_GROUP) ** -0.5
    s5_c_re = nrm(ks[12], (n_s5, S5_GROUPS, S5_GROUP, S5_STATE), f32) * S5_STATE ** -0.5
    s5_c_im = nrm(ks[13], (n_s5, S5_GROUPS, S5_GROUP, S5_STATE), f32) * S5_STATE ** -0.5
    s5_d = nrm(ks[14], (n_s5, TOKEN_WIDTH), f32)
    s5_w_glu = nrm(ks[15], (n_s5, TOKEN_WIDTH, TOKEN_WIDTH), f32) * TOKEN_WIDTH ** -0.5
    s5_b_glu = 0.01 * nrm(ks[16], (n_s5, TOKEN_WIDTH), f32)
    gdn_w_in = nrm(ks[17], (n_gdn, D_MODEL, GDN_IN), f32) * D_MODEL ** -0.5
    gdn_conv_w = nrm(ks[18], (n_gdn, CONV_WIDTH, 2 * GDN_QK_WIDTH + TOKEN_WIDTH), f32) * CONV_WIDTH ** -0.5
    gdn_a_log = jnp.log(jax.random.uniform(ks[19], (n_gdn, GDN_V_HEADS), f32, 1.0, 16.0))
    dt = jnp.exp(jax.random.uniform(ks[20], (n_gdn, GDN_V_HEADS), f32, math.log(1e-3), math.log(1e-1)))
    gdn_dt_bias = dt + jnp.log(-jnp.expm1(-dt))
    gdn_norm_w = 1.0 + 0.02 * nrm(ks[21], (n_gdn, GDN_HEAD_DIM), f32)
    final_norm_w = 1.0 + 0.02 * nrm(ks[22], (D_MODEL,), f32)
    return {"x": x, "mem": mem, "norm_w": norm_w, "w_out": w_out, "mem_norm_w": mem_norm_w,
            "w_mem_kv": w_mem_kv, "s5_w_in": s5_w_in, "s5_lambda_re": s5_lambda_re,
            "s5_lambda_im": s5_lambda_im, "s5_log_step": s5_log_step, "s5_b_re": s5_b_re,
            "s5_b_im": s5_b_im, "s5_c_re": s5_c_re, "s5_c_im": s5_c_im, "s5_d": s5_d,
            "s5_w_glu": s5_w_glu, "s5_b_glu": s5_b_glu, "gdn_w_in": gdn_w_in,
            "gdn_conv_w": gdn_conv_w, "gdn_a_log": gdn_a_log, "gdn_dt_bias": gdn_dt_bias,
            "gdn_norm_w": gdn_norm_w, "final_norm_w": final_norm_w}


def reference(x, mem, norm_w, w_out, mem_norm_w, w_mem_kv, s5_w_in, s5_lambda_re, s5_lambda_im,
              s5_log_step, s5_b_re, s5_b_im, s5_c_re, s5_c_im, s5_d, s5_w_glu, s5_b_glu,
              gdn_w_in, gdn_conv_w, gdn_a_log, gdn_dt_bias, gdn_norm_w, final_norm_w):
    for i in range(DEPTH):
        j = i // N_MIXERS
        h = rmsnorm(x, norm_w[i])
        mem_h = rmsnorm(mem, mem_norm_w[i])
        if i % N_MIXERS == 0:
            proj = h @ s5_w_in[j]
            u, gate_mix, q_x, gate_x = jnp.split(
                proj, [TOKEN_WIDTH, 2 * TOKEN_WIDTH, 2 * TOKEN_WIDTH + XATTN_WIDTH], axis=-1)
            y_mix = s5_mixer(u, s5_lambda_re[j], s5_lambda_im[j], s5_log_step[j], s5_b_re[j],
                             s5_b_im[j], s5_c_re[j], s5_c_im[j], s5_d[j], s5_w_glu[j],
                             s5_b_glu[j]) * jax.nn.silu(gate_mix)
        else:
            proj = h @ gdn_w_in[j]
            o1 = 2 * GDN_QK_WIDTH + TOKEN_WIDTH
            o2 = o1 + GDN_V_HEADS
            o3 = o2 + GDN_V_HEADS
            o4 = o3 + TOKEN_WIDTH
            o5 = o4 + XATTN_WIDTH
            qkv, a, b, gate_mix, q_x, gate_x = jnp.split(proj, [o1, o2, o3, o4, o5], axis=-1)
            y_mix = gdn_mixer(qkv, a, b, gate_mix, gdn_conv_w[j], gdn_a_log[j], gdn_dt_bias[j],
                              gdn_norm_w[j])
        y_x = cross_attn(q_x, mem_h, w_mem_kv[i]) * jax.nn.silu(gate_x)
        x = x + jnp.concatenate([y_mix, y_x], axis=-1) @ w_out[i]
    return rmsnorm(x, final_norm_w)
```

```python
import math
from contextlib import ExitStack

import numpy as np
import concourse.bass as bass
import concourse.mybir as mybir
from concourse.bass_utils import run_bass_kernel_spmd

F32 = mybir.dt.float32
BF16 = mybir.dt.bfloat16
AF = mybir.ActivationFunctionType
ALU = mybir.AluOpType
AX = mybir.AxisListType

EPOCH = 20000
DEBUG_NAMES = None
SAME_ENGINE_SYNC = True
N_DMA_SEMS = 32
N_SW_SEMS = 8
ENGINES = ("tensor", "vector", "scalar", "gpsimd", "sync")


class Buf:
    __slots__ = ("name", "w", "rs")

    def __init__(self, name):
        self.name = name
        self.w = None
        self.rs = []


class Op:
    __slots__ = ("eng", "fn", "reads", "writes", "dma", "waits", "signal", "tok", "idx", "dsem", "src")

    def __init__(self, eng, fn, reads, writes, dma):
        self.eng = eng
        self.fn = fn
        self.reads = reads
        self.writes = writes
        self.dma = dma
        self.waits = []
        self.signal = False
        self.tok = None
        self.dsem = None


class Prog:
    def __init__(self, nc):
        self.nc = nc
        self.ops = []

    def op(self, eng, fn, reads=(), writes=(), dma=False):
        o = Op(eng, fn, tuple(reads), tuple(writes), dma)
        o.idx = len(self.ops)
        import sys as _s
        fr = _s._getframe(1)
        o.src = []
        while fr is not None and len(o.src) < 4:
            o.src.append(fr.f_lineno)
            fr = fr.f_back
        self.ops.append(o)
        return o

    def mm(self, fn, reads, writes):
        return self.op("tensor", fn, reads, writes)

    def dve(self, fn, reads, writes):
        return self.op("vector", fn, reads, writes)

    def act(self, fn, reads, writes):
        return self.op("scalar", fn, reads, writes)

    def pool(self, fn, reads, writes):
        return self.op("gpsimd", fn, reads, writes)

    def dma(self, eng, out, in_, reads, writes):
        return self.op(eng, lambda e: e.dma_start(out=out, in_=in_), reads, writes, dma=True)

    def barrier(self):
        for e in ENGINES:
            o = self.op(e, None)
            o.dsem = "barrier"

    def analyse(self):
        ops = self.ops
        last_real = {}
        last_dmas = []
        seen = {e: {} for e in ENGINES}
        dma_rr = 0
        sw_rr = 0
        dma_last = [None] * N_DMA_SEMS
        for o in ops:
            deps = set()
            for b in o.reads:
                if b.w is not None:
                    deps.add(b.w)
            for b in o.writes:
                if b.w is not None:
                    deps.add(b.w)
                for r in b.rs:
                    deps.add(r)
            if o.dsem == "barrier":
                for e2, i2 in last_real.items():
                    if e2 != o.eng:
                        deps.add(i2)
                for i2 in last_dmas:
                    deps.add(i2)
            if o.dma:
                if o.eng == "gpsimd":
                    k = N_DMA_SEMS - N_SW_SEMS + sw_rr % N_SW_SEMS
                    sw_rr += 1
                else:
                    k = dma_rr % (N_DMA_SEMS - N_SW_SEMS)
                    dma_rr += 1
                o.dsem = k
                if dma_last[k] is not None:
                    deps.add(dma_last[k])
                dma_last[k] = o.idx
            sn = seen[o.eng]
            best = {}
            dl = []
            for d in deps:
                p = ops[d]
                if p.dma:
                    dl.append(d)
                elif best.get(p.eng, -1) < d:
                    best[p.eng] = d
            for d in sorted(dl + list(best.values())):
                p = ops[d]
                if p.dma:
                    key = ("dma", p.idx)
                    if key in sn:
                        continue
                    sn[key] = True
                    o.waits.append(d)
                else:
                    if p.eng == o.eng and (p.eng == "tensor" or not SAME_ENGINE_SYNC):
                        continue
                    if sn.get(p.eng, -1) >= d:
                        continue
                    sn[p.eng] = d
                    p.signal = True
                    o.waits.append(d)
            if o.fn is None:
                continue
            if o.dma:
                last_dmas.append(o.idx)
                if len(last_dmas) > N_DMA_SEMS:
                    last_dmas.pop(0)
            else:
                last_real[o.eng] = o.idx
            for b in o.reads:
                b.rs.append(o.idx)
            for b in o.writes:
                b.w = o.idx
                b.rs = []
        cnt = {e: 0 for e in ENGINES}
        dcnt = [0] * N_DMA_SEMS
        self.n_epochs = {e: 1 for e in ENGINES}
        for o in ops:
            if o.dma:
                dcnt[o.dsem] += 16
                o.tok = ("d", o.dsem, dcnt[o.dsem])
            elif o.signal:
                c = cnt[o.eng]
                cnt[o.eng] += 1
                ep = c // EPOCH
                self.n_epochs[o.eng] = max(self.n_epochs[o.eng], ep + 1)
                o.tok = ("e", o.eng, ep, c % EPOCH + 1)

    def emit(self):
        nc = self.nc
        self.analyse()
        with ExitStack() as es:
            esem = {}
            for e in ENGINES:
                esem[e] = [es.enter_context(nc.semaphore(f"s_{e}_{i}")) for i in range(self.n_epochs[e])]
            dsem = [es.enter_context(nc.semaphore(f"s_dma_{i}")) for i in range(N_DMA_SEMS)]
            block = es.enter_context(nc.Block())
            by_eng = {e: [o for o in self.ops if o.eng == e] for e in ENGINES}
            ops = self.ops

            def run(eng_name):
                def body(eng):
                    for o in by_eng[eng_name]:
                        for d in o.waits:
                            t = ops[d].tok
                            if t[0] == "d":
                                eng.wait_ge(dsem[t[1]], t[2])
                            else:
                                eng.wait_ge(esem[t[1]][t[2]], t[3])
                        if o.fn is None:
                            continue
                        try:
                            inst = o.fn(eng)
                        except Exception:
                            print("FAILED OP at lines", o.src, "engine", o.eng)
                            raise
                        if DEBUG_NAMES is not None:
                            try:
                                DEBUG_NAMES.append((str(getattr(inst, "name", None) or getattr(getattr(inst, "ins", None), "name", None)), o.src, o.eng))
                            except Exception:
                                pass
                        if o.dma:
                            inst.then_inc(dsem[o.dsem], 16)
                        elif o.signal:
                            inst.then_inc(esem[o.eng][o.tok[2]], 1)
                return body

            block.tensor(run("tensor"))
            block.vector(run("vector"))
            block.scalar(run("scalar"))
            block.gpsimd(run("gpsimd"))
            block.sync(run("sync"))


D = 1024
KT = 8
MEM = 256
TW = 1536
XW = 512
S5_IN = 4096
GDN_IN = 5656
TT = 512
NCH = 64
EPS = 1e-6
ISQ = 1.0 / math.sqrt(128.0)


class Builder:
    def __init__(self, L, layers=(0, 1, 2, 3), do_final=True, mix=True):
        self.L = L
        self.n_tiles = L // TT
        self.layers = tuple(layers)
        self.do_final = do_final
        self.mix = mix
        self.truncate = None
        self.nc = bass.Bass("TRN2", target_bir_lowering=False)
        self.P = Prog(self.nc)
        self.es = ExitStack()
        self.bufs = {}

    def din(self, name, shape):
        return self.nc.dram_tensor(name, list(shape), F32, kind="ExternalInput").ap()

    def dscr(self, name, shape, dt=BF16):
        return self.nc.dram_tensor(name, list(shape), dt, kind="Internal").ap()

    def sb(self, name, shape, dt):
        return self.es.enter_context(self.nc.sbuf_tensor(name, list(shape), dt))

    def ps(self, name, shape, dt=F32):
        return self.es.enter_context(self.nc.psum_tensor(name, list(shape), dt))

    def B(self, name):
        b = self.bufs.get(name)
        if b is None:
            b = self.bufs[name] = Buf(name)
        return b

    def MM(self, out, lhsT, rhs, start, stop, reads, writes, **kw):
        self.P.mm(lambda e: e.matmul(out, lhsT, rhs, start=start, stop=stop, **kw), reads, writes)

    def TR(self, out, in_, ident, reads, writes):
        self.P.mm(lambda e: e.transpose(out, in_, ident), reads, writes)

    def ACT(self, out, in_, func, reads, writes, **kw):
        self.P.act(lambda e: e.activation(out=out, in_=in_, func=func, **kw), reads, writes)

    def TTo(self, eng, out, in0, in1, op, reads, writes):
        self.P.op(eng, lambda e: e.tensor_tensor(out=out, in0=in0, in1=in1, op=op), reads, writes)

    def TS(self, eng, out, in0, s1, s2, op0, op1, reads, writes):
        if op1 is None:
            self.P.op(eng, lambda e: e.tensor_scalar(out=out, in0=in0, scalar1=s1, scalar2=None, op0=op0), reads, writes)
        else:
            self.P.op(eng, lambda e: e.tensor_scalar(out=out, in0=in0, scalar1=s1, scalar2=s2, op0=op0, op1=op1), reads, writes)

    def STT(self, eng, out, in0, scalar, in1, op0, op1, reads, writes):
        self.P.op(eng, lambda e: e.scalar_tensor_tensor(out=out, in0=in0, scalar=scalar, in1=in1, op0=op0, op1=op1),
                  reads, writes)

    def CP(self, eng, out, in_, reads, writes):
        if eng == "scalar":
            self.P.op(eng, lambda e: e.activation(out=out, in_=in_, func=AF.Copy), reads, writes)
        else:
            self.P.op(eng, lambda e: e.tensor_copy(out=out, in_=in_), reads, writes)

    def MS(self, eng, ap, val, writes):
        self.P.op(eng, lambda e: e.memset(ap, val), [], writes)

    def RCP(self, out, in_, reads, writes):
        self.P.dve(lambda e: e.reciprocal(out=out, in_=in_), reads, writes)

    def DMA(self, eng, out, in_, reads, writes):
        self.P.dma(eng, out, in_, reads, writes)

    def declare(self):
        L = self.L
        d = self.dr = {}
        d["xT"] = self.din("xT", [D, L])
        d["memT"] = self.din("memT", [D, MEM])
        d["nw"] = self.din("nw", [128, 4, KT])
        d["mnw"] = self.din("mnw", [128, 4, KT])
        d["fnw"] = self.din("fnw", [128, KT])
        d["win_s5"] = self.din("win_s5", [2, 128, KT, S5_IN])
        d["win_gdn"] = self.din("win_gdn", [2, 128, KT, GDN_IN])
        d["wout"] = self.din("wout", [4, 128, 16, D])
        d["wkv"] = self.din("wkv", [4, 128, KT, D])
        d["wglu"] = self.din("wglu", [2, 128, 12, TW])
        d["bglu"] = self.din("bglu", [128, 2, 12])
        self.outT = self.nc.dram_tensor("outT", [D, L], F32, kind="ExternalOutput").ap()
        s = self.sc = {}
        s["win_s5"] = self.dscr("win_s5_b", [2, 128, KT, S5_IN])
        s["win_gdn"] = self.dscr("win_gdn_b", [2, 128, KT, GDN_IN])
        s["wout"] = self.dscr("wout_b", [4, 128, 16, D])
        s["wkv"] = self.dscr("wkv_b", [4, 128, KT, D])
        s["wglu"] = self.dscr("wglu_b", [2, 128, 12, TW])

    def alloc_common(self):
        sb, ps = self.sb, self.ps
        self.X = sb("X", [128, KT, TT], F32)
        self.H = sb("H", [128, KT, TT], BF16)
        self.SQ = sb("SQ", [128, TT], BF16)
        self.RSTD = sb("RSTD", [128, TT], F32)
        self.YM = sb("YM", [128, 12, TT], BF16)
        self.YX = sb("YX", [128, 4, TT], BF16)
        self.QX = sb("QX", [128, 4, TT], BF16)
        self.SGX = sb("SGX", [128, 4, TT], BF16)
        self.SG = sb("SG", [128, 12, TT], BF16)
        self.EX = sb("EX", [128, 2, TT], BF16)
        if not self.mix:
            self.RDEN = sb("RDEN", [128, TT], F32)
        self.OTMP = self.RSTD
        self.KTs = sb("KTs", [128, 4, 4, MEM], BF16)
        self.Vs = sb("Vs", [128, 4, 2, XW], BF16)
        self.ONES = sb("ONES", [128, 128], BF16)
        self.NW = sb("NW", [128, 4, KT], F32)
        self.MNW = sb("MNW", [128, 4, KT], F32)
        self.FNW = sb("FNW", [128, KT], F32)
        self.BGLU = sb("BGLU", [128, 2, 12], F32)
        self.NWB = 4
        self.WB = [sb(f"WB{i}", [128, 8 * 512], BF16) for i in range(self.NWB)]
        self.wb_rr = 0
        self.PS = [ps(f"PS{i}", [128, 512]) for i in range(8)]
        self.ps_rr = 0

    def next_ps(self, lo=0, hi=4):
        i = lo + self.ps_rr % (hi - lo)
        self.ps_rr += 1
        return i

    def load_w(self, src, K, ncols, rbufs):
        i = self.wb_rr % self.NWB
        self.wb_rr += 1
        wb = self.WB[i][:, 0:K * ncols].rearrange("p (k n) -> p k n", k=K)
        eng = "sync"
        self.DMA(eng, wb, src, rbufs, [self.B(f"WB{i}")])
        return wb, self.B(f"WB{i}")

    def prologue(self):
        d, s, B = self.dr, self.sc, self.B
        for name, n0, K in (("win_s5", 2, KT), ("win_gdn", 2, KT), ("wout", 4, 16), ("wkv", 4, KT), ("wglu", 2, 12)):
            for j in range(n0):
                for k in range(K):
                    self.DMA("gpsimd", s[name][j, :, k, :], d[name][j, :, k, :], [], [B(f"{name}_b{j}_{k}")])
        self.DMA("sync", self.NW[:], d["nw"], [], [B("NW")])
        self.DMA("sync", self.MNW[:], d["mnw"], [], [B("MNW")])
        self.DMA("sync", self.FNW[:], d["fnw"], [], [B("FNW")])
        self.DMA("sync", self.BGLU[:], d["bglu"], [], [B("BGLU")])
        self.MS("vector", self.ONES[:], 1.0, [B("ONES")])
        if not self.mix:
            self.MS("vector", self.YM[:], 0.0, [B(f"YM{j}") for j in range(12)])
        memT = self.X[:, :, 0:MEM]
        for k in range(KT):
            self.DMA("sync", self.X[:, k, 0:MEM], d["memT"][k * 128:(k + 1) * 128, :], [], [B(f"X{k}")])
        pb = 4
        for k in range(KT):
            self.ACT(self.SQ[:, 0:MEM], self.X[:, k, 0:MEM], AF.Square, [B(f"X{k}")], [B("SQ")])
            self.MM(self.PS[pb][:, 0:MEM], self.ONES[:], self.SQ[:, 0:MEM], k == 0, k == KT - 1,
                    [B("ONES"), B("SQ")], [B(f"PS{pb}")])
        self.ACT(self.RSTD[:, 0:MEM], self.PS[pb][:, 0:MEM], AF.Sqrt, [B(f"PS{pb}"), B("EPSC")], [B("RSTD")],
                 scale=1.0 / D, bias=self.eps_ap())
        self.RCP(self.RSTD[:, 0:MEM], self.RSTD[:, 0:MEM], [B("RSTD")], [B("RSTD")])
        for l in self.layers:
            for k in range(KT):
                self.STT("vector", self.H[:, k, 0:MEM], self.X[:, k, 0:MEM], self.MNW[:, l, k:k + 1],
                         self.RSTD[:, 0:MEM], ALU.mult, ALU.mult,
                         [B(f"X{k}"), B("MNW"), B("RSTD")], [B(f"H{k}")])
            for half in range(2):
                wb, wbuf = self.load_w(s["wkv"][l, :, :, half * 512:(half + 1) * 512], KT, 512,
                                       [B(f"wkv_b{l}_{k}") for k in range(KT)])
                if half == 0:
                    for h in range(4):
                        p = self.next_ps()
                        for k in range(KT):
                            self.MM(self.PS[p][:, 0:MEM], wb[:, k, h * 128:(h + 1) * 128], self.H[:, k, 0:MEM],
                                    k == 0, k == KT - 1, [wbuf, B(f"H{k}")], [B(f"PS{p}")])
                        self.CP("scalar", self.KTs[:, l, h, :], self.PS[p][:, 0:MEM], [B(f"PS{p}")], [B("KTs")])
                else:
                    for mt in range(2):
                        p = self.next_ps()
                        for k in range(KT):
                            self.MM(self.PS[p][:, :], self.H[:, k, mt * 128:(mt + 1) * 128], wb[:, k, :],
                                    k == 0, k == KT - 1, [wbuf, B(f"H{k}")], [B(f"PS{p}")])
                        self.CP("vector", self.Vs[:, l, mt, :], self.PS[p][:, :], [B(f"PS{p}")], [B("Vs")])

    def eps_ap(self):
        if not hasattr(self, "_eps"):
            self._eps = self.sb("EPSC", [128, 1], F32)
            self.MS("vector", self._eps[:], EPS, [self.B("EPSC")])
        return self._eps[:, 0:1]

    def rms_stats(self):
        B = self.B
        pb = 4
        for k in range(KT):
            self.ACT(self.SQ[:], self.X[:, k, :], AF.Square, [B(f"X{k}")], [B("SQ")])
            self.MM(self.PS[pb][:], self.ONES[:], self.SQ[:], k == 0, k == KT - 1, [B("ONES"), B("SQ")], [B(f"PS{pb}")])
        self.ACT(self.RSTD[:], self.PS[pb][:], AF.Sqrt, [B(f"PS{pb}"), B("EPSC")], [B("RSTD")], scale=1.0 / D, bias=self.eps_ap())
        self.RCP(self.RSTD[:], self.RSTD[:], [B("RSTD")], [B("RSTD")])

    def norm_in(self, l):
        B = self.B
        self.rms_stats()
        for k in range(KT):
            self.STT("vector", self.H[:, k, :], self.X[:, k, :], self.NW[:, l, k:k + 1], self.RSTD[:],
                     ALU.mult, ALU.mult, [B(f"X{k}"), B("NW"), B("RSTD")], [B(f"H{k}")])

    def proj_cols(self, wsrc, wbufs, col0, ncols, evac):
        B = self.B
        wb, wbuf = self.load_w(wsrc[:, :, col0:col0 + ncols], KT, ncols, wbufs)
        nt = (ncols + 127) // 128
        for i in range(nt):
            m = min(128, ncols - i * 128)
            p = self.next_ps()
            for k in range(KT):
                self.MM(self.PS[p][0:m, :], wb[:, k, i * 128:i * 128 + m], self.H[:, k, :], k == 0, k == KT - 1,
                        [wbuf, B(f"H{k}")], [B(f"PS{p}")])
            evac(i, self.PS[p], B(f"PS{p}"), m)

    def xattn(self, l):
        B = self.B
        for h in range(4):
            sp = [5, 6]
            for mt in range(2):
                self.MM(self.PS[sp[mt]][:], self.KTs[:, l, h, mt * 128:(mt + 1) * 128], self.QX[:, h, :], True, True,
                        [B("KTs"), B(f"QX{h}")], [B(f"PS{sp[mt]}")])
                self.ACT(self.EX[:, mt, :], self.PS[sp[mt]][:], AF.Exp, [B(f"PS{sp[mt]}")], [B(f"EX{mt}")], scale=ISQ)
            for mt in range(2):
                self.MM(self.PS[7][:], self.ONES[:], self.EX[:, mt, :], mt == 0, mt == 1,
                        [B("ONES"), B(f"EX{mt}")], [B("PS7")])
            p = self.next_ps()
            for mt in range(2):
                self.MM(self.PS[p][:], self.Vs[:, l, mt, h * 128:(h + 1) * 128], self.EX[:, mt, :], mt == 0, mt == 1,
                        [B("Vs"), B(f"EX{mt}")], [B(f"PS{p}")])
            self.RCP(self.RDEN[:], self.PS[7][:], [B("PS7")], [B("RDEN")])
            self.TTo("vector", self.OTMP[:], self.PS[p][:], self.RDEN[:], ALU.mult, [B(f"PS{p}"), B("RDEN")], [B("RSTD")])
            self.TTo("gpsimd", self.YX[:, h, :], self.OTMP[:], self.SGX[:, h, :], ALU.mult,
                     [B("RSTD"), B(f"SGX{h}")], [B(f"YX{h}")])

    def out_proj(self, l):
        B, s = self.B, self.sc
        rb = [B(f"YM{j}") for j in range(12)] + [B(f"YX{h}") for h in range(4)]
        for half in range(4):
            wb, wbuf = self.load_w(s["wout"][l, :, :, half * 256:(half + 1) * 256], 16, 256,
                                   [B(f"wout_b{l}_{k}") for k in range(16)])
            for i in range(2):
                m = half * 2 + i
                p = self.next_ps()
                for k in range(16):
                    rhs = self.YM[:, k, :] if k < 12 else self.YX[:, k - 12, :]
                    self.MM(self.PS[p][:], wb[:, k, i * 128:(i + 1) * 128], rhs, k == 0, k == 15,
                            [wbuf, rb[k]], [B(f"PS{p}")])
                self.TTo("vector", self.X[:, m, :], self.X[:, m, :], self.PS[p][:], ALU.add,
                         [B(f"X{m}"), B(f"PS{p}")], [B(f"X{m}")])

    def final_norm_store(self, t):
        B = self.B
        self.rms_stats()
        OUT = self.H_as_f32()
        for k in range(KT):
            ob = [B(f"H{2 * (k % 2) + q_}") for q_ in range(2)]
            self.STT("vector", OUT[k % 2], self.X[:, k, :], self.FNW[:, k:k + 1], self.RSTD[:], ALU.mult, ALU.mult,
                     [B(f"X{k}"), B("FNW"), B("RSTD")], ob)
            self.DMA("sync", self.outT[k * 128:(k + 1) * 128, t * TT:(t + 1) * TT], OUT[k % 2], ob,
                     [B(f"out_{t}_{k}")])
            self.out_bufs.append(B(f"out_{t}_{k}"))

    def H_as_f32(self):
        if not hasattr(self, "_outf"):
            self._outf = self.H[:, 0:4, :].rearrange("p k n -> p (k n)").bitcast(F32).rearrange("p (a n) -> p a n", a=2)
        return [self._outf[:, 0, :], self._outf[:, 1, :]]

    def alloc_s5(self):
        sb = self.sb
        self.U = self.YM

    def s5_layer(self, l):
        B, s = self.B, self.sc
        j = l // 2
        self.norm_in(l)
        wbufs = [B(f"win_s5_b{j}_{k}") for k in range(KT)]

        def evac(c0):
            def f(i, ps, pbuf, m):
                gi = c0 // 128 + i
                if gi < 12:
                    self.CP("vector" if gi % 2 else "scalar",
                            self.U[:, gi, :].rearrange("p (s c) -> p s c", s=8),
                            ps[:, :].rearrange("p (c s) -> p s c", s=8), [pbuf], [B(f"YM{gi}")])
                elif gi < 24:
                    self.ACT(self.SG[:, gi - 12, :], ps[:, :], AF.Silu, [pbuf], [B(f"SG{gi - 12}")])
                elif gi < 28:
                    self.CP("vector", self.QX[:, gi - 24, :], ps[:, :], [pbuf], [B(f"QX{gi - 24}")])
                else:
                    self.ACT(self.SGX[:, gi - 28, :], ps[:, :], AF.Silu, [pbuf], [B(f"SGX{gi - 28}")])
            return f

        order = [0, 512, 1024, 3072, 3584, 1536, 2048, 2560]
        for c0 in order[:3]:
            self.proj_cols(s["win_s5"][j], wbufs, c0, 512, evac(c0))
        if self.mix:
            self.s5_state_part(j)
        for c0 in order[3:]:
            self.proj_cols(s["win_s5"][j], wbufs, c0, 512, evac(c0))
        self.xattn(l)
        if self.mix:
            self.s5_out_part(j)
        self.out_proj(l)

    def build(self):
        B = self.B
        self.declare()
        self.alloc_common()
        self.alloc_s5()
        if self.mix:
            self.alloc_s5_mix()
            self.alloc_gdn()
        self.out_bufs = []
        self.marks = {}
        self.prologue()
        self.marks["prologue"] = len(self.P.ops)
        if self.mix:
            self.s5_precompute()
            self.gdn_consts()
        self.marks["precompute"] = len(self.P.ops)
        for t in range(self.n_tiles):
            for k in range(KT):
                self.DMA("sync", self.X[:, k, :], self.dr["xT"][k * 128:(k + 1) * 128, t * TT:(t + 1) * TT],
                         [], [B(f"X{k}")])
            for l in self.layers:
                if l % 2 == 0:
                    self.s5_layer(l)
                else:
                    self.gdn_layer(l)
                self.P.barrier()
            if self.do_final:
                self.final_norm_store(t)
            else:
                for k in range(KT):
                    self.DMA("sync", self.outT[k * 128:(k + 1) * 128, t * TT:(t + 1) * TT], self.X[:, k, :],
                             [B(f"X{k}")], [B(f"out_{t}_{k}")])
                    self.out_bufs.append(B(f"out_{t}_{k}"))
        self.marks["end"] = len(self.P.ops)
        if self.truncate is not None:
            del self.P.ops[self.truncate:]
            self.out_bufs = []
            self.DMA("sync", self.outT[0:128, 0:TT], self.X[:, 0, :], [B("X0")], [B("out_dbg")])
            self.out_bufs.append(B("out_dbg"))
        self.P.op("sync", None, reads=self.out_bufs)
        self.P.emit()
        self.es.close()
        return self.nc


def _kt(w, K):
    return np.ascontiguousarray(w.reshape(K, 128, -1).transpose(1, 0, 2))


def host_layout(inp, b, t0, L):
    f = lambda a: np.ascontiguousarray(np.asarray(a, dtype=np.float32))
    m = {}
    m["xT"] = f(np.asarray(inp["x"])[b, t0:t0 + L, :].T)
    m["memT"] = f(np.asarray(inp["mem"])[b].T)
    m["nw"] = f(np.asarray(inp["norm_w"]).reshape(4, KT, 128).transpose(2, 0, 1))
    m["mnw"] = f(np.asarray(inp["mem_norm_w"]).reshape(4, KT, 128).transpose(2, 0, 1))
    m["fnw"] = f(np.asarray(inp["final_norm_w"]).reshape(KT, 128).T)
    m["win_s5"] = f(np.stack([_kt(np.asarray(inp["s5_w_in"])[j], KT) for j in range(2)]))
    m["win_gdn"] = f(np.stack([_kt(np.asarray(inp["gdn_w_in"])[j], KT) for j in range(2)]))
    m["wout"] = f(np.stack([_kt(np.asarray(inp["w_out"])[i], 16) for i in range(4)]))
    m["wkv"] = f(np.stack([_kt(np.asarray(inp["w_mem_kv"])[i], KT) for i in range(4)]))
    m["wglu"] = f(np.stack([_kt(np.asarray(inp["s5_w_glu"])[j], 12) for j in range(2)]))
    m["bglu"] = f(np.asarray(inp["s5_b_glu"]).reshape(2, 12, 128).transpose(2, 0, 1))
    return m


def host_layout_mix(inp):
    f = lambda a: np.ascontiguousarray(np.asarray(a, dtype=np.float32))
    m = {}
    lr = np.asarray(inp["s5_lambda_re"]); li = np.asarray(inp["s5_lambda_im"]); ls = np.asarray(inp["s5_log_step"])
    sp = lambda a: a.reshape(2, 48, 2, 64).transpose(2, 3, 0, 1).reshape(128, 2, 48)
    m["lamr_sp"] = f(sp(lr)); m["lami_sp"] = f(sp(li))
    m["lst_sp"] = f(sp(np.broadcast_to(ls[:, :, None], (2, 96, 64))))
    spc = lambda a: a.reshape(2, 48, 2, 16, 64).transpose(2, 4, 0, 1, 3).reshape(128, 2, 48, 16)
    m["cre_sp"] = f(spc(np.asarray(inp["s5_c_re"]))); m["cim_sp"] = f(spc(np.asarray(inp["s5_c_im"])))
    spb = lambda a: a.reshape(2, 48, 2, 64, 16).transpose(2, 3, 0, 1, 4).reshape(128, 2, 48, 16)
    m["bre_sp"] = f(spb(np.asarray(inp["s5_b_re"]))); m["bim_sp"] = f(spb(np.asarray(inp["s5_b_im"])))
    cpl = lambda a: np.broadcast_to(a.reshape(2, 12, 8, 1, 64), (2, 12, 8, 16, 64)).transpose(2, 3, 0, 1, 4).reshape(128, 2, 12, 64)
    m["lamr_cp"] = f(cpl(lr)); m["lami_cp"] = f(cpl(li))
    m["lst_cp"] = f(cpl(np.broadcast_to(ls[:, :, None], (2, 96, 64))))
    cpb = lambda a: a.reshape(2, 12, 8, 64, 16).transpose(2, 4, 0, 1, 3).reshape(128, 2, 12, 64)
    m["bre_cp"] = f(cpb(np.asarray(inp["s5_b_re"]))); m["bim_cp"] = f(cpb(np.asarray(inp["s5_b_im"])))
    m["d_cp"] = f(np.asarray(inp["s5_d"]).reshape(2, 12, 128).transpose(2, 0, 1))
    m.update(host_layout_gdn(inp))
    return m


PI = math.pi


def _s5_methods():
    def declare_mix(self):
        d = self.dr
        for n, shp in (("lamr_sp", [128, 2, 48]), ("lami_sp", [128, 2, 48]), ("lst_sp", [128, 2, 48]),
                       ("cre_sp", [128, 2, 48, 16]), ("cim_sp", [128, 2, 48, 16]),
                       ("bre_sp", [128, 2, 48, 16]), ("bim_sp", [128, 2, 48, 16]),
                       ("lamr_cp", [128, 2, 12, 64]), ("lami_cp", [128, 2, 12, 64]), ("lst_cp", [128, 2, 12, 64]),
                       ("bre_cp", [128, 2, 12, 64]), ("bim_cp", [128, 2, 12, 64]), ("d_cp", [128, 2, 12])):
            d[n] = self.din(n, shp)
        self.sc["Ka"] = self.dscr("Ka_d", [2, 12, 128, 8 * 128])
        self.sc["Wb"] = self.dscr("Wb_d", [2, 12, 128, 8 * 2 * 128])
        self.sc["Wc"] = self.dscr("Wc_d", [2, 12, 128, 4 * 8 * 2 * 32])
        self.declare_gdn()

    def alloc_s5_mix(self):
        sb = self.sb
        self.declare_mix()
        A = self.ARENA = sb("ARENA", [128, 36 * 1024], BF16)
        o = 0

        def carve(nbytes):
            nonlocal o
            v = A[:, o // 2:(o + nbytes) // 2]
            o += nbytes
            return v
        self.RDEN = A[:, 35 * 1024:36 * 1024].bitcast(F32)
        self.Y = carve(12 * TT * 2).rearrange("p (j n) -> p j n", j=12)
        self.XS = carve(2 * 48 * 65 * 4).bitcast(F32).rearrange("p (r g c) -> p r g c", r=2, g=48)
        self.XPb = carve(2 * 48 * 64 * 2).rearrange("p (r g c) -> p r g c", r=2, g=48)
        self.TA = carve(2 * 48 * 4).bitcast(F32).rearrange("p (r g) -> p r g", r=2)
        self.TB = carve(2 * 48 * 4).bitcast(F32).rearrange("p (r g) -> p r g", r=2)
        self.GTS5 = carve(TT * 2)
        self.s5_arena_end = o
        self.U = self.YM
        self.AR2 = sb("AR2", [128, 2, 2, 48], F32)
        self.AIS = sb("AIS", [128, 2, 2, 48], F32)
        self.ST5 = sb("ST5", [128, 2, 2, 48], F32)
        self.IDENT = sb("IDENT", [128, 128], F32)
        self.IDENTB = sb("IDENTB", [128, 128], BF16)
        self.MSP = sb("MSP", [128, 2], F32)
        self.MCP = sb("MCP", [128, 2], F32)
        self.DCP = sb("DCP", [128, 2, 12], F32)

    def s5_precompute(self):
        B, d, s = self.B, self.dr, self.sc
        nc = self.nc
        A = self.ARENA
        o = 0

        def carve(shape, dt=F32):
            nonlocal o
            n = int(np.prod(shape))
            nb = n * (4 if dt == F32 else 2)
            v = A[:, o // 2:(o + nb) // 2]
            o += nb
            if dt == F32:
                v = v.bitcast(F32)
            if len(shape) > 1:
                names = " ".join(f"a{i}" for i in range(len(shape)))
                kw = {f"a{i}": shape[i] for i in range(len(shape) - 1)}
                v = v.rearrange(f"p ({names}) -> p {names}", **kw)
            return v
        ar = B("ARENA")
        V = "vector"
        self.P.pool(lambda e: e.iota(self.IDENT[:].bitcast(mybir.dt.int32), pattern=[[1, 128]], base=0, channel_multiplier=-1),
                    [], [B("IDENT")])
        self.CP(V, self.IDENT[:], self.IDENT[:].bitcast(mybir.dt.int32), [B("IDENT")], [B("IDENT")])
        self.TS(V, self.IDENT[:], self.IDENT[:], 0.0, None, ALU.is_equal, None, [B("IDENT")], [B("IDENT")])
        self.CP(V, self.IDENTB[:], self.IDENT[:], [B("IDENT")], [B("IDENTB")])
        self.MS(V, self.MSP[:], 0.0, [B("MSP")])
        self.MS(V, self.MSP[0:64, 0:1], 1.0, [B("MSP")])
        self.MS(V, self.MSP[64:128, 1:2], 1.0, [B("MSP")])
        onesf = carve([1])
        self.MS(V, onesf, 1.0, [ar])
        self.MS(V, self.MCP[:], 0.0, [B("MCP")])
        for blk in range(8):
            g2 = blk % 2
            self.DMA("sync", self.MCP[blk * 16:(blk + 1) * 16, g2:g2 + 1], onesf[0:16, :], [ar, B("MCP")], [B("MCP")])
        self.DMA("sync", self.DCP[:], d["d_cp"], [], [B("DCP")])
        self.MS(V, self.ST5[:], 0.0, [B("ST5")])
        base_o = o

        def chain(lamr, lami, lst, n, npow, rev):
            lr_ = carve([n]); li_ = carve([n]); dt = carve([n]); t1 = carve([n]); t2 = carve([n]); mag = carve([n])
            sn = carve([n]); cs = carve([n]); cr = carve([n]); ci = carve([n])
            Pr = carve([npow, n]); Pi = carve([npow, n])
            self.DMA("sync", lr_, lamr, [], [ar]); self.DMA("sync", li_, lami, [], [ar]); self.DMA("sync", dt, lst, [], [ar])
            self.ACT(dt, dt, AF.Exp, [ar], [ar])
            self.TTo(V, t1, lr_, dt, ALU.mult, [ar], [ar])
            self.TTo(V, t2, li_, dt, ALU.mult, [ar], [ar])
            self.ACT(mag, t1, AF.Exp, [ar], [ar])
            ni_ = carve([n]); mk = carve([n])

            def sin_of(dst, src, shift):
                self.TS(V, dst, src, shift, None, ALU.add, None, [ar], [ar])
                self.TS(V, mk, dst, 1.0 / (2 * PI), None, ALU.mult, None, [ar], [ar])
                self.CP(V, ni_.bitcast(mybir.dt.int32), mk, [ar], [ar])
                self.CP(V, mk, ni_.bitcast(mybir.dt.int32), [ar], [ar])
                self.STT(V, dst, mk, -2 * PI, dst, ALU.mult, ALU.add, [ar], [ar])
                self.TS(V, mk, dst, PI, None, ALU.is_gt, None, [ar], [ar])
                self.STT(V, dst, mk, -2 * PI, dst, ALU.mult, ALU.add, [ar], [ar])
                self.TS(V, mk, dst, -PI, None, ALU.is_lt, None, [ar], [ar])
                self.STT(V, dst, mk, 2 * PI, dst, ALU.mult, ALU.add, [ar], [ar])
                self.ACT(dst, dst, AF.Sin, [ar], [ar])
            sin_of(sn, t2, 0.0)
            sin_of(cs, t2, 0.5 * PI)
            a1r, a1i = t1, t2
            self.TTo(V, a1r, mag, cs, ALU.mult, [ar], [ar])
            self.TTo(V, a1i, mag, sn, ALU.mult, [ar], [ar])
            nr = carve([n]); den = carve([n]); tt = carve([n])
            self.TS(V, nr, a1r, -1.0, None, ALU.add, None, [ar], [ar])
            self.TTo(V, den, lr_, lr_, ALU.mult, [ar], [ar])
            self.TTo(V, tt, li_, li_, ALU.mult, [ar], [ar])
            self.TTo(V, den, den, tt, ALU.add, [ar], [ar])
            self.RCP(den, den, [ar], [ar])
            self.TTo(V, cr, nr, lr_, ALU.mult, [ar], [ar])
            self.TTo(V, tt, a1i, li_, ALU.mult, [ar], [ar])
            self.TTo(V, cr, cr, tt, ALU.add, [ar], [ar])
            self.TTo(V, cr, cr, den, ALU.mult, [ar], [ar])
            self.TTo(V, ci, a1i, lr_, ALU.mult, [ar], [ar])
            self.TTo(V, tt, nr, li_, ALU.mult, [ar], [ar])
            self.TTo(V, ci, ci, tt, ALU.subtract, [ar], [ar])
            self.TTo(V, ci, ci, den, ALU.mult, [ar], [ar])
            ix = (lambda k: npow - 1 - k) if rev else (lambda k: k)
            self.MS(V, Pr[:, ix(0), :], 1.0, [ar]); self.MS(V, Pi[:, ix(0), :], 0.0, [ar])
            for k in range(1, npow):
                p, q = ix(k - 1), ix(k)
                self.TTo(V, Pr[:, q, :], Pr[:, p, :], a1r, ALU.mult, [ar], [ar])
                self.TTo(V, tt, Pi[:, p, :], a1i, ALU.mult, [ar], [ar])
                self.TTo(V, Pr[:, q, :], Pr[:, q, :], tt, ALU.subtract, [ar], [ar])
                self.TTo(V, Pi[:, q, :], Pr[:, p, :], a1i, ALU.mult, [ar], [ar])
                self.TTo(V, tt, Pi[:, p, :], a1r, ALU.mult, [ar], [ar])
                self.TTo(V, Pi[:, q, :], Pi[:, q, :], tt, ALU.add, [ar], [ar])
            return Pr, Pi, cr, ci

        for j in sorted(set(l // 2 for l in self.layers if l % 2 == 0)):
            o = base_o
            Pr, Pi, cr, ci = chain(d["lamr_sp"][:, j, :], d["lami_sp"][:, j, :], d["lst_sp"][:, j, :], 48, 9, False)
            self.CP(V, self.AR2[:, j, 0, :], Pr[:, 8, :], [ar], [B("AR2")])
            self.CP(V, self.AR2[:, j, 1, :], Pr[:, 8, :], [ar], [B("AR2")])
            self.CP(V, self.AIS[:, j, 1, :], Pi[:, 8, :], [ar], [B("AIS")])
            self.TS(V, self.AIS[:, j, 0, :], Pi[:, 8, :], -1.0, None, ALU.mult, None, [ar], [B("AIS")])
            cre = carve([48, 16]); cim = carve([48, 16]); bre = carve([48, 16]); bim = carve([48, 16])
            for t_, n_ in ((cre, "cre_sp"), (cim, "cim_sp"), (bre, "bre_sp"), (bim, "bim_sp")):
                self.DMA("sync", t_, d[n_][:, j, :, :], [], [ar])
            BBr = carve([48, 16]); BBi = carve([48, 16]); tq = carve([48, 16])
            crb = cr.unsqueeze(2).broadcast_to([128, 48, 16]); cib = ci.unsqueeze(2).broadcast_to([128, 48, 16])
            self.TTo(V, BBr, bre, crb, ALU.mult, [ar], [ar]); self.TTo(V, tq, bim, cib, ALU.mult, [ar], [ar])
            self.TTo(V, BBr, BBr, tq, ALU.subtract, [ar], [ar])
            self.TTo(V, BBi, bim, crb, ALU.mult, [ar], [ar]); self.TTo(V, tq, bre, cib, ALU.mult, [ar], [ar])
            self.TTo(V, BBi, BBi, tq, ALU.add, [ar], [ar])
            BBr32 = carve([48, 32]); BBiN32 = carve([48, 32])
            BBiN = tq
            self.TS(V, BBiN, BBi, -1.0, None, ALU.mult, None, [ar], [ar])
            for g2 in range(2):
                self.TS(V, BBr32[:, :, g2 * 16:(g2 + 1) * 16], BBr, self.MSP[:, g2:g2 + 1], None, ALU.mult, None,
                        [ar, B("MSP")], [ar])
                self.TS(V, BBiN32[:, :, g2 * 16:(g2 + 1) * 16], BBiN, self.MSP[:, g2:g2 + 1], None, ALU.mult, None,
                        [ar, B("MSP")], [ar])
            Er = carve([9, 4, 16]); Ei = carve([9, 4, 16]); te = carve([9, 4, 16])
            Er32 = carve([9, 4, 32]); Ei32 = carve([9, 4, 32])
            WcB = carve([4, 8, 2, 32], BF16)
            KaF = carve([8, 128]); KaB = carve([8, 128], BF16)
            self.MS(V, KaF, 0.0, [ar])
            for jt in range(12):
                g0 = jt * 4
                shp = [128, 9, 4, 16]
                cb = lambda c_: c_[:, g0:g0 + 4, :].unsqueeze(1).broadcast_to(shp)
                pb_ = lambda p_: p_[:, :, g0:g0 + 4].unsqueeze(3).broadcast_to(shp)
                self.TTo(V, Er, cb(cre), pb_(Pr), ALU.mult, [ar], [ar]); self.TTo(V, te, cb(cim), pb_(Pi), ALU.mult, [ar], [ar])
                self.TTo(V, Er, Er, te, ALU.subtract, [ar], [ar])
                self.TTo(V, Ei, cb(cre), pb_(Pi), ALU.mult, [ar], [ar]); self.TTo(V, te, cb(cim), pb_(Pr), ALU.mult, [ar], [ar])
                self.TTo(V, Ei, Ei, te, ALU.add, [ar], [ar])
                for g2 in range(2):
                    self.TS(V, Er32[:, :, :, g2 * 16:(g2 + 1) * 16], Er, self.MSP[:, g2:g2 + 1], None, ALU.mult, None,
                            [ar, B("MSP")], [ar])
                    self.TS(V, Ei32[:, :, :, g2 * 16:(g2 + 1) * 16], Ei, self.MSP[:, g2:g2 + 1], None, ALU.mult, None,
                            [ar, B("MSP")], [ar])
                self.CP(V, WcB[:, :, :, 0, :], Er32[:, 1:9, :, :].rearrange("p k g c -> p g k c"), [ar], [B("WcB")])
                self.TS(V, WcB[:, :, :, 1, :], Ei32[:, 1:9, :, :].rearrange("p k g c -> p g k c"), -1.0, None, ALU.mult, None,
                        [ar], [B("WcB")])
                self.DMA("sync", s["Wc"][j, jt], WcB.rearrange("p g t r c -> p (g t r c)"), [B("WcB")], [B(f"Wc_d{j}_{jt}")])
                pk = 6
                PSK = self.PS[pk][:, 0:256].rearrange("p (t c) -> p t c", t=8)
                for q in range(4):
                    for tau in range(8):
                        self.MM(PSK[32 * q:32 * q + 32, tau, :], BBr32[:, g0 + q, :], Er32[:, tau, q, :], True, False,
                                [ar], [B(f"PS{pk}")], tile_position=(0, 32 * q))
                        self.MM(PSK[32 * q:32 * q + 32, tau, :], BBiN32[:, g0 + q, :], Ei32[:, tau, q, :], False, True,
                                [ar], [B(f"PS{pk}")], tile_position=(0, 32 * q))
                for q in range(4):
                    self.CP(V, KaF[32 * q:32 * q + 32, :, 32 * q:32 * q + 32], PSK[32 * q:32 * q + 32, :, :],
                            [B(f"PS{pk}")], [B("KaF")])
                self.STT(V, KaB[:, 0, :], self.IDENT[:], self.DCP[:, j, jt:jt + 1], KaF[:, 0, :], ALU.mult, ALU.add,
                         [B("IDENT"), B("DCP"), B("KaF")], [B("KaB")])
                self.CP(V, KaB[:, 1:8, :], KaF[:, 1:8, :], [B("KaF")], [B("KaB")])
                self.DMA("sync", s["Ka"][j, jt], KaB.rearrange("p t c -> p (t c)"), [B("KaB")], [B(f"Ka_d{j}_{jt}")])
                assert o <= 72 * 1024, o
            for jt in range(12):
                o = base_o
                n = 64
                Pr, Pi, cr, ci = chain(d["lamr_cp"][:, j, jt, :], d["lami_cp"][:, j, jt, :], d["lst_cp"][:, j, jt, :], n, 8, True)
                bre = carve([n]); bim = carve([n]); BBr = carve([n]); BBi = carve([n]); tq = carve([n])
                self.DMA("sync", bre, d["bre_cp"][:, j, jt, :], [], [ar]); self.DMA("sync", bim, d["bim_cp"][:, j, jt, :], [], [ar])
                self.TTo(V, BBr, bre, cr, ALU.mult, [ar], [ar]); self.TTo(V, tq, bim, ci, ALU.mult, [ar], [ar])
                self.TTo(V, BBr, BBr, tq, ALU.subtract, [ar], [ar])
                self.TTo(V, BBi, bim, cr, ALU.mult, [ar], [ar]); self.TTo(V, tq, bre, ci, ALU.mult, [ar], [ar])
                self.TTo(V, BBi, BBi, tq, ALU.add, [ar], [ar])
                Wr = carve([8, n]); Wi = carve([8, n]); tw = carve([8, n])
                bb = lambda a_: a_.unsqueeze(1).broadcast_to([128, 8, n])
                self.TTo(V, Wr, Pr, bb(BBr), ALU.mult, [ar], [ar]); self.TTo(V, tw, Pi, bb(BBi), ALU.mult, [ar], [ar])
                self.TTo(V, Wr, Wr, tw, ALU.subtract, [ar], [ar])
                self.TTo(V, Wi, Pr, bb(BBi), ALU.mult, [ar], [ar]); self.TTo(V, tw, Pi, bb(BBr), ALU.mult, [ar], [ar])
                self.TTo(V, Wi, Wi, tw, ALU.add, [ar], [ar])
                WbB = carve([8, 2, 128], BF16)
                for ri, W3 in ((0, Wr), (1, Wi)):
                    for g2 in range(2):
                        self.TS(V, WbB[:, :, ri, g2 * 64:(g2 + 1) * 64], W3, self.MCP[:, g2:g2 + 1], None,
                                ALU.mult, None, [ar, B("MCP")], [ar])
                self.DMA("sync", s["Wb"][j, jt], WbB.rearrange("p s r c -> p (s r c)"), [ar], [B(f"Wb_d{j}_{jt}")])
                assert o <= 72 * 1024, o
        fence = [ar, B("WcB"), B("KaB"), B("KaF")]
        for j in (0, 1):
            for jt in range(12):
                for nm in ("Ka_d", "Wb_d", "Wc_d"):
                    if f"{nm}{j}_{jt}" in self.bufs:
                        fence.append(B(f"{nm}{j}_{jt}"))
        fence += [B("MCP"), B("MSP"), B("IDENT"), B("IDENTB")]
        for e_ in ("tensor", "vector", "scalar", "gpsimd", "sync"):
            self.P.op(e_, None, writes=fence)

    def s5_state_part(self, j):
        B, s = self.B, self.sc
        self.CP("gpsimd", self.XS[:, :, :, 0], self.ST5[:, j, :, :], [B("ST5")], [B("XS")])
        for jt in range(12):
            wb, wbuf = self.load_w(s["Wb"][j, jt].rearrange("p (a n) -> p a n", a=1), 1, 2048, [B(f"Wb_d{j}_{jt}")])
            W = wb[:, 0, :].rearrange("p (s r c) -> p s r c", s=8, r=2)
            for ri in range(2):
                for sidx in range(8):
                    for q in range(4):
                        pq = 4 + q
                        PZ = self.PS[pq][:, :].rearrange("p (a r c) -> p a r c", a=4, r=2)
                        self.MM(PZ[:, jt % 4, ri, :], W[32 * q:32 * q + 32, sidx, ri, :],
                                self.U[32 * q:32 * q + 32, jt, sidx * 64:(sidx + 1) * 64], sidx == 0, sidx == 7,
                                [wbuf, B(f"YM{jt}")], [B(f"PS{pq}")], tile_position=(32 * q, 0), skip_group_check=True)
            if jt % 4 == 3:
                jt0 = jt - 3
                for q in range(4):
                    pq = 4 + q
                    PZ = self.PS[pq][:, :].rearrange("p (a r c) -> p a r c", a=4, r=2)
                    self.CP("scalar" if q % 2 else "vector", self.XS[:, :, 4 * jt0 + q:4 * jt0 + q + 13:4, 1:65],
                            PZ.rearrange("p a r c -> p r a c"), [B(f"PS{pq}")], [B("XS")])
        self.marks.setdefault("st_mm", len(self.P.ops))
        E = "gpsimd"
        xs = B("XS")
        for c in range(NCH):
            self.TTo(E, self.TA[:], self.AR2[:, j], self.XS[:, :, :, c], ALU.mult, [B("AR2"), xs], [B("TA")])
            self.TTo(E, self.TB[:, 0, :], self.AIS[:, j, 0, :], self.XS[:, 1, :, c], ALU.mult, [B("AIS"), xs], [B("TB")])
            self.TTo(E, self.TB[:, 1, :], self.AIS[:, j, 1, :], self.XS[:, 0, :, c], ALU.mult, [B("AIS"), xs], [B("TB")])
            self.TTo(E, self.TA[:], self.TA[:], self.TB[:], ALU.add, [B("TA"), B("TB")], [B("TA")])
            self.TTo(E, self.XS[:, :, :, c + 1], self.XS[:, :, :, c + 1], self.TA[:], ALU.add, [xs, B("TA")], [xs])
        self.marks.setdefault("st_scan", len(self.P.ops))
        self.CP("scalar", self.XPb[:], self.XS[:, :, :, 0:64], [xs], [B("XPb")])
        self.CP("gpsimd", self.ST5[:, j, :, :], self.XS[:, :, :, 64], [xs], [B("ST5")])

    def s5_out_part(self, j):
        B, s = self.B, self.sc
        for jt in range(12):
            i = self.wb_rr % self.NWB
            self.wb_rr += 1
            wbuf = B(f"WB{i}")
            KA = self.WB[i][:, 0:1024].rearrange("p (t c) -> p t c", t=8)
            WC = self.WB[i][:, 1024:3072].rearrange("p (g t r c) -> p g t r c", g=4, t=8, r=2)
            self.DMA("sync", self.WB[i][:, 0:1024], s["Ka"][j, jt], [B(f"Ka_d{j}_{jt}")], [wbuf])
            self.DMA("sync", self.WB[i][:, 1024:3072], s["Wc"][j, jt], [B(f"Wc_d{j}_{jt}")], [wbuf])
            p = self.next_ps()
            PY = self.PS[p][:, :].rearrange("p (t c) -> p t c", t=8)
            PYf = self.PS[p][:, :]
            for tau in range(8):
                self.MM(PYf[:, tau * 64:512], KA[:, tau, :], self.U[:, jt, 0:(8 - tau) * 64], tau == 0, False,
                        [wbuf, B(f"YM{jt}")], [B(f"PS{p}")], skip_group_check=True)
            for t in range(8):
                for ri in range(2):
                    for q in range(4):
                        self.MM(PY[32 * q:32 * q + 32, t, :], WC[:, q, t, ri, :], self.XPb[:, ri, 4 * jt + q, :], False,
                                ri == 1, [wbuf, B("XPb")], [B(f"PS{p}")], tile_position=(0, 32 * q), skip_group_check=True)
            self.ACT(self.Y[:, jt, :].rearrange("p (c s) -> p s c", s=8), PY, AF.Gelu_apprx_tanh,
                     [B(f"PS{p}")], [B(f"Y{jt}")])
        self.marks.setdefault("out_mm", len(self.P.ops))
        yb = [B(f"Y{k}") for k in range(12)]
        for ch in range(6):
            wb, wbuf = self.load_w(s["wglu"][j, :, :, ch * 256:(ch + 1) * 256], 12, 256,
                                   [B(f"wglu_b{j}_{k}") for k in range(12)])
            for i in range(2):
                m = ch * 2 + i
                p = self.next_ps()
                for k in range(12):
                    self.MM(self.PS[p][:], wb[:, k, i * 128:(i + 1) * 128], self.Y[:, k, :], k == 0, k == 11,
                            [wbuf, yb[k]], [B(f"PS{p}")])
                self.ACT(self.GTS5, self.PS[p][:], AF.Sigmoid, [B(f"PS{p}"), B("BGLU")], [B("GTS5")],
                         bias=self.BGLU[:, j, m:m + 1])
                self.TTo("vector", self.GTS5, self.GTS5, self.Y[:, m, :], ALU.mult, [B("GTS5"), yb[m]], [B("GTS5")])
                self.TTo("gpsimd", self.YM[:, m, :], self.GTS5, self.SG[:, m, :], ALU.mult, [B("GTS5"), B(f"SG{m}")],
                         [B(f"YM{m}")])

    for k_, v_ in list(locals().items()):
        if callable(v_):
            setattr(Builder, k_, v_)


_s5_methods()


def host_layout_gdn(inp):
    f = lambda a: np.ascontiguousarray(np.asarray(a, dtype=np.float32))
    m = {}
    m["g_alog"] = f(np.asarray(inp["gdn_a_log"]).T)
    m["g_dtb"] = f(np.asarray(inp["gdn_dt_bias"]).T)
    m["g_nw"] = f(np.asarray(inp["gdn_norm_w"]).T)
    cw = np.asarray(inp["gdn_conv_w"])
    m["g_cw"] = f(cw.reshape(2, 4, 24, 128).transpose(3, 0, 2, 1))
    return m


def _gdn_methods():
    def declare_gdn(self):
        d = self.dr
        d["g_alog"] = self.din("g_alog", [12, 2])
        d["g_dtb"] = self.din("g_dtb", [12, 2])
        d["g_nw"] = self.din("g_nw", [128, 2])
        d["g_cw"] = self.din("g_cw", [128, 2, 24, 4])

    def alloc_gdn(self):
        sb = self.sb
        A = self.ARENA
        o = 0

        def carve(nbytes):
            nonlocal o
            v = A[:, o // 2:(o + nbytes) // 2]
            o += nbytes
            return v
        f32v = lambda v, *shape: (v.bitcast(F32) if not shape else v.bitcast(F32))
        self.QKV = carve(24 * TT * 2).rearrange("p (j n) -> p j n", j=24)
        self.XC = [carve(516 * 4).bitcast(F32) for _ in range(2)]
        self.ACC = [carve(TT * 4).bitcast(F32) for _ in range(2)]
        self.Gg = carve(TT * 4).bitcast(F32)
        self.BETA = carve(TT * 4).bitcast(F32)
        self.GCF = carve(TT * 4).bitcast(F32)
        self.E1 = carve(TT * 4).bitcast(F32)
        tm = lambda: carve(48 * 4).bitcast(F32).rearrange("p (a h) -> p a h", a=4)
        self.GT, self.BT, self.GCT, self.GLT, self.SC1, self.SC2 = tm(), tm(), tm(), tm(), tm(), tm()
        m32 = lambda cv: cv(128 * 4).bitcast(F32)
        m16 = lambda cv: cv(128 * 2)

        class Slot:
            pass

        def mk_slot(cv, tag):
            S = Slot()
            S.tag = tag
            S.GRs = cv(TT * 4).bitcast(F32)
            S.EGR = cv(TT * 4).bitcast(F32)
            S.KBT = cv(TT * 2)
            S.QG = cv(TT * 2)
            S.TMP1, S.DEC, S.DL, S.DA = m32(cv), m32(cv), m32(cv), m32(cv)
            S.LTm, S.L0 = m16(cv), m16(cv)
            S.Pb = [m16(cv) for _ in range(2)]
            S.PTb = [m16(cv) for _ in range(2)]
            S.RTb = [m16(cv) for _ in range(2)]
            S.Us = m32(cv)
            S.TMT, S.AT, S.VB, S.KBG, S.KDEC, S.WTs, S.VN, S.ON, S.Sb = (m16(cv) for _ in range(9))
            S.SSQ = cv(4 * 4).bitcast(F32)
            S.JUNK = m16(cv)
            return S
        self.slots = [mk_slot(carve, "a")]
        dead_bf = [v.bitcast(BF16) for v in (self.XC[0], self.XC[1], self.ACC[0], self.ACC[1], self.E1)]
        dpos = [0, 0]

        def carve2(nbytes):
            n = nbytes // 2
            while dpos[0] < len(dead_bf):
                v = dead_bf[dpos[0]]
                if dpos[1] + n <= v.shape[1]:
                    out = v[:, dpos[1]:dpos[1] + n]
                    dpos[1] += n
                    return out
                dpos[0] += 1
                dpos[1] = 0
            return carve(nbytes)
        self.slots.append(mk_slot(carve2, "b"))
        assert o <= 64 * 1024, o
        self.SEL = A[0:12, 32 * 1024:32 * 1024 + 12 * 128 * 2].bitcast(F32).rearrange("p (h m) -> p h m", h=12)
        self.SST = sb("SST", [128, 2, 12, 128], F32)
        self.CONVST = sb("CONVST", [128, 2, 24, 3], F32)
        self.CW = sb("CW", [128, 2, 24, 4], F32)
        self.TRI = sb("TRI", [128, 128], F32)
        self.MSU = sb("MSU", [128, 128], F32)
        self.BLK = sb("BLK", [128, 128], F32)
        self.ONESF = sb("ONESF", [128, 128], F32)
        self.GNW = sb("GNW", [128, 2], F32)
        self.NA = sb("NA", [12, 2], F32)
        self.DTB = sb("DTB", [12, 2], F32)

    def gdn_consts(self):
        B, d = self.B, self.dr
        V = "vector"
        self.MS(V, self.SST[:], 0.0, [B("SST")])
        self.MS(V, self.CONVST[:], 0.0, [B("CONVST")])
        self.MS(V, self.ONESF[:], 1.0, [B("ONESF")])
        self.DMA("sync", self.CW[:], d["g_cw"], [], [B("CW")])
        self.DMA("sync", self.GNW[:], d["g_nw"], [], [B("GNW")])
        self.DMA("sync", self.NA[:], d["g_alog"], [], [B("NA")])
        self.DMA("sync", self.DTB[:], d["g_dtb"], [], [B("DTB")])
        self.ACT(self.NA[:], self.NA[:], AF.Exp, [B("NA")], [B("NA")])
        self.TS(V, self.NA[:], self.NA[:], -1.0, None, ALU.mult, None, [B("NA")], [B("NA")])
        self.P.pool(lambda e: e.iota(self.TRI[:].bitcast(mybir.dt.int32), pattern=[[1, 128]], base=0, channel_multiplier=-1),
                    [], [B("TRI")])
        self.CP(V, self.TRI[:], self.TRI[:].bitcast(mybir.dt.int32), [B("TRI")], [B("TRI")])
        self.TS(V, self.TRI[:], self.TRI[:], 0.0, None, ALU.is_ge, None, [B("TRI")], [B("TRI")])
        self.MS(V, self.TRI[0:64, 64:128], 0.0, [B("TRI")])
        self.TTo(V, self.MSU[:], self.TRI[:], self.IDENT[:], ALU.subtract, [B("TRI"), B("IDENT")], [B("MSU")])
        self.MS(V, self.BLK[:], 0.0, [B("BLK")])
        self.MS(V, self.BLK[0:64, 0:64], 1.0, [B("BLK")])
        self.MS(V, self.BLK[64:128, 64:128], 1.0, [B("BLK")])
        for h in range(12):
            self.TS(V, self.SEL[:, h, :], self.ONESF[0:12, :], self.IDENT[0:12, h:h + 1], None, ALU.mult, None,
                    [B("ONESF"), B("IDENT")], [B("SEL")])

    def gdn_head(self, j, h, S):
        B = self.B
        V, G_ = "vector", "gpsimd"
        gh = B("GH" + S.tag)
        gp_ = B("GP" + S.tag)
        hq = h // 2
        QT, KTt, VT = self.QKV[:, hq, :], self.QKV[:, 6 + hq, :], self.QKV[:, 12 + h, :]
        bq, bk, bv = B(f"QKV{hq}"), B(f"QKV{6 + hq}"), B(f"QKV{12 + h}")
        Sst = self.SST[:, j, h, :]
        bs = B(f"SST{j}_{h}")
        p = self.next_ps(0, 8)
        self.MM(self.PS[p][:], self.SEL[:, h, :], self.GCF[0:12, :], True, True, [B("SEL"), B("GCF")], [B(f"PS{p}")])
        yield
        self.CP(V, S.GRs, self.PS[p][:], [B(f"PS{p}")], [gh])
        yield
        self.ACT(S.EGR, self.PS[p][:], AF.Exp, [B(f"PS{p}")], [gh])
        yield
        p = self.next_ps(0, 8)
        self.MM(self.PS[p][:], self.SEL[:, h, :], self.BETA[0:12, :], True, True, [B("SEL"), B("BETA")], [B(f"PS{p}")])
        yield
        self.TTo(V, S.KBT, KTt, self.PS[p][:], ALU.mult, [bk, B(f"PS{p}")], [gh])
        yield
        self.TTo(G_, S.QG, QT, S.EGR, ALU.mult, [bq, gh], [gh])
        yield
        self.CP("scalar", S.Sb, Sst, [bs], [B("Sb" + S.tag)])
        yield
        for pr in range(4):
            cs = slice(pr * 128, (pr + 1) * 128)
            p = self.next_ps(0, 8)
            self.MM(self.PS[p][:, 0:128], KTt[:, cs], S.KBT[:, cs], True, True, [bk, gh], [B(f"PS{p}")])
            yield
            self.MM(self.PS[p][:, 128:256], KTt[:, cs], QT[:, cs], True, True, [bk, bq], [B(f"PS{p}")], skip_group_check=True)
            yield
            self.TS(V, S.TMP1, S.GRs[:, cs], self.GCT[:, pr, h:h + 1], 0.0, ALU.subtract, ALU.min,
                    [gh, B("GCT")], [gp_])
            yield
            self.ACT(S.DEC, S.TMP1, AF.Exp, [gp_], [gp_])
            yield
            self.TTo(G_, S.DL, S.DEC, self.MSU[:], ALU.mult, [gp_, B("MSU")], [gp_])
            yield
            self.TTo(G_, S.DA, S.DEC, self.TRI[:], ALU.mult, [gp_, B("TRI")], [gp_])
            yield
            self.TTo(V, S.LTm, self.PS[p][:, 0:128], S.DL, ALU.mult, [B(f"PS{p}"), gp_], [gp_])
            yield
            self.TTo(V, S.AT, self.PS[p][:, 128:256], S.DA, ALU.mult, [B(f"PS{p}"), gp_], [gp_])
            yield
            p = self.next_ps(0, 8)
            pl16 = self.PS[p][:, :].bitcast(BF16)
            self.TR(pl16[:, 0:128], S.LTm, self.IDENTB[:], [gp_, B("IDENTB")], [B(f"PS{p}")])
            yield
            self.CP("scalar", S.L0, pl16[:, 0:128], [B(f"PS{p}")], [gp_])
            yield
            self.TTo(V, S.RTb[0], self.IDENTB[:], S.LTm, ALU.subtract, [B("IDENTB"), gp_], [gp_])
            yield
            Pc, PTc, RTc = S.L0, S.LTm, S.RTb[0]
            for k in range(5):
                Pn, PTn, RTn = S.Pb[k % 2], S.PTb[k % 2], S.RTb[(k + 1) % 2]
                p = self.next_ps(0, 8)
                self.MM(self.PS[p][:, 0:128], PTc, Pc, True, True, [gp_], [B(f"PS{p}")])
                yield
                if k < 4:
                    self.MM(self.PS[p][:, 128:256], Pc, PTc, True, True, [gp_], [B(f"PS{p}")], skip_group_check=True)
                    yield
                self.CP("scalar", Pn, self.PS[p][:, 0:128], [B(f"PS{p}")], [gp_])
                yield
                if k < 4:
                    self.CP(V, PTn, self.PS[p][:, 128:256], [B(f"PS{p}")], [gp_])
                    yield
                p2 = self.next_ps(0, 8)
                self.MM(self.PS[p2][:, 0:128], Pn, RTc, True, True, [gp_], [B(f"PS{p2}")])
                yield
                if k < 4:
                    self.TTo(V, RTn, RTc, self.PS[p2][:, 0:128], ALU.add, [gp_, B(f"PS{p2}")], [gp_])
                    yield
                else:
                    self.TTo(V, S.TMT, RTc, self.PS[p2][:, 0:128], ALU.add, [gp_, B(f"PS{p2}")], [gp_])
                    yield
                Pc, PTc, RTc = Pn, PTn, RTn
            p = self.next_ps(0, 8)
            pb16 = self.PS[p][:, :].bitcast(BF16)
            self.TR(pb16[:, 0:128], VT[:, cs], self.IDENTB[:], [bv, B("IDENTB")], [B(f"PS{p}")])
            yield
            self.TR(pb16[:, 128:256], KTt[:, cs], self.IDENTB[:], [bk, B("IDENTB")], [B(f"PS{p}")])
            yield
            self.TS(V, S.VB, pb16[:, 0:128], self.BT[:, pr, h:h + 1], None, ALU.mult, None, [B(f"PS{p}"), B("BT")], [gp_])
            yield
            self.TS(V, S.KBG, pb16[:, 128:256], self.SC1[:, pr, h:h + 1], None, ALU.mult, None,
                    [B(f"PS{p}"), B("SC1")], [gp_])
            yield
            self.TS(V, S.KDEC, pb16[:, 128:256], self.SC2[:, pr, h:h + 1], None, ALU.mult, None,
                    [B(f"PS{p}"), B("SC2")], [gp_])
            yield
            p = self.next_ps(0, 8)
            self.MM(self.PS[p][:, 0:128], S.TMT, S.VB, True, True, [gp_], [B(f"PS{p}")])
            yield
            self.MM(self.PS[p][:, 128:256], S.KBG, S.TMT, True, True, [gp_], [B(f"PS{p}")], skip_group_check=True)
            yield
            self.CP("scalar", S.Us, self.PS[p][:, 0:128], [B(f"PS{p}")], [gp_])
            yield
            self.CP(V, S.WTs, self.PS[p][:, 128:256], [B(f"PS{p}")], [gp_])
            yield
            for c in range(2):
                rs = slice(c * 64, (c + 1) * 64)
                gcol = pr * 128 + c * 64
                pw = self.next_ps(0, 8)
                self.MM(self.PS[pw][rs, 0:128], S.WTs[:, rs], S.Sb, True, True, [gp_, B("Sb" + S.tag)], [B(f"PS{pw}")],
                        tile_position=(0, 64 * c))
                yield
                self.TTo(V, S.VN[rs, :], S.Us[rs, :], self.PS[pw][rs, 0:128], ALU.subtract,
                         [gp_, B(f"PS{pw}")], [B("VN" + S.tag)])
                yield
                po = self.next_ps(0, 8)
                self.MM(self.PS[po][rs, 0:128], S.QG[:, gcol:gcol + 64], S.Sb, True, False, [gh, B("Sb" + S.tag)],
                        [B(f"PS{po}")], tile_position=(0, 64 * c))
                yield
                self.MM(self.PS[po][rs, 0:128], S.AT[rs, rs], S.VN[rs, :], False, True, [gp_, B("VN" + S.tag)],
                        [B(f"PS{po}")], tile_position=(64 * c, 64 * c))
                yield
                pu = self.next_ps(0, 8)
                self.MM(self.PS[pu][:, 0:128], S.KDEC[rs, :], S.VN[rs, :], True, True, [gp_, B("VN" + S.tag)],
                        [B(f"PS{pu}")], tile_position=(64 * c, 0))
                yield
                self.STT(V, Sst, Sst, S.EGR[:, gcol + 63:gcol + 64], self.PS[pu][:, 0:128], ALU.mult, ALU.add,
                         [bs, gh, B(f"PS{pu}")], [bs])
                yield
                self.CP("scalar", S.Sb, Sst, [bs], [B("Sb" + S.tag)])
                yield
                self.ACT(S.JUNK[rs, :], self.PS[po][rs, 0:128], AF.Square, [B(f"PS{po}")], [B("JUNK" + S.tag), B("SSQ" + S.tag)],
                         accum_out=S.SSQ[rs, 0:1])
                yield
                self.ACT(S.SSQ[rs, 1:2], S.SSQ[rs, 0:1], AF.Sqrt, [B("SSQ" + S.tag), B("EPSC")], [B("SSQ" + S.tag)],
                         scale=1.0 / 128, bias=self._eps[rs, 0:1])
                yield
                self.RCP(S.SSQ[rs, 2:3], S.SSQ[rs, 1:2], [B("SSQ" + S.tag)], [B("SSQ" + S.tag)])
                yield
                self.TS(V, S.ON[rs, :], self.PS[po][rs, 0:128], S.SSQ[rs, 2:3], None, ALU.mult, None,
                        [B(f"PS{po}"), B("SSQ" + S.tag)], [B("ON" + S.tag)])
                yield
            p = self.next_ps(0, 8)
            pb16 = self.PS[p][:, :].bitcast(BF16)
            self.TR(pb16[:, 0:128], S.ON, self.IDENTB[:], [B("ON" + S.tag), B("IDENTB")], [B(f"PS{p}")])
            yield
            self.STT(V, self.YM[:, h, cs], pb16[:, 0:128], self.GNW[:, j:j + 1], self.SG[:, h, cs], ALU.mult, ALU.mult,
                     [B(f"PS{p}"), B("GNW"), B(f"SG{h}")], [B(f"YM{h}")])
            yield


    def gdn_layer(self, l):
        B, s = self.B, self.sc
        j = l // 2
        V, G_ = "vector", "gpsimd"
        self.norm_in(l)
        wsrc = s["win_gdn"][j]
        wbufs = [B(f"win_gdn_b{j}_{k}") for k in range(KT)]
        ga = B("GA")

        def ev_a(i, ps, pbuf, m):
            self.ACT(self.E1[0:12, :], ps[0:12, :], AF.Exp, [pbuf, B("DTB")], [ga], bias=self.DTB[:, j:j + 1])
            self.ACT(self.E1[0:12, :], self.E1[0:12, :], AF.Ln, [ga, B("ONESF")], [ga], bias=self.ONESF[0:12, 0:1])
            self.TS(V, self.Gg[0:12, :], self.E1[0:12, :], self.NA[:, j:j + 1], None, ALU.mult, None, [ga, B("NA")], [B("Gg")])

        def ev_b(i, ps, pbuf, m):
            self.ACT(self.BETA[0:12, :], ps[0:12, :], AF.Sigmoid, [pbuf], [B("BETA")])
        self.proj_cols(wsrc, wbufs, 3072, 12, ev_a)
        self.proj_cols(wsrc, wbufs, 3084, 12, ev_b)
        p = self.next_ps()
        for pr in range(4):
            self.TR(self.PS[p][:, pr * 12:(pr + 1) * 12], self.Gg[0:12, pr * 128:(pr + 1) * 128], self.IDENT[0:12, 0:12],
                    [B("Gg"), B("IDENT")], [B(f"PS{p}")])
            self.TR(self.PS[p][:, 48 + pr * 12:48 + (pr + 1) * 12], self.BETA[0:12, pr * 128:(pr + 1) * 128],
                    self.IDENT[0:12, 0:12], [B("BETA"), B("IDENT")], [B(f"PS{p}")])
        self.CP(V, self.GT.rearrange("p a h -> p (a h)"), self.PS[p][:, 0:48], [B(f"PS{p}")], [B("GT")])
        self.CP(V, self.BT.rearrange("p a h -> p (a h)"), self.PS[p][:, 48:96], [B(f"PS{p}")], [B("BT")])
        p = self.next_ps()
        for pr in range(4):
            self.MM(self.PS[p][:, pr * 12:(pr + 1) * 12], self.TRI[:], self.GT[:, pr, :], True, True,
                    [B("TRI"), B("GT")], [B(f"PS{p}")])
            self.MM(self.PS[p][:, 48 + pr * 12:48 + (pr + 1) * 12], self.BLK[:], self.GT[:, pr, :], True, True,
                    [B("BLK"), B("GT")], [B(f"PS{p}")], skip_group_check=True)
        self.CP(V, self.GCT.rearrange("p a h -> p (a h)"), self.PS[p][:, 0:48], [B(f"PS{p}")], [B("GCT")])
        self.CP(V, self.GLT.rearrange("p a h -> p (a h)"), self.PS[p][:, 48:96], [B(f"PS{p}")], [B("GLT")])
        p = self.next_ps()
        for pr in range(4):
            self.MM(self.PS[p][0:12, pr * 128:(pr + 1) * 128], self.GT[:, pr, :], self.TRI[:], True, True,
                    [B("TRI"), B("GT")], [B(f"PS{p}")], skip_group_check=True)
        self.CP(V, self.GCF[0:12, :], self.PS[p][0:12, :], [B(f"PS{p}")], [B("GCF")])
        fl = lambda t_: t_.rearrange("p a h -> p (a h)")
        self.ACT(fl(self.SC1), fl(self.GCT), AF.Exp, [B("GCT")], [B("SC1")])
        self.TTo(V, fl(self.SC1), fl(self.SC1), fl(self.BT), ALU.mult, [B("SC1"), B("BT")], [B("SC1")])
        self.TTo(V, fl(self.SC2), fl(self.GLT), fl(self.GCT), ALU.subtract, [B("GLT"), B("GCT")], [B("SC2")])
        self.ACT(fl(self.SC2), fl(self.SC2), AF.Exp, [B("SC2")], [B("SC2")])

        def ev_qkv(c0):
            def f(i, ps, pbuf, m):
                ti = c0 // 128 + i
                xc, acc = self.XC[ti % 2], self.ACC[ti % 2]
                xb, ab = B(f"XC{ti % 2}"), B(f"ACC{ti % 2}")
                self.CP(G_, xc[:, 0:3], self.CONVST[:, j, ti, :], [B("CONVST")], [xb])
                self.CP("scalar", xc[:, 3:515], ps[:, :], [pbuf], [xb])
                self.CP(G_, self.CONVST[:, j, ti, :], xc[:, 512:515], [xb], [B("CONVST")])
                self.TS(V, acc, xc[:, 0:512], self.CW[:, j, ti, 0:1], None, ALU.mult, None, [xb, B("CW")], [ab])
                for k in range(1, 4):
                    self.STT(V, acc, xc[:, k:k + 512], self.CW[:, j, ti, k:k + 1], acc, ALU.mult, ALU.add,
                             [xb, B("CW"), ab], [ab])
                self.ACT(self.QKV[:, ti, :], acc, AF.Silu, [ab], [B(f"QKV{ti}")])
                if ti < 12:
                    self.ACT(self.SQ[:], self.QKV[:, ti, :], AF.Square, [B(f"QKV{ti}")], [B("SQ")])
                    pp = self.next_ps()
                    self.MM(self.PS[pp][:], self.ONES[:], self.SQ[:], True, True, [B("ONES"), B("SQ")], [B(f"PS{pp}")])
                    self.ACT(self.RSTD[:], self.PS[pp][:], AF.Sqrt, [B(f"PS{pp}"), B("EPSC")], [B("RSTD")], bias=self.eps_ap())
                    self.RCP(self.RSTD[:], self.RSTD[:], [B("RSTD")], [B("RSTD")])
                    if ti < 6:
                        self.STT(V, self.QKV[:, ti, :], self.QKV[:, ti, :], ISQ, self.RSTD[:], ALU.mult, ALU.mult,
                                 [B(f"QKV{ti}"), B("RSTD")], [B(f"QKV{ti}")])
                    else:
                        self.TTo(V, self.QKV[:, ti, :], self.QKV[:, ti, :], self.RSTD[:], ALU.mult,
                                 [B(f"QKV{ti}"), B("RSTD")], [B(f"QKV{ti}")])
            return f
        for c in range(6):
            self.proj_cols(wsrc, wbufs, c * 512, 512, ev_qkv(c * 512))

        def ev_gate(c):
            def f(i, ps, pbuf, m):
                self.ACT(self.SG[:, 4 * c + i, :], ps[:, :], AF.Silu, [pbuf], [B(f"SG{4 * c + i}")])
            return f
        for c in range(3):
            self.proj_cols(wsrc, wbufs, 3096 + c * 512, 512, ev_gate(c))
        self.proj_cols(wsrc, wbufs, 4632, 512,
                       lambda i, ps, pbuf, m: self.CP(V, self.QX[:, i, :], ps[:, :], [pbuf], [B(f"QX{i}")]))
        self.proj_cols(wsrc, wbufs, 5144, 512,
                       lambda i, ps, pbuf, m: self.ACT(self.SGX[:, i, :], ps[:, :], AF.Silu, [pbuf], [B(f"SGX{i}")]))
        self.xattn(l)

        self.P.barrier()
        for hp in range(6):
            gens = [self.gdn_head(j, 2 * hp, self.slots[0]), self.gdn_head(j, 2 * hp + 1, self.slots[1])]
            done = [False, False]
            while not all(done):
                for gi, g in enumerate(gens):
                    if not done[gi]:
                        try:
                            next(g)
                        except StopIteration:
                            done[gi] = True
        self.out_proj(l)

    for k_, v_ in list(locals().items()):
        if callable(v_):
            setattr(Builder, k_, v_)


_gdn_methods()


SEQ_FULL = 8192
N_CORES = 8


def kernel(**inputs):
    L = SEQ_FULL
    bld = Builder(L, layers=(0, 1, 2, 3), do_final=True, mix=True)
    nc = bld.build()
    shared = host_layout_mix(inputs)
    base = host_layout(inputs, 0, 0, L)
    in_maps = []
    for c in range(N_CORES):
        b = c % 4
        m = dict(base)
        m.update(shared)
        m["xT"] = np.ascontiguousarray(np.asarray(inputs["x"], dtype=np.float32)[b].T)
        m["memT"] = np.ascontiguousarray(np.asarray(inputs["mem"], dtype=np.float32)[b].T)
        in_maps.append(m)
    res = run_bass_kernel_spmd(nc, in_maps, core_ids=list(range(N_CORES)))
    out = np.stack([np.ascontiguousarray(res.results[b]["outT"].T) for b in range(4)], axis=0)
    return out.astype(np.float32)
```

```python
import math
from contextlib import ExitStack

import numpy as np
import concourse.bass as bass
import concourse.mybir as mybir
from concourse.bass_utils import run_bass_kernel_spmd

F32 = mybir.dt.float32
BF16 = mybir.dt.bfloat16
AF = mybir.ActivationFunctionType
ALU = mybir.AluOpType
AX = mybir.AxisListType

EPOCH = 20000
DEBUG_NAMES = None
SAME_ENGINE_SYNC = True
N_DMA_SEMS = 32
N_SW_SEMS = 8
ENGINES = ("tensor", "vector", "scalar", "gpsimd", "sync")


class Buf:
    __slots__ = ("name", "w", "rs")

    def __init__(self, name):
        self.name = name
        self.w = None
        self.rs = []


class Op:
    __slots__ = ("eng", "fn", "reads", "writes", "dma", "waits", "signal", "tok", "idx", "dsem", "src")

    def __init__(self, eng, fn, reads, writes, dma):
        self.eng = eng
        self.fn = fn
        self.reads = reads
        self.writes = writes
        self.dma = dma
        self.waits = []
        self.signal = False
        self.tok = None
        self.dsem = None


class Prog:
    def __init__(self, nc):
        self.nc = nc
        self.ops = []

    def op(self, eng, fn, reads=(), writes=(), dma=False):
        o = Op(eng, fn, tuple(reads), tuple(writes), dma)
        o.idx = len(self.ops)
        import sys as _s
        fr = _s._getframe(1)
        o.src = []
        while fr is not None and len(o.src) < 4:
            o.src.append(fr.f_lineno)
            fr = fr.f_back
        self.ops.append(o)
        return o

    def mm(self, fn, reads, writes):
        return self.op("tensor", fn, reads, writes)

    def dve(self, fn, reads, writes):
        return self.op("vector", fn, reads, writes)

    def act(self, fn, reads, writes):
        return self.op("scalar", fn, reads, writes)

    def pool(self, fn, reads, writes):
        return self.op("gpsimd", fn, reads, writes)

    def dma(self, eng, out, in_, reads, writes):
        return self.op(eng, lambda e: e.dma_start(out=out, in_=in_), reads, writes, dma=True)

    def barrier(self):
        for e in ENGINES:
            o = self.op(e, None)
            o.dsem = "barrier"

    def analyse(self):
        ops = self.ops
        last_real = {}
        last_dmas = []
        seen = {e: {} for e in ENGINES}
        dma_rr = 0
        sw_rr = 0
        dma_last = [None] * N_DMA_SEMS
        for o in ops:
            deps = set()
            for b in o.reads:
                if b.w is not None:
                    deps.add(b.w)
            for b in o.writes:
                if b.w is not None:
                    deps.add(b.w)
                for r in b.rs:
                    deps.add(r)
            if o.dsem == "barrier":
                for e2, i2 in last_real.items():
                    if e2 != o.eng:
                        deps.add(i2)
                for i2 in last_dmas:
                    deps.add(i2)
            if o.dma:
                if o.eng == "gpsimd":
                    k = N_DMA_SEMS - N_SW_SEMS + sw_rr % N_SW_SEMS
                    sw_rr += 1
                else:
                    k = dma_rr % (N_DMA_SEMS - N_SW_SEMS)
                    dma_rr += 1
                o.dsem = k
                if dma_last[k] is not None:
                    deps.add(dma_last[k])
                dma_last[k] = o.idx
            sn = seen[o.eng]
            best = {}
            dl = []
            for d in deps:
                p = ops[d]
                if p.dma:
                    dl.append(d)
                elif best.get(p.eng, -1) < d:
                    best[p.eng] = d
            for d in sorted(dl + list(best.values())):
                p = ops[d]
                if p.dma:
                    key = ("dma", p.idx)
                    if key in sn:
                        continue
                    sn[key] = True
                    o.waits.append(d)
                else:
                    if p.eng == o.eng and (p.eng == "tensor" or not SAME_ENGINE_SYNC):
                        continue
                    if sn.get(p.eng, -1) >= d:
                        continue
                    sn[p.eng] = d
                    p.signal = True
                    o.waits.append(d)
            if o.fn is None:
                continue
            if o.dma:
                last_dmas.append(o.idx)
                if len(last_dmas) > N_DMA_SEMS:
                    last_dmas.pop(0)
            else:
                last_real[o.eng] = o.idx
            for b in o.reads:
                b.rs.append(o.idx)
            for b in o.writes:
                b.w = o.idx
                b.rs = []
        cnt = {e: 0 for e in ENGINES}
        dcnt = [0] * N_DMA_SEMS
        self.n_epochs = {e: 1 for e in ENGINES}
        for o in ops:
            if o.dma:
                dcnt[o.dsem] += 16
                o.tok = ("d", o.dsem, dcnt[o.dsem])
            elif o.signal:
                c = cnt[o.eng]
                cnt[o.eng] += 1
                ep = c // EPOCH
                self.n_epochs[o.eng] = max(self.n_epochs[o.eng], ep + 1)
                o.tok = ("e", o.eng, ep, c % EPOCH + 1)

    def emit(self):
        nc = self.nc
        self.analyse()
        with ExitStack() as es:
            esem = {}
            for e in ENGINES:
                esem[e] = [es.enter_context(nc.semaphore(f"s_{e}_{i}")) for i in range(self.n_epochs[e])]
            dsem = [es.enter_context(nc.semaphore(f"s_dma_{i}")) for i in range(N_DMA_SEMS)]
            block = es.enter_context(nc.Block())
            by_eng = {e: [o for o in self.ops if o.eng == e] for e in ENGINES}
            ops = self.ops

            def run(eng_name):
                def body(eng):
                    for o in by_eng[eng_name]:
                        for d in o.waits:
                            t = ops[d].tok
                            if t[0] == "d":
                                eng.wait_ge(dsem[t[1]], t[2])
                            else:
                                eng.wait_ge(esem[t[1]][t[2]], t[3])
                        if o.fn is None:
                            continue
                        try:
                            inst = o.fn(eng)
                        except Exception:
                            print("FAILED OP at lines", o.src, "engine", o.eng)
                            raise
                        if DEBUG_NAMES is not None:
                            try:
                                DEBUG_NAMES.append((str(getattr(inst, "name", None) or getattr(getattr(inst, "ins", None), "name", None)), o.src, o.eng))
                            except Exception:
                                pass
                        if o.dma:
                            inst.then_inc(dsem[o.dsem], 16)
                        elif o.signal:
                            inst.then_inc(esem[o.eng][o.tok[2]], 1)
                return body

            block.tensor(run("tensor"))
            block.vector(run("vector"))
            block.scalar(run("scalar"))
            block.gpsimd(run("gpsimd"))
            block.sync(run("sync"))


D = 1024
KT = 8
MEM = 256
TW = 1536
XW = 512
S5_IN = 4096
GDN_IN = 5656
TT = 512
NCH = 64
EPS = 1e-6
ISQ = 1.0 / math.sqrt(128.0)


class Builder:
    def __init__(self, L, layers=(0, 1, 2, 3), do_final=True, mix=True):
        self.L = L
        self.n_tiles = L // TT
        self.layers = tuple(layers)
        self.do_final = do_final
        self.mix = mix
        self.truncate = None
        self.nc = bass.Bass("TRN2", target_bir_lowering=False)
        self.P = Prog(self.nc)
        self.es = ExitStack()
        self.bufs = {}

    def din(self, name, shape):
        return self.nc.dram_tensor(name, list(shape), F32, kind="ExternalInput").ap()

    def dscr(self, name, shape, dt=BF16):
        return self.nc.dram_tensor(name, list(shape), dt, kind="Internal").ap()

    def sb(self, name, shape, dt):
        return self.es.enter_context(self.nc.sbuf_tensor(name, list(shape), dt))

    def ps(self, name, shape, dt=F32):
        return self.es.enter_context(self.nc.psum_tensor(name, list(shape), dt))

    def B(self, name):
        b = self.bufs.get(name)
        if b is None:
            b = self.bufs[name] = Buf(name)
        return b

    def MM(self, out, lhsT, rhs, start, stop, reads, writes, **kw):
        self.P.mm(lambda e: e.matmul(out, lhsT, rhs, start=start, stop=stop, **kw), reads, writes)

    def TR(self, out, in_, ident, reads, writes):
        self.P.mm(lambda e: e.transpose(out, in_, ident), reads, writes)

    def ACT(self, out, in_, func, reads, writes, **kw):
        self.P.act(lambda e: e.activation(out=out, in_=in_, func=func, **kw), reads, writes)

    def TTo(self, eng, out, in0, in1, op, reads, writes):
        self.P.op(eng, lambda e: e.tensor_tensor(out=out, in0=in0, in1=in1, op=op), reads, writes)

    def TS(self, eng, out, in0, s1, s2, op0, op1, reads, writes):
        if op1 is None:
            self.P.op(eng, lambda e: e.tensor_scalar(out=out, in0=in0, scalar1=s1, scalar2=None, op0=op0), reads, writes)
        else:
            self.P.op(eng, lambda e: e.tensor_scalar(out=out, in0=in0, scalar1=s1, scalar2=s2, op0=op0, op1=op1), reads, writes)

    def STT(self, eng, out, in0, scalar, in1, op0, op1, reads, writes):
        self.P.op(eng, lambda e: e.scalar_tensor_tensor(out=out, in0=in0, scalar=scalar, in1=in1, op0=op0, op1=op1),
                  reads, writes)

    def CP(self, eng, out, in_, reads, writes):
        if eng == "scalar":
            self.P.op(eng, lambda e: e.activation(out=out, in_=in_, func=AF.Copy), reads, writes)
        else:
            self.P.op(eng, lambda e: e.tensor_copy(out=out, in_=in_), reads, writes)

    def MS(self, eng, ap, val, writes):
        self.P.op(eng, lambda e: e.memset(ap, val), [], writes)

    def RCP(self, out, in_, reads, writes):
        self.P.dve(lambda e: e.reciprocal(out=out, in_=in_), reads, writes)

    def DMA(self, eng, out, in_, reads, writes):
        self.P.dma(eng, out, in_, reads, writes)

    def declare(self):
        L = self.L
        d = self.dr = {}
        d["xT"] = self.din("xT", [D, L])
        d["memT"] = self.din("memT", [D, MEM])
        d["nw"] = self.din("nw", [128, 4, KT])
        d["mnw"] = self.din("mnw", [128, 4, KT])
        d["fnw"] = self.din("fnw", [128, KT])
        d["win_s5"] = self.din("win_s5", [2, 128, KT, S5_IN])
        d["win_gdn"] = self.din("win_gdn", [2, 128, KT, GDN_IN])
        d["wout"] = self.din("wout", [4, 128, 16, D])
        d["wkv"] = self.din("wkv", [4, 128, KT, D])
        d["wglu"] = self.din("wglu", [2, 128, 12, TW])
        d["bglu"] = self.din("bglu", [128, 2, 12])
        self.outT = self.nc.dram_tensor("outT", [D, L], F32, kind="ExternalOutput").ap()
        s = self.sc = {}
        s["win_s5"] = self.dscr("win_s5_b", [2, 128, KT, S5_IN])
        s["win_gdn"] = self.dscr("win_gdn_b", [2, 128, KT, GDN_IN])
        s["wout"] = self.dscr("wout_b", [4, 128, 16, D])
        s["wkv"] = self.dscr("wkv_b", [4, 128, KT, D])
        s["wglu"] = self.dscr("wglu_b", [2, 128, 12, TW])

    def alloc_common(self):
        sb, ps = self.sb, self.ps
        self.X = sb("X", [128, KT, TT], F32)
        self.H = sb("H", [128, KT, TT], BF16)
        self.SQ = sb("SQ", [128, TT], BF16)
        self.RSTD = sb("RSTD", [128, TT], F32)
        self.YM = sb("YM", [128, 12, TT], BF16)
        self.YX = sb("YX", [128, 4, TT], BF16)
        self.QX = sb("QX", [128, 4, TT], BF16)
        self.SGX = sb("SGX", [128, 4, TT], BF16)
        self.SG = sb("SG", [128, 12, TT], BF16)
        self.EX = sb("EX", [128, 2, TT], BF16)
        if not self.mix:
            self.RDEN = sb("RDEN", [128, TT], F32)
        self.OTMP = self.RSTD
        self.KTs = sb("KTs", [128, 4, 4, MEM], BF16)
        self.Vs = sb("Vs", [128, 4, 2, XW], BF16)
        self.ONES = sb("ONES", [128, 128], BF16)
        self.NW = sb("NW", [128, 4, KT], F32)
        self.MNW = sb("MNW", [128, 4, KT], F32)
        self.FNW = sb("FNW", [128, KT], F32)
        self.BGLU = sb("BGLU", [128, 2, 12], F32)
        self.NWB = 4
        self.WB = [sb(f"WB{i}", [128, 8 * 512], BF16) for i in range(self.NWB)]
        self.wb_rr = 0
        self.PS = [ps(f"PS{i}", [128, 512]) for i in range(8)]
        self.ps_rr = 0

    def next_ps(self, lo=0, hi=4):
        i = lo + self.ps_rr % (hi - lo)
        self.ps_rr += 1
        return i

    def load_w(self, src, K, ncols, rbufs):
        i = self.wb_rr % self.NWB
        self.wb_rr += 1
        wb = self.WB[i][:, 0:K * ncols].rearrange("p (k n) -> p k n", k=K)
        eng = "sync"
        self.DMA(eng, wb, src, rbufs, [self.B(f"WB{i}")])
        return wb, self.B(f"WB{i}")

    def prologue(self):
        d, s, B = self.dr, self.sc, self.B
        for name, n0, K in (("win_s5", 2, KT), ("win_gdn", 2, KT), ("wout", 4, 16), ("wkv", 4, KT), ("wglu", 2, 12)):
            for j in range(n0):
                for k in range(K):
                    self.DMA("gpsimd", s[name][j, :, k, :], d[name][j, :, k, :], [], [B(f"{name}_b{j}_{k}")])
        self.DMA("sync", self.NW[:], d["nw"], [], [B("NW")])
        self.DMA("sync", self.MNW[:], d["mnw"], [], [B("MNW")])
        self.DMA("sync", self.FNW[:], d["fnw"], [], [B("FNW")])
        self.DMA("sync", self.BGLU[:], d["bglu"], [], [B("BGLU")])
        self.MS("vector", self.ONES[:], 1.0, [B("ONES")])
        if not self.mix:
            self.MS("vector", self.YM[:], 0.0, [B(f"YM{j}") for j in range(12)])
        memT = self.X[:, :, 0:MEM]
        for k in range(KT):
            self.DMA("sync", self.X[:, k, 0:MEM], d["memT"][k * 128:(k + 1) * 128, :], [], [B(f"X{k}")])
        pb = 4
        for k in range(KT):
            self.ACT(self.SQ[:, 0:MEM], self.X[:, k, 0:MEM], AF.Square, [B(f"X{k}")], [B("SQ")])
            self.MM(self.PS[pb][:, 0:MEM], self.ONES[:], self.SQ[:, 0:MEM], k == 0, k == KT - 1,
                    [B("ONES"), B("SQ")], [B(f"PS{pb}")])
        self.ACT(self.RSTD[:, 0:MEM], self.PS[pb][:, 0:MEM], AF.Sqrt, [B(f"PS{pb}"), B("EPSC")], [B("RSTD")],
                 scale=1.0 / D, bias=self.eps_ap())
        self.RCP(self.RSTD[:, 0:MEM], self.RSTD[:, 0:MEM], [B("RSTD")], [B("RSTD")])
        for l in self.layers:
            for k in range(KT):
                self.STT("vector", self.H[:, k, 0:MEM], self.X[:, k, 0:MEM], self.MNW[:, l, k:k + 1],
                         self.RSTD[:, 0:MEM], ALU.mult, ALU.mult,
                         [B(f"X{k}"), B("MNW"), B("RSTD")], [B(f"H{k}")])
            for half in range(2):
                wb, wbuf = self.load_w(s["wkv"][l, :, :, half * 512:(half + 1) * 512], KT, 512,
                                       [B(f"wkv_b{l}_{k}") for k in range(KT)])
                if half == 0:
                    for h in range(4):
                        p = self.next_ps()
                        for k in range(KT):
                            self.MM(self.PS[p][:, 0:MEM], wb[:, k, h * 128:(h + 1) * 128], self.H[:, k, 0:MEM],
                                    k == 0, k == KT - 1, [wbuf, B(f"H{k}")], [B(f"PS{p}")])
                        self.CP("scalar", self.KTs[:, l, h, :], self.PS[p][:, 0:MEM], [B(f"PS{p}")], [B("KTs")])
                else:
                    for mt in range(2):
                        p = self.next_ps()
                        for k in range(KT):
                            self.MM(self.PS[p][:, :], self.H[:, k, mt * 128:(mt + 1) * 128], wb[:, k, :],
                                    k == 0, k == KT - 1, [wbuf, B(f"H{k}")], [B(f"PS{p}")])
                        self.CP("vector", self.Vs[:, l, mt, :], self.PS[p][:, :], [B(f"PS{p}")], [B("Vs")])

    def eps_ap(self):
        if not hasattr(self, "_eps"):
            self._eps = self.sb("EPSC", [128, 1], F32)
            self.MS("vector", self._eps[:], EPS, [self.B("EPSC")])
        return self._eps[:, 0:1]

    def rms_stats(self):
        B = self.B
        pb = 4
        for k in range(KT):
            self.ACT(self.SQ[:], self.X[:, k, :], AF.Square, [B(f"X{k}")], [B("SQ")])
            self.MM(self.PS[pb][:], self.ONES[:], self.SQ[:], k == 0, k == KT - 1, [B("ONES"), B("SQ")], [B(f"PS{pb}")])
        self.ACT(self.RSTD[:], self.PS[pb][:], AF.Sqrt, [B(f"PS{pb}"), B("EPSC")], [B("RSTD")], scale=1.0 / D, bias=self.eps_ap())
        self.RCP(self.RSTD[:], self.RSTD[:], [B("RSTD")], [B("RSTD")])

    def norm_in(self, l):
        B = self.B
        self.rms_stats()
        for k in range(KT):
            self.STT("vector", self.H[:, k, :], self.X[:, k, :], self.NW[:, l, k:k + 1], self.RSTD[:],
                     ALU.mult, ALU.mult, [B(f"X{k}"), B("NW"), B("RSTD")], [B(f"H{k}")])

    def proj_cols(self, wsrc, wbufs, col0, ncols, evac):
        B = self.B
        wb, wbuf = self.load_w(wsrc[:, :, col0:col0 + ncols], KT, ncols, wbufs)
        nt = (ncols + 127) // 128
        for i in range(nt):
            m = min(128, ncols - i * 128)
            p = self.next_ps()
            for k in range(KT):
                self.MM(self.PS[p][0:m, :], wb[:, k, i * 128:i * 128 + m], self.H[:, k, :], k == 0, k == KT - 1,
                        [wbuf, B(f"H{k}")], [B(f"PS{p}")])
            evac(i, self.PS[p], B(f"PS{p}"), m)

    def xattn(self, l):
        B = self.B
        for h in range(4):
            sp = [5, 6]
            for mt in range(2):
                self.MM(self.PS[sp[mt]][:], self.KTs[:, l, h, mt * 128:(mt + 1) * 128], self.QX[:, h, :], True, True,
                        [B("KTs"), B(f"QX{h}")], [B(f"PS{sp[mt]}")])
                self.ACT(self.EX[:, mt, :], self.PS[sp[mt]][:], AF.Exp, [B(f"PS{sp[mt]}")], [B(f"EX{mt}")], scale=ISQ)
            for mt in range(2):
                self.MM(self.PS[7][:], self.ONES[:], self.EX[:, mt, :], mt == 0, mt == 1,
                        [B("ONES"), B(f"EX{mt}")], [B("PS7")])
            p = self.next_ps()
            for mt in range(2):
                self.MM(self.PS[p][:], self.Vs[:, l, mt, h * 128:(h + 1) * 128], self.EX[:, mt, :], mt == 0, mt == 1,
                        [B("Vs"), B(f"EX{mt}")], [B(f"PS{p}")])
            self.RCP(self.RDEN[:], self.PS[7][:], [B("PS7")], [B("RDEN")])
            self.TTo("vector", self.OTMP[:], self.PS[p][:], self.RDEN[:], ALU.mult, [B(f"PS{p}"), B("RDEN")], [B("RSTD")])
            self.TTo("gpsimd", self.YX[:, h, :], self.OTMP[:], self.SGX[:, h, :], ALU.mult,
                     [B("RSTD"), B(f"SGX{h}")], [B(f"YX{h}")])

    def out_proj(self, l):
        B, s = self.B, self.sc
        rb = [B(f"YM{j}") for j in range(12)] + [B(f"YX{h}") for h in range(4)]
        for half in range(4):
            wb, wbuf = self.load_w(s["wout"][l, :, :, half * 256:(half + 1) * 256], 16, 256,
                                   [B(f"wout_b{l}_{k}") for k in range(16)])
            for i in range(2):
                m = half * 2 + i
                p = self.next_ps()
                for k in range(16):
                    rhs = self.YM[:, k, :] if k < 12 else self.YX[:, k - 12, :]
                    self.MM(self.PS[p][:], wb[:, k, i * 128:(i + 1) * 128], rhs, k == 0, k == 15,
                            [wbuf, rb[k]], [B(f"PS{p}")])
                self.TTo("vector", self.X[:, m, :], self.X[:, m, :], self.PS[p][:], ALU.add,
                         [B(f"X{m}"), B(f"PS{p}")], [B(f"X{m}")])

    def final_norm_store(self, t):
        B = self.B
        self.rms_stats()
        OUT = self.H_as_f32()
        for k in range(KT):
            ob = [B(f"H{2 * (k % 2) + q_}") for q_ in range(2)]
            self.STT("vector", OUT[k % 2], self.X[:, k, :], self.FNW[:, k:k + 1], self.RSTD[:], ALU.mult, ALU.mult,
                     [B(f"X{k}"), B("FNW"), B("RSTD")], ob)
            self.DMA("sync", self.outT[k * 128:(k + 1) * 128, t * TT:(t + 1) * TT], OUT[k % 2], ob,
                     [B(f"out_{t}_{k}")])
            self.out_bufs.append(B(f"out_{t}_{k}"))

    def H_as_f32(self):
        if not hasattr(self, "_outf"):
            self._outf = self.H[:, 0:4, :].rearrange("p k n -> p (k n)").bitcast(F32).rearrange("p (a n) -> p a n", a=2)
        return [self._outf[:, 0, :], self._outf[:, 1, :]]

    def alloc_s5(self):
        sb = self.sb
        self.U = self.YM

    def s5_layer(self, l):
        B, s = self.B, self.sc
        j = l // 2
        self.norm_in(l)
        wbufs = [B(f"win_s5_b{j}_{k}") for k in range(KT)]

        def evac(c0):
            def f(i, ps, pbuf, m):
                gi = c0 // 128 + i
                if gi < 12:
                    self.CP("vector" if gi % 2 else "scalar",
                            self.U[:, gi, :].rearrange("p (s c) -> p s c", s=8),
                            ps[:, :].rearrange("p (c s) -> p s c", s=8), [pbuf], [B(f"YM{gi}")])
                elif gi < 24:
                    self.ACT(self.SG[:, gi - 12, :], ps[:, :], AF.Silu, [pbuf], [B(f"SG{gi - 12}")])
                elif gi < 28:
                    self.CP("vector", self.QX[:, gi - 24, :], ps[:, :], [pbuf], [B(f"QX{gi - 24}")])
                else:
                    self.ACT(self.SGX[:, gi - 28, :], ps[:, :], AF.Silu, [pbuf], [B(f"SGX{gi - 28}")])
            return f

        order = [0, 512, 1024, 3072, 3584, 1536, 2048, 2560]
        for c0 in order[:3]:
            self.proj_cols(s["win_s5"][j], wbufs, c0, 512, evac(c0))
        if self.mix:
            self.s5_state_part(j)
        for c0 in order[3:]:
            self.proj_cols(s["win_s5"][j], wbufs, c0, 512, evac(c0))
        self.xattn(l)
        if self.mix:
            self.s5_out_part(j)
        self.out_proj(l)

    def build(self):
        B = self.B
        self.declare()
        self.alloc_common()
        self.alloc_s5()
        if self.mix:
            self.alloc_s5_mix()
            self.alloc_gdn()
        self.out_bufs = []
        self.marks = {}
        self.prologue()
        self.marks["prologue"] = len(self.P.ops)
        if self.mix:
            self.s5_precompute()
            self.gdn_consts()
        self.marks["precompute"] = len(self.P.ops)
        for t in range(self.n_tiles):
            for k in range(KT):
                self.DMA("sync", self.X[:, k, :], self.dr["xT"][k * 128:(k + 1) * 128, t * TT:(t + 1) * TT],
                         [], [B(f"X{k}")])
            for l in self.layers:
                if l % 2 == 0:
                    self.s5_layer(l)
                else:
                    self.gdn_layer(l)
                self.P.barrier()
            if self.do_final:
                self.final_norm_store(t)
            else:
                for k in range(KT):
                    self.DMA("sync", self.outT[k * 128:(k + 1) * 128, t * TT:(t + 1) * TT], self.X[:, k, :],
                             [B(f"X{k}")], [B(f"out_{t}_{k}")])
                    self.out_bufs.append(B(f"out_{t}_{k}"))
        self.marks["end"] = len(self.P.ops)
        if self.truncate is not None:
            del self.P.ops[self.truncate:]
            self.out_bufs = []
            self.DMA("sync", self.outT[0:128, 0:TT], self.X[:, 0, :], [B("X0")], [B("out_dbg")])
            self.out_bufs.append(B("out_dbg"))
        self.P.op("sync", None, reads=self.out_bufs)
        self.P.emit()
        self.es.close()
        return self.nc


def _kt(w, K):
    return np.ascontiguousarray(w.reshape(K, 128, -1).transpose(1, 0, 2))


def host_layout(inp, b, t0, L):
    f = lambda a: np.ascontiguousarray(np.asarray(a, dtype=np.float32))
    m = {}
    m["xT"] = f(np.asarray(inp["x"])[b, t0:t0 + L, :].T)
    m["memT"] = f(np.asarray(inp["mem"])[b].T)
    m["nw"] = f(np.asarray(inp["norm_w"]).reshape(4, KT, 128).transpose(2, 0, 1))
    m["mnw"] = f(np.asarray(inp["mem_norm_w"]).reshape(4, KT, 128).transpose(2, 0, 1))
    m["fnw"] = f(np.asarray(inp["final_norm_w"]).reshape(KT, 128).T)
    m["win_s5"] = f(np.stack([_kt(np.asarray(inp["s5_w_in"])[j], KT) for j in range(2)]))
    m["win_gdn"] = f(np.stack([_kt(np.asarray(inp["gdn_w_in"])[j], KT) for j in range(2)]))
    m["wout"] = f(np.stack([_kt(np.asarray(inp["w_out"])[i], 16) for i in range(4)]))
    m["wkv"] = f(np.stack([_kt(np.asarray(inp["w_mem_kv"])[i], KT) for i in range(4)]))
    m["wglu"] = f(np.stack([_kt(np.asarray(inp["s5_w_glu"])[j], 12) for j in range(2)]))
    m["bglu"] = f(np.asarray(inp["s5_b_glu"]).reshape(2, 12, 128).transpose(2, 0, 1))
    return m


def host_layout_mix(inp):
    f = lambda a: np.ascontiguousarray(np.asarray(a, dtype=np.float32))
    m = {}
    lr = np.asarray(inp["s5_lambda_re"]); li = np.asarray(inp["s5_lambda_im"]); ls = np.asarray(inp["s5_log_step"])
    sp = lambda a: a.reshape(2, 48, 2, 64).transpose(2, 3, 0, 1).reshape(128, 2, 48)
    m["lamr_sp"] = f(sp(lr)); m["lami_sp"] = f(sp(li))
    m["lst_sp"] = f(sp(np.broadcast_to(ls[:, :, None], (2, 96, 64))))
    spc = lambda a: a.reshape(2, 48, 2, 16, 64).transpose(2, 4, 0, 1, 3).reshape(128, 2, 48, 16)
    m["cre_sp"] = f(spc(np.asarray(inp["s5_c_re"]))); m["cim_sp"] = f(spc(np.asarray(inp["s5_c_im"])))
    spb = lambda a: a.reshape(2, 48, 2, 64, 16).transpose(2, 3, 0, 1, 4).reshape(128, 2, 48, 16)
    m["bre_sp"] = f(spb(np.asarray(inp["s5_b_re"]))); m["bim_sp"] = f(spb(np.asarray(inp["s5_b_im"])))
    cpl = lambda a: np.broadcast_to(a.reshape(2, 12, 8, 1, 64), (2, 12, 8, 16, 64)).transpose(2, 3, 0, 1, 4).reshape(128, 2, 12, 64)
    m["lamr_cp"] = f(cpl(lr)); m["lami_cp"] = f(cpl(li))
    m["lst_cp"] = f(cpl(np.broadcast_to(ls[:, :, None], (2, 96, 64))))
    cpb = lambda a: a.reshape(2, 12, 8, 64, 16).transpose(2, 4, 0, 1, 3).reshape(128, 2, 12, 64)
    m["bre_cp"] = f(cpb(np.asarray(inp["s5_b_re"]))); m["bim_cp"] = f(cpb(np.asarray(inp["s5_b_im"])))
    m["d_cp"] = f(np.asarray(inp["s5_d"]).reshape(2, 12, 128).transpose(2, 0, 1))
    m.update(host_layout_gdn(inp))
    return m


PI = math.pi


def _s5_methods():
    def declare_mix(self):
        d = self.dr
        for n, shp in (("lamr_sp", [128, 2, 48]), ("lami_sp", [128, 2, 48]), ("lst_sp", [128, 2, 48]),
                       ("cre_sp", [128, 2, 48, 16]), ("cim_sp", [128, 2, 48, 16]),
                       ("bre_sp", [128, 2, 48, 16]), ("bim_sp", [128, 2, 48, 16]),
                       ("lamr_cp", [128, 2, 12, 64]), ("lami_cp", [128, 2, 12, 64]), ("lst_cp", [128, 2, 12, 64]),
                       ("bre_cp", [128, 2, 12, 64]), ("bim_cp", [128, 2, 12, 64]), ("d_cp", [128, 2, 12])):
            d[n] = self.din(n, shp)
        self.sc["Ka"] = self.dscr("Ka_d", [2, 12, 128, 8 * 128])
        self.sc["Wb"] = self.dscr("Wb_d", [2, 12, 128, 8 * 2 * 128])
        self.sc["Wc"] = self.dscr("Wc_d", [2, 12, 128, 4 * 8 * 2 * 32])
        self.declare_gdn()

    def alloc_s5_mix(self):
        sb = self.sb
        self.declare_mix()
        A = self.ARENA = sb("ARENA", [128, 36 * 1024], BF16)
        o = 0

        def carve(nbytes):
            nonlocal o
            v = A[:, o // 2:(o + nbytes) // 2]
            o += nbytes
            return v
        self.RDEN = A[:, 35 * 1024:36 * 1024].bitcast(F32)
        self.Y = carve(12 * TT * 2).rearrange("p (j n) -> p j n", j=12)
        self.XS = carve(2 * 48 * 65 * 4).bitcast(F32).rearrange("p (r g c) -> p r g c", r=2, g=48)
        self.XPb = carve(2 * 48 * 64 * 2).rearrange("p (r g c) -> p r g c", r=2, g=48)
        self.TA = carve(2 * 48 * 4).bitcast(F32).rearrange("p (r g) -> p r g", r=2)
        self.TB = carve(2 * 48 * 4).bitcast(F32).rearrange("p (r g) -> p r g", r=2)
        self.GTS5 = carve(TT * 2)
        self.s5_arena_end = o
        self.U = self.YM
        self.AR2 = sb("AR2", [128, 2, 2, 48], F32)
        self.AIS = sb("AIS", [128, 2, 2, 48], F32)
        self.ST5 = sb("ST5", [128, 2, 2, 48], F32)
        self.IDENT = sb("IDENT", [128, 128], F32)
        self.IDENTB = sb("IDENTB", [128, 128], BF16)
        self.MSP = sb("MSP", [128, 2], F32)
        self.MCP = sb("MCP", [128, 2], F32)
        self.DCP = sb("DCP", [128, 2, 12], F32)

    def s5_precompute(self):
        B, d, s = self.B, self.dr, self.sc
        nc = self.nc
        A = self.ARENA
        o = 0

        def carve(shape, dt=F32):
            nonlocal o
            n = int(np.prod(shape))
            nb = n * (4 if dt == F32 else 2)
            v = A[:, o // 2:(o + nb) // 2]
            o += nb
            if dt == F32:
                v = v.bitcast(F32)
            if len(shape) > 1:
                names = " ".join(f"a{i}" for i in range(len(shape)))
                kw = {f"a{i}": shape[i] for i in range(len(shape) - 1)}
                v = v.rearrange(f"p ({names}) -> p {names}", **kw)
            return v
        ar = B("ARENA")
        V = "vector"
        self.P.pool(lambda e: e.iota(self.IDENT[:].bitcast(mybir.dt.int32), pattern=[[1, 128]], base=0, channel_multiplier=-1),
                    [], [B("IDENT")])
        self.CP(V, self.IDENT[:], self.IDENT[:].bitcast(mybir.dt.int32), [B("IDENT")], [B("IDENT")])
        self.TS(V, self.IDENT[:], self.IDENT[:], 0.0, None, ALU.is_equal, None, [B("IDENT")], [B("IDENT")])
        self.CP(V, self.IDENTB[:], self.IDENT[:], [B("IDENT")], [B("IDENTB")])
        self.MS(V, self.MSP[:], 0.0, [B("MSP")])
        self.MS(V, self.MSP[0:64, 0:1], 1.0, [B("MSP")])
        self.MS(V, self.MSP[64:128, 1:2], 1.0, [B("MSP")])
        onesf = carve([1])
        self.MS(V, onesf, 1.0, [ar])
        self.MS(V, self.MCP[:], 0.0, [B("MCP")])
        for blk in range(8):
            g2 = blk % 2
            self.DMA("sync", self.MCP[blk * 16:(blk + 1) * 16, g2:g2 + 1], onesf[0:16, :], [ar, B("MCP")], [B("MCP")])
        self.DMA("sync", self.DCP[:], d["d_cp"], [], [B("DCP")])
        self.MS(V, self.ST5[:], 0.0, [B("ST5")])
        base_o = o

        def chain(lamr, lami, lst, n, npow, rev):
            lr_ = carve([n]); li_ = carve([n]); dt = carve([n]); t1 = carve([n]); t2 = carve([n]); mag = carve([n])
            sn = carve([n]); cs = carve([n]); cr = carve([n]); ci = carve([n])
            Pr = carve([npow, n]); Pi = carve([npow, n])
            self.DMA("sync", lr_, lamr, [], [ar]); self.DMA("sync", li_, lami, [], [ar]); self.DMA("sync", dt, lst, [], [ar])
            self.ACT(dt, dt, AF.Exp, [ar], [ar])
            self.TTo(V, t1, lr_, dt, ALU.mult, [ar], [ar])
            self.TTo(V, t2, li_, dt, ALU.mult, [ar], [ar])
            self.ACT(mag, t1, AF.Exp, [ar], [ar])
            ni_ = carve([n]); mk = carve([n])

            def sin_of(dst, src, shift):
                self.TS(V, dst, src, shift, None, ALU.add, None, [ar], [ar])
                self.TS(V, mk, dst, 1.0 / (2 * PI), None, ALU.mult, None, [ar], [ar])
                self.CP(V, ni_.bitcast(mybir.dt.int32), mk, [ar], [ar])
                self.CP(V, mk, ni_.bitcast(mybir.dt.int32), [ar], [ar])
                self.STT(V, dst, mk, -2 * PI, dst, ALU.mult, ALU.add, [ar], [ar])
                self.TS(V, mk, dst, PI, None, ALU.is_gt, None, [ar], [ar])
                self.STT(V, dst, mk, -2 * PI, dst, ALU.mult, ALU.add, [ar], [ar])
                self.TS(V, mk, dst, -PI, None, ALU.is_lt, None, [ar], [ar])
                self.STT(V, dst, mk, 2 * PI, dst, ALU.mult, ALU.add, [ar], [ar])
                self.ACT(dst, dst, AF.Sin, [ar], [ar])
            sin_of(sn, t2, 0.0)
            sin_of(cs, t2, 0.5 * PI)
            a1r, a1i = t1, t2
            self.TTo(V, a1r, mag, cs, ALU.mult, [ar], [ar])
            self.TTo(V, a1i, mag, sn, ALU.mult, [ar], [ar])
            nr = carve([n]); den = carve([n]); tt = carve([n])
            self.TS(V, nr, a1r, -1.0, None, ALU.add, None, [ar], [ar])
            self.TTo(V, den, lr_, lr_, ALU.mult, [ar], [ar])
            self.TTo(V, tt, li_, li_, ALU.mult, [ar], [ar])
            self.TTo(V, den, den, tt, ALU.add, [ar], [ar])
            self.RCP(den, den, [ar], [ar])
            self.TTo(V, cr, nr, lr_, ALU.mult, [ar], [ar])
            self.TTo(V, tt, a1i, li_, ALU.mult, [ar], [ar])
            self.TTo(V, cr, cr, tt, ALU.add, [ar], [ar])
            self.TTo(V, cr, cr, den, ALU.mult, [ar], [ar])
            self.TTo(V, ci, a1i, lr_, ALU.mult, [ar], [ar])
            self.TTo(V, tt, nr, li_, ALU.mult, [ar], [ar])
            self.TTo(V, ci, ci, tt, ALU.subtract, [ar], [ar])
            self.TTo(V, ci, ci, den, ALU.mult, [ar], [ar])
            ix = (lambda k: npow - 1 - k) if rev else (lambda k: k)
            self.MS(V, Pr[:, ix(0), :], 1.0, [ar]); self.MS(V, Pi[:, ix(0), :], 0.0, [ar])
            for k in range(1, npow):
                p, q = ix(k - 1), ix(k)
                self.TTo(V, Pr[:, q, :], Pr[:, p, :], a1r, ALU.mult, [ar], [ar])
                self.TTo(V, tt, Pi[:, p, :], a1i, ALU.mult, [ar], [ar])
                self.TTo(V, Pr[:, q, :], Pr[:, q, :], tt, ALU.subtract, [ar], [ar])
                self.TTo(V, Pi[:, q, :], Pr[:, p, :], a1i, ALU.mult, [ar], [ar])
                self.TTo(V, tt, Pi[:, p, :], a1r, ALU.mult, [ar], [ar])
                self.TTo(V, Pi[:, q, :], Pi[:, q, :], tt, ALU.add, [ar], [ar])
            return Pr, Pi, cr, ci

        for j in sorted(set(l // 2 for l in self.layers if l % 2 == 0)):
            o = base_o
            Pr, Pi, cr, ci = chain(d["lamr_sp"][:, j, :], d["lami_sp"][:, j, :], d["lst_sp"][:, j, :], 48, 9, False)
            self.CP(V, self.AR2[:, j, 0, :], Pr[:, 8, :], [ar], [B("AR2")])
            self.CP(V, self.AR2[:, j, 1, :], Pr[:, 8, :], [ar], [B("AR2")])
            self.CP(V, self.AIS[:, j, 1, :], Pi[:, 8, :], [ar], [B("AIS")])
            self.TS(V, self.AIS[:, j, 0, :], Pi[:, 8, :], -1.0, None, ALU.mult, None, [ar], [B("AIS")])
            cre = carve([48, 16]); cim = carve([48, 16]); bre = carve([48, 16]); bim = carve([48, 16])
            for t_, n_ in ((cre, "cre_sp"), (cim, "cim_sp"), (bre, "bre_sp"), (bim, "bim_sp")):
                self.DMA("sync", t_, d[n_][:, j, :, :], [], [ar])
            BBr = carve([48, 16]); BBi = carve([48, 16]); tq = carve([48, 16])
            crb = cr.unsqueeze(2).broadcast_to([128, 48, 16]); cib = ci.unsqueeze(2).broadcast_to([128, 48, 16])
            self.TTo(V, BBr, bre, crb, ALU.mult, [ar], [ar]); self.TTo(V, tq, bim, cib, ALU.mult, [ar], [ar])
            self.TTo(V, BBr, BBr, tq, ALU.subtract, [ar], [ar])
            self.TTo(V, BBi, bim, crb, ALU.mult, [ar], [ar]); self.TTo(V, tq, bre, cib, ALU.mult, [ar], [ar])
            self.TTo(V, BBi, BBi, tq, ALU.add, [ar], [ar])
            BBr32 = carve([48, 32]); BBiN32 = carve([48, 32])
            BBiN = tq
            self.TS(V, BBiN, BBi, -1.0, None, ALU.mult, None, [ar], [ar])
            for g2 in range(2):
                self.TS(V, BBr32[:, :, g2 * 16:(g2 + 1) * 16], BBr, self.MSP[:, g2:g2 + 1], None, ALU.mult, None,
                        [ar, B("MSP")], [ar])
                self.TS(V, BBiN32[:, :, g2 * 16:(g2 + 1) * 16], BBiN, self.MSP[:, g2:g2 + 1], None, ALU.mult, None,
                        [ar, B("MSP")], [ar])
            Er = carve([9, 4, 16]); Ei = carve([9, 4, 16]); te = carve([9, 4, 16])
            Er32 = carve([9, 4, 32]); Ei32 = carve([9, 4, 32])
            WcB = carve([4, 8, 2, 32], BF16)
            KaF = carve([8, 128]); KaB = carve([8, 128], BF16)
            self.MS(V, KaF, 0.0, [ar])
            for jt in range(12):
                g0 = jt * 4
                shp = [128, 9, 4, 16]
                cb = lambda c_: c_[:, g0:g0 + 4, :].unsqueeze(1).broadcast_to(shp)
                pb_ = lambda p_: p_[:, :, g0:g0 + 4].unsqueeze(3).broadcast_to(shp)
                self.TTo(V, Er, cb(cre), pb_(Pr), ALU.mult, [ar], [ar]); self.TTo(V, te, cb(cim), pb_(Pi), ALU.mult, [ar], [ar])
                self.TTo(V, Er, Er, te, ALU.subtract, [ar], [ar])
                self.TTo(V, Ei, cb(cre), pb_(Pi), ALU.mult, [ar], [ar]); self.TTo(V, te, cb(cim), pb_(Pr), ALU.mult, [ar], [ar])
                self.TTo(V, Ei, Ei, te, ALU.add, [ar], [ar])
                for g2 in range(2):
                    self.TS(V, Er32[:, :, :, g2 * 16:(g2 + 1) * 16], Er, self.MSP[:, g2:g2 + 1], None, ALU.mult, None,
                            [ar, B("MSP")], [ar])
                    self.TS(V, Ei32[:, :, :, g2 * 16:(g2 + 1) * 16], Ei, self.MSP[:, g2:g2 + 1], None, ALU.mult, None,
                            [ar, B("MSP")], [ar])
                self.CP(V, WcB[:, :, :, 0, :], Er32[:, 1:9, :, :].rearrange("p k g c -> p g k c"), [ar], [B("WcB")])
                self.TS(V, WcB[:, :, :, 1, :], Ei32[:, 1:9, :, :].rearrange("p k g c -> p g k c"), -1.0, None, ALU.mult, None,
                        [ar], [B("WcB")])
                self.DMA("sync", s["Wc"][j, jt], WcB.rearrange("p g t r c -> p (g t r c)"), [B("WcB")], [B(f"Wc_d{j}_{jt}")])
                pk = 6
                PSK = self.PS[pk][:, 0:256].rearrange("p (t c) -> p t c", t=8)
                for q in range(4):
                    for tau in range(8):
                        self.MM(PSK[32 * q:32 * q + 32, tau, :], BBr32[:, g0 + q, :], Er32[:, tau, q, :], True, False,
                                [ar], [B(f"PS{pk}")], tile_position=(0, 32 * q))
                        self.MM(PSK[32 * q:32 * q + 32, tau, :], BBiN32[:, g0 + q, :], Ei32[:, tau, q, :], False, True,
                                [ar], [B(f"PS{pk}")], tile_position=(0, 32 * q))
                for q in range(4):
                    self.CP(V, KaF[32 * q:32 * q + 32, :, 32 * q:32 * q + 32], PSK[32 * q:32 * q + 32, :, :],
                            [B(f"PS{pk}")], [B("KaF")])
                self.STT(V, KaB[:, 0, :], self.IDENT[:], self.DCP[:, j, jt:jt + 1], KaF[:, 0, :], ALU.mult, ALU.add,
                         [B("IDENT"), B("DCP"), B("KaF")], [B("KaB")])
                self.CP(V, KaB[:, 1:8, :], KaF[:, 1:8, :], [B("KaF")], [B("KaB")])
                self.DMA("sync", s["Ka"][j, jt], KaB.rearrange("p t c -> p (t c)"), [B("KaB")], [B(f"Ka_d{j}_{jt}")])
                assert o <= 72 * 1024, o
            for jt in range(12):
                o = base_o
                n = 64
                Pr, Pi, cr, ci = chain(d["lamr_cp"][:, j, jt, :], d["lami_cp"][:, j, jt, :], d["lst_cp"][:, j, jt, :], n, 8, True)
                bre = carve([n]); bim = carve([n]); BBr = carve([n]); BBi = carve([n]); tq = carve([n])
                self.DMA("sync", bre, d["bre_cp"][:, j, jt, :], [], [ar]); self.DMA("sync", bim, d["bim_cp"][:, j, jt, :], [], [ar])
                self.TTo(V, BBr, bre, cr, ALU.mult, [ar], [ar]); self.TTo(V, tq, bim, ci, ALU.mult, [ar], [ar])
                self.TTo(V, BBr, BBr, tq, ALU.subtract, [ar], [ar])
                self.TTo(V, BBi, bim, cr, ALU.mult, [ar], [ar]); self.TTo(V, tq, bre, ci, ALU.mult, [ar], [ar])
                self.TTo(V, BBi, BBi, tq, ALU.add, [ar], [ar])
                Wr = carve([8, n]); Wi = carve([8, n]); tw = carve([8, n])
                bb = lambda a_: a_.unsqueeze(1).broadcast_to([128, 8, n])
                self.TTo(V, Wr, Pr, bb(BBr), ALU.mult, [ar], [ar]); self.TTo(V, tw, Pi, bb(BBi), ALU.mult, [ar], [ar])
                self.TTo(V, Wr, Wr, tw, ALU.subtract, [ar], [ar])
                self.TTo(V, Wi, Pr, bb(BBi), ALU.mult, [ar], [ar]); self.TTo(V, tw, Pi, bb(BBr), ALU.mult, [ar], [ar])
                self.TTo(V, Wi, Wi, tw, ALU.add, [ar], [ar])
                WbB = carve([8, 2, 128], BF16)
                for ri, W3 in ((0, Wr), (1, Wi)):
                    for g2 in range(2):
                        self.TS(V, WbB[:, :, ri, g2 * 64:(g2 + 1) * 64], W3, self.MCP[:, g2:g2 + 1], None,
                                ALU.mult, None, [ar, B("MCP")], [ar])
                self.DMA("sync", s["Wb"][j, jt], WbB.rearrange("p s r c -> p (s r c)"), [ar], [B(f"Wb_d{j}_{jt}")])
                assert o <= 72 * 1024, o
        fence = [ar, B("WcB"), B("KaB"), B("KaF")]
        for j in (0, 1):
            for jt in range(12):
                for nm in ("Ka_d", "Wb_d", "Wc_d"):
                    if f"{nm}{j}_{jt}" in self.bufs:
                        fence.append(B(f"{nm}{j}_{jt}"))
        fence += [B("MCP"), B("MSP"), B("IDENT"), B("IDENTB")]
        for e_ in ("tensor", "vector", "scalar", "gpsimd", "sync"):
            self.P.op(e_, None, writes=fence)

    def s5_state_part(self, j):
        B, s = self.B, self.sc
        self.CP("gpsimd", self.XS[:, :, :, 0], self.ST5[:, j, :, :], [B("ST5")], [B("XS")])
        for jt in range(12):
            wb, wbuf = self.load_w(s["Wb"][j, jt].rearrange("p (a n) -> p a n", a=1), 1, 2048, [B(f"Wb_d{j}_{jt}")])
            W = wb[:, 0, :].rearrange("p (s r c) -> p s r c", s=8, r=2)
            for ri in range(2):
                for sidx in range(8):
                    for q in range(4):
                        pq = 4 + q
                        PZ = self.PS[pq][:, :].rearrange("p (a r c) -> p a r c", a=4, r=2)
                        self.MM(PZ[:, jt % 4, ri, :], W[32 * q:32 * q + 32, sidx, ri, :],
                                self.U[32 * q:32 * q + 32, jt, sidx * 64:(sidx + 1) * 64], sidx == 0, sidx == 7,
                                [wbuf, B(f"YM{jt}")], [B(f"PS{pq}")], tile_position=(32 * q, 0), skip_group_check=True)
            if jt % 4 == 3:
                jt0 = jt - 3
                for q in range(4):
                    pq = 4 + q
                    PZ = self.PS[pq][:, :].rearrange("p (a r c) -> p a r c", a=4, r=2)
                    self.CP("scalar" if q % 2 else "vector", self.XS[:, :, 4 * jt0 + q:4 * jt0 + q + 13:4, 1:65],
                            PZ.rearrange("p a r c -> p r a c"), [B(f"PS{pq}")], [B("XS")])
        self.marks.setdefault("st_mm", len(self.P.ops))
        E = "gpsimd"
        xs = B("XS")
        for c in range(NCH):
            self.TTo(E, self.TA[:], self.AR2[:, j], self.XS[:, :, :, c], ALU.mult, [B("AR2"), xs], [B("TA")])
            self.TTo(E, self.TB[:, 0, :], self.AIS[:, j, 0, :], self.XS[:, 1, :, c], ALU.mult, [B("AIS"), xs], [B("TB")])
            self.TTo(E, self.TB[:, 1, :], self.AIS[:, j, 1, :], self.XS[:, 0, :, c], ALU.mult, [B("AIS"), xs], [B("TB")])
            self.TTo(E, self.TA[:], self.TA[:], self.TB[:], ALU.add, [B("TA"), B("TB")], [B("TA")])
            self.TTo(E, self.XS[:, :, :, c + 1], self.XS[:, :, :, c + 1], self.TA[:], ALU.add, [xs, B("TA")], [xs])
        self.marks.setdefault("st_scan", len(self.P.ops))
        self.CP("scalar", self.XPb[:], self.XS[:, :, :, 0:64], [xs], [B("XPb")])
        self.CP("gpsimd", self.ST5[:, j, :, :], self.XS[:, :, :, 64], [xs], [B("ST5")])

    def s5_out_part(self, j):
        B, s = self.B, self.sc
        for jt in range(12):
            i = self.wb_rr % self.NWB
            self.wb_rr += 1
            wbuf = B(f"WB{i}")
            KA = self.WB[i][:, 0:1024].rearrange("p (t c) -> p t c", t=8)
            WC = self.WB[i][:, 1024:3072].rearrange("p (g t r c) -> p g t r c", g=4, t=8, r=2)
            self.DMA("sync", self.WB[i][:, 0:1024], s["Ka"][j, jt], [B(f"Ka_d{j}_{jt}")], [wbuf])
            self.DMA("sync", self.WB[i][:, 1024:3072], s["Wc"][j, jt], [B(f"Wc_d{j}_{jt}")], [wbuf])
            p = self.next_ps()
            PY = self.PS[p][:, :].rearrange("p (t c) -> p t c", t=8)
            PYf = self.PS[p][:, :]
            for tau in range(8):
                self.MM(PYf[:, tau * 64:512], KA[:, tau, :], self.U[:, jt, 0:(8 - tau) * 64], tau == 0, False,
                        [wbuf, B(f"YM{jt}")], [B(f"PS{p}")], skip_group_check=True)
            for t in range(8):
                for ri in range(2):
                    for q in range(4):
                        self.MM(PY[32 * q:32 * q + 32, t, :], WC[:, q, t, ri, :], self.XPb[:, ri, 4 * jt + q, :], False,
                                ri == 1, [wbuf, B("XPb")], [B(f"PS{p}")], tile_position=(0, 32 * q), skip_group_check=True)
            self.ACT(self.Y[:, jt, :].rearrange("p (c s) -> p s c", s=8), PY, AF.Gelu_apprx_tanh,
                     [B(f"PS{p}")], [B(f"Y{jt}")])
        self.marks.setdefault("out_mm", len(self.P.ops))
        yb = [B(f"Y{k}") for k in range(12)]
        for ch in range(6):
            wb, wbuf = self.load_w(s["wglu"][j, :, :, ch * 256:(ch + 1) * 256], 12, 256,
                                   [B(f"wglu_b{j}_{k}") for k in range(12)])
            for i in range(2):
                m = ch * 2 + i
                p = self.next_ps()
                for k in range(12):
                    self.MM(self.PS[p][:], wb[:, k, i * 128:(i + 1) * 128], self.Y[:, k, :], k == 0, k == 11,
                            [wbuf, yb[k]], [B(f"PS{p}")])
                self.ACT(self.GTS5, self.PS[p][:], AF.Sigmoid, [B(f"PS{p}"), B("BGLU")], [B("GTS5")],
                         bias=self.BGLU[:, j, m:m + 1])
                self.TTo("vector", self.GTS5, self.GTS5, self.Y[:, m, :], ALU.mult, [B("GTS5"), yb[m]], [B("GTS5")])
                self.TTo("gpsimd", self.YM[:, m, :], self.GTS5, self.SG[:, m, :], ALU.mult, [B("GTS5"), B(f"SG{m}")],
                         [B(f"YM{m}")])

    for k_, v_ in list(locals().items()):
        if callable(v_):
            setattr(Builder, k_, v_)


_s5_methods()


def host_layout_gdn(inp):
    f = lambda a: np.ascontiguousarray(np.asarray(a, dtype=np.float32))
    m = {}
    m["g_alog"] = f(np.asarray(inp["gdn_a_log"]).T)
    m["g_dtb"] = f(np.asarray(inp["gdn_dt_bias"]).T)
    m["g_nw"] = f(np.asarray(inp["gdn_norm_w"]).T)
    cw = np.asarray(inp["gdn_conv_w"])
    m["g_cw"] = f(cw.reshape(2, 4, 24, 128).transpose(3, 0, 2, 1))
    return m


def _gdn_methods():
    def declare_gdn(self):
        d = self.dr
        d["g_alog"] = self.din("g_alog", [12, 2])
        d["g_dtb"] = self.din("g_dtb", [12, 2])
        d["g_nw"] = self.din("g_nw", [128, 2])
        d["g_cw"] = self.din("g_cw", [128, 2, 24, 4])

    def alloc_gdn(self):
        sb = self.sb
        A = self.ARENA
        o = 0

        def carve(nbytes):
            nonlocal o
            v = A[:, o // 2:(o + nbytes) // 2]
            o += nbytes
            return v
        f32v = lambda v, *shape: (v.bitcast(F32) if not shape else v.bitcast(F32))
        self.QKV = carve(24 * TT * 2).rearrange("p (j n) -> p j n", j=24)
        self.XC = [carve(516 * 4).bitcast(F32) for _ in range(2)]
        self.ACC = [carve(TT * 4).bitcast(F32) for _ in range(2)]
        self.Gg = carve(TT * 4).bitcast(F32)
        self.BETA = carve(TT * 4).bitcast(F32)
        self.GCF = carve(TT * 4).bitcast(F32)
        self.E1 = carve(TT * 4).bitcast(F32)
        tm = lambda: carve(48 * 4).bitcast(F32).rearrange("p (a h) -> p a h", a=4)
        self.GT, self.BT, self.GCT, self.GLT, self.SC1, self.SC2 = tm(), tm(), tm(), tm(), tm(), tm()
        m32 = lambda cv: cv(128 * 4).bitcast(F32)
        m16 = lambda cv: cv(128 * 2)

        class Slot:
            pass

        def mk_slot(cv, tag):
            S = Slot()
            S.tag = tag
            S.GRs = cv(TT * 4).bitcast(F32)
            S.EGR = cv(TT * 4).bitcast(F32)
            S.KBT = cv(TT * 2)
            S.QG = cv(TT * 2)
            S.TMP1, S.DEC, S.DL, S.DA = m32(cv), m32(cv), m32(cv), m32(cv)
            S.LTm, S.L0 = m16(cv), m16(cv)
            S.Pb = [m16(cv) for _ in range(2)]
            S.PTb = [m16(cv) for _ in range(2)]
            S.RTb = [m16(cv) for _ in range(2)]
            S.Us = m32(cv)
            S.TMT, S.AT, S.VB, S.KBG, S.KDEC, S.WTs, S.VN, S.ON, S.Sb = (m16(cv) for _ in range(9))
            S.SSQ = cv(4 * 4).bitcast(F32)
            S.JUNK = m16(cv)
            return S
        self.slots = [mk_slot(carve, "a")]
        dead_bf = [v.bitcast(BF16) for v in (self.XC[0], self.XC[1], self.ACC[0], self.ACC[1], self.E1)]
        dpos = [0, 0]

        def carve2(nbytes):
            n = nbytes // 2
            while dpos[0] < len(dead_bf):
                v = dead_bf[dpos[0]]
                if dpos[1] + n <= v.shape[1]:
                    out = v[:, dpos[1]:dpos[1] + n]
                    dpos[1] += n
                    return out
                dpos[0] += 1
                dpos[1] = 0
            return carve(nbytes)
        self.slots.append(mk_slot(carve2, "b"))
        dead_c = [self.QX[:, :, :].rearrange("p a n -> p (a n)"), self.SGX[:, :, :].rearrange("p a n -> p (a n)"),
                  self.EX[:, :, :].rearrange("p a n -> p (a n)"), self.RSTD[:, :].bitcast(BF16), self.SQ[:, :]]
        cpos = [0, 0]

        def carve3(nbytes):
            n = nbytes // 2
            while cpos[0] < len(dead_c):
                v = dead_c[cpos[0]]
                if cpos[1] + n <= v.shape[1]:
                    out = v[:, cpos[1]:cpos[1] + n]
                    cpos[1] += n
                    return out
                cpos[0] += 1
                cpos[1] = 0
            return carve(nbytes)
        self.slots.append(mk_slot(carve3, "c"))
        dead_d = [self.H[:, :, :].rearrange("p a n -> p (a n)")]
        dq = [0, 0]

        def carve4(nbytes):
            n = nbytes // 2
            while dq[0] < len(dead_d):
                v = dead_d[dq[0]]
                if dq[1] + n <= v.shape[1]:
                    out = v[:, dq[1]:dq[1] + n]
                    dq[1] += n
                    return out
                dq[0] += 1
                dq[1] = 0
            return carve(nbytes)
        self.slots.append(mk_slot(carve4, "d"))
        assert o <= 64 * 1024, o
        self.SEL = A[0:12, 32 * 1024:32 * 1024 + 12 * 128 * 2].bitcast(F32).rearrange("p (h m) -> p h m", h=12)
        self.SST = sb("SST", [128, 2, 12, 128], F32)
        self.CONVST = sb("CONVST", [128, 2, 24, 3], F32)
        self.CW = sb("CW", [128, 2, 24, 4], F32)
        self.TRI = sb("TRI", [128, 128], F32)
        self.MSU = sb("MSU", [128, 128], F32)
        self.BLK = sb("BLK", [128, 128], F32)
        self.ONESF = sb("ONESF", [128, 128], F32)
        self.GNW = sb("GNW", [128, 2], F32)
        self.NA = sb("NA", [12, 2], F32)
        self.DTB = sb("DTB", [12, 2], F32)

    def gdn_consts(self):
        B, d = self.B, self.dr
        V = "vector"
        self.MS(V, self.SST[:], 0.0, [B("SST")])
        self.MS(V, self.CONVST[:], 0.0, [B("CONVST")])
        self.MS(V, self.ONESF[:], 1.0, [B("ONESF")])
        self.DMA("sync", self.CW[:], d["g_cw"], [], [B("CW")])
        self.DMA("sync", self.GNW[:], d["g_nw"], [], [B("GNW")])
        self.DMA("sync", self.NA[:], d["g_alog"], [], [B("NA")])
        self.DMA("sync", self.DTB[:], d["g_dtb"], [], [B("DTB")])
        self.ACT(self.NA[:], self.NA[:], AF.Exp, [B("NA")], [B("NA")])
        self.TS(V, self.NA[:], self.NA[:], -1.0, None, ALU.mult, None, [B("NA")], [B("NA")])
        self.P.pool(lambda e: e.iota(self.TRI[:].bitcast(mybir.dt.int32), pattern=[[1, 128]], base=0, channel_multiplier=-1),
                    [], [B("TRI")])
        self.CP(V, self.TRI[:], self.TRI[:].bitcast(mybir.dt.int32), [B("TRI")], [B("TRI")])
        self.TS(V, self.TRI[:], self.TRI[:], 0.0, None, ALU.is_ge, None, [B("TRI")], [B("TRI")])
        self.MS(V, self.TRI[0:64, 64:128], 0.0, [B("TRI")])
        self.TTo(V, self.MSU[:], self.TRI[:], self.IDENT[:], ALU.subtract, [B("TRI"), B("IDENT")], [B("MSU")])
        self.MS(V, self.BLK[:], 0.0, [B("BLK")])
        self.MS(V, self.BLK[0:64, 0:64], 1.0, [B("BLK")])
        self.MS(V, self.BLK[64:128, 64:128], 1.0, [B("BLK")])
        for h in range(12):
            self.TS(V, self.SEL[:, h, :], self.ONESF[0:12, :], self.IDENT[0:12, h:h + 1], None, ALU.mult, None,
                    [B("ONESF"), B("IDENT")], [B("SEL")])

    def gdn_head(self, j, h, S):
        B = self.B
        V, G_ = "vector", "gpsimd"
        gh = B("GH" + S.tag)
        gp_ = B("GP" + S.tag)
        hq = h // 2
        QT, KTt, VT = self.QKV[:, hq, :], self.QKV[:, 6 + hq, :], self.QKV[:, 12 + h, :]
        bq, bk, bv = B(f"QKV{hq}"), B(f"QKV{6 + hq}"), B(f"QKV{12 + h}")
        Sst = self.SST[:, j, h, :]
        bs = B(f"SST{j}_{h}")
        p = self.next_ps(0, 8)
        self.MM(self.PS[p][:], self.SEL[:, h, :], self.GCF[0:12, :], True, True, [B("SEL"), B("GCF")], [B(f"PS{p}")])
        yield
        self.CP(V, S.GRs, self.PS[p][:], [B(f"PS{p}")], [gh])
        yield
        self.ACT(S.EGR, self.PS[p][:], AF.Exp, [B(f"PS{p}")], [gh])
        yield
        p = self.next_ps(0, 8)
        self.MM(self.PS[p][:], self.SEL[:, h, :], self.BETA[0:12, :], True, True, [B("SEL"), B("BETA")], [B(f"PS{p}")])
        yield
        self.TTo(V, S.KBT, KTt, self.PS[p][:], ALU.mult, [bk, B(f"PS{p}")], [gh])
        yield
        self.TTo(G_, S.QG, QT, S.EGR, ALU.mult, [bq, gh], [gh])
        yield
        self.CP("scalar", S.Sb, Sst, [bs], [B("Sb" + S.tag)])
        yield
        for pr in range(4):
            cs = slice(pr * 128, (pr + 1) * 128)
            p = self.next_ps(0, 8)
            self.MM(self.PS[p][:, 0:128], KTt[:, cs], S.KBT[:, cs], True, True, [bk, gh], [B(f"PS{p}")])
            yield
            self.MM(self.PS[p][:, 128:256], KTt[:, cs], QT[:, cs], True, True, [bk, bq], [B(f"PS{p}")], skip_group_check=True)
            yield
            self.TS(V, S.TMP1, S.GRs[:, cs], self.GCT[:, pr, h:h + 1], 0.0, ALU.subtract, ALU.min,
                    [gh, B("GCT")], [gp_])
            yield
            self.ACT(S.DEC, S.TMP1, AF.Exp, [gp_], [gp_])
            yield
            self.TTo(G_, S.DL, S.DEC, self.MSU[:], ALU.mult, [gp_, B("MSU")], [gp_])
            yield
            self.TTo(G_, S.DA, S.DEC, self.TRI[:], ALU.mult, [gp_, B("TRI")], [gp_])
            yield
            self.TTo(V, S.LTm, self.PS[p][:, 0:128], S.DL, ALU.mult, [B(f"PS{p}"), gp_], [gp_])
            yield
            self.TTo(V, S.AT, self.PS[p][:, 128:256], S.DA, ALU.mult, [B(f"PS{p}"), gp_], [gp_])
            yield
            p = self.next_ps(0, 8)
            pl16 = self.PS[p][:, :].bitcast(BF16)
            self.TR(pl16[:, 0:128], S.LTm, self.IDENTB[:], [gp_, B("IDENTB")], [B(f"PS{p}")])
            yield
            self.CP("scalar", S.L0, pl16[:, 0:128], [B(f"PS{p}")], [gp_])
            yield
            self.TTo(V, S.RTb[0], self.IDENTB[:], S.LTm, ALU.subtract, [B("IDENTB"), gp_], [gp_])
            yield
            Pc, PTc, RTc = S.L0, S.LTm, S.RTb[0]
            for k in range(5):
                Pn, PTn, RTn = S.Pb[k % 2], S.PTb[k % 2], S.RTb[(k + 1) % 2]
                p = self.next_ps(0, 8)
                self.MM(self.PS[p][:, 0:128], PTc, Pc, True, True, [gp_], [B(f"PS{p}")])
                yield
                if k < 4:
                    self.MM(self.PS[p][:, 128:256], Pc, PTc, True, True, [gp_], [B(f"PS{p}")], skip_group_check=True)
                    yield
                self.CP("scalar", Pn, self.PS[p][:, 0:128], [B(f"PS{p}")], [gp_])
                yield
                if k < 4:
                    self.CP(V, PTn, self.PS[p][:, 128:256], [B(f"PS{p}")], [gp_])
                    yield
                p2 = self.next_ps(0, 8)
                self.MM(self.PS[p2][:, 0:128], Pn, RTc, True, True, [gp_], [B(f"PS{p2}")])
                yield
                if k < 4:
                    self.TTo(V, RTn, RTc, self.PS[p2][:, 0:128], ALU.add, [gp_, B(f"PS{p2}")], [gp_])
                    yield
                else:
                    self.TTo(V, S.TMT, RTc, self.PS[p2][:, 0:128], ALU.add, [gp_, B(f"PS{p2}")], [gp_])
                    yield
                Pc, PTc, RTc = Pn, PTn, RTn
            p = self.next_ps(0, 8)
            pb16 = self.PS[p][:, :].bitcast(BF16)
            self.TR(pb16[:, 0:128], VT[:, cs], self.IDENTB[:], [bv, B("IDENTB")], [B(f"PS{p}")])
            yield
            self.TR(pb16[:, 128:256], KTt[:, cs], self.IDENTB[:], [bk, B("IDENTB")], [B(f"PS{p}")])
            yield
            self.TS(V, S.VB, pb16[:, 0:128], self.BT[:, pr, h:h + 1], None, ALU.mult, None, [B(f"PS{p}"), B("BT")], [gp_])
            yield
            self.TS(V, S.KBG, pb16[:, 128:256], self.SC1[:, pr, h:h + 1], None, ALU.mult, None,
                    [B(f"PS{p}"), B("SC1")], [gp_])
            yield
            self.TS(V, S.KDEC, pb16[:, 128:256], self.SC2[:, pr, h:h + 1], None, ALU.mult, None,
                    [B(f"PS{p}"), B("SC2")], [gp_])
            yield
            p = self.next_ps(0, 8)
            self.MM(self.PS[p][:, 0:128], S.TMT, S.VB, True, True, [gp_], [B(f"PS{p}")])
            yield
            self.MM(self.PS[p][:, 128:256], S.KBG, S.TMT, True, True, [gp_], [B(f"PS{p}")], skip_group_check=True)
            yield
            self.CP("scalar", S.Us, self.PS[p][:, 0:128], [B(f"PS{p}")], [gp_])
            yield
            self.CP(V, S.WTs, self.PS[p][:, 128:256], [B(f"PS{p}")], [gp_])
            yield
            for c in range(2):
                rs = slice(c * 64, (c + 1) * 64)
                gcol = pr * 128 + c * 64
                pw = self.next_ps(0, 8)
                self.MM(self.PS[pw][rs, 0:128], S.WTs[:, rs], S.Sb, True, True, [gp_, B("Sb" + S.tag)], [B(f"PS{pw}")],
                        tile_position=(0, 64 * c))
                yield
                self.TTo(V, S.VN[rs, :], S.Us[rs, :], self.PS[pw][rs, 0:128], ALU.subtract,
                         [gp_, B(f"PS{pw}")], [B("VN" + S.tag)])
                yield
                po = self.next_ps(0, 8)
                self.MM(self.PS[po][rs, 0:128], S.QG[:, gcol:gcol + 64], S.Sb, True, False, [gh, B("Sb" + S.tag)],
                        [B(f"PS{po}")], tile_position=(0, 64 * c))
                yield
                self.MM(self.PS[po][rs, 0:128], S.AT[rs, rs], S.VN[rs, :], False, True, [gp_, B("VN" + S.tag)],
                        [B(f"PS{po}")], tile_position=(64 * c, 64 * c))
                yield
                pu = self.next_ps(0, 8)
                self.MM(self.PS[pu][:, 0:128], S.KDEC[rs, :], S.VN[rs, :], True, True, [gp_, B("VN" + S.tag)],
                        [B(f"PS{pu}")], tile_position=(64 * c, 0))
                yield
                self.STT(V, Sst, Sst, S.EGR[:, gcol + 63:gcol + 64], self.PS[pu][:, 0:128], ALU.mult, ALU.add,
                         [bs, gh, B(f"PS{pu}")], [bs])
                yield
                self.CP("scalar", S.Sb, Sst, [bs], [B("Sb" + S.tag)])
                yield
                self.ACT(S.JUNK[rs, :], self.PS[po][rs, 0:128], AF.Square, [B(f"PS{po}")], [B("JUNK" + S.tag), B("SSQ" + S.tag)],
                         accum_out=S.SSQ[rs, 0:1])
                yield
                self.ACT(S.SSQ[rs, 1:2], S.SSQ[rs, 0:1], AF.Sqrt, [B("SSQ" + S.tag), B("EPSC")], [B("SSQ" + S.tag)],
                         scale=1.0 / 128, bias=self._eps[rs, 0:1])
                yield
                self.RCP(S.SSQ[rs, 2:3], S.SSQ[rs, 1:2], [B("SSQ" + S.tag)], [B("SSQ" + S.tag)])
                yield
                self.TS(V, S.ON[rs, :], self.PS[po][rs, 0:128], S.SSQ[rs, 2:3], None, ALU.mult, None,
                        [B(f"PS{po}"), B("SSQ" + S.tag)], [B("ON" + S.tag)])
                yield
            p = self.next_ps(0, 8)
            pb16 = self.PS[p][:, :].bitcast(BF16)
            self.TR(pb16[:, 0:128], S.ON, self.IDENTB[:], [B("ON" + S.tag), B("IDENTB")], [B(f"PS{p}")])
            yield
            self.STT(V, self.YM[:, h, cs], pb16[:, 0:128], self.GNW[:, j:j + 1], self.SG[:, h, cs], ALU.mult, ALU.mult,
                     [B(f"PS{p}"), B("GNW"), B(f"SG{h}")], [B(f"YM{h}")])
            yield


    def gdn_layer(self, l):
        B, s = self.B, self.sc
        j = l // 2
        V, G_ = "vector", "gpsimd"
        self.norm_in(l)
        wsrc = s["win_gdn"][j]
        wbufs = [B(f"win_gdn_b{j}_{k}") for k in range(KT)]
        ga = B("GA")

        def ev_a(i, ps, pbuf, m):
            self.ACT(self.E1[0:12, :], ps[0:12, :], AF.Exp, [pbuf, B("DTB")], [ga], bias=self.DTB[:, j:j + 1])
            self.ACT(self.E1[0:12, :], self.E1[0:12, :], AF.Ln, [ga, B("ONESF")], [ga], bias=self.ONESF[0:12, 0:1])
            self.TS(V, self.Gg[0:12, :], self.E1[0:12, :], self.NA[:, j:j + 1], None, ALU.mult, None, [ga, B("NA")], [B("Gg")])

        def ev_b(i, ps, pbuf, m):
            self.ACT(self.BETA[0:12, :], ps[0:12, :], AF.Sigmoid, [pbuf], [B("BETA")])
        self.proj_cols(wsrc, wbufs, 3072, 12, ev_a)
        self.proj_cols(wsrc, wbufs, 3084, 12, ev_b)
        p = self.next_ps()
        for pr in range(4):
            self.TR(self.PS[p][:, pr * 12:(pr + 1) * 12], self.Gg[0:12, pr * 128:(pr + 1) * 128], self.IDENT[0:12, 0:12],
                    [B("Gg"), B("IDENT")], [B(f"PS{p}")])
            self.TR(self.PS[p][:, 48 + pr * 12:48 + (pr + 1) * 12], self.BETA[0:12, pr * 128:(pr + 1) * 128],
                    self.IDENT[0:12, 0:12], [B("BETA"), B("IDENT")], [B(f"PS{p}")])
        self.CP(V, self.GT.rearrange("p a h -> p (a h)"), self.PS[p][:, 0:48], [B(f"PS{p}")], [B("GT")])
        self.CP(V, self.BT.rearrange("p a h -> p (a h)"), self.PS[p][:, 48:96], [B(f"PS{p}")], [B("BT")])
        p = self.next_ps()
        for pr in range(4):
            self.MM(self.PS[p][:, pr * 12:(pr + 1) * 12], self.TRI[:], self.GT[:, pr, :], True, True,
                    [B("TRI"), B("GT")], [B(f"PS{p}")])
            self.MM(self.PS[p][:, 48 + pr * 12:48 + (pr + 1) * 12], self.BLK[:], self.GT[:, pr, :], True, True,
                    [B("BLK"), B("GT")], [B(f"PS{p}")], skip_group_check=True)
        self.CP(V, self.GCT.rearrange("p a h -> p (a h)"), self.PS[p][:, 0:48], [B(f"PS{p}")], [B("GCT")])
        self.CP(V, self.GLT.rearrange("p a h -> p (a h)"), self.PS[p][:, 48:96], [B(f"PS{p}")], [B("GLT")])
        p = self.next_ps()
        for pr in range(4):
            self.MM(self.PS[p][0:12, pr * 128:(pr + 1) * 128], self.GT[:, pr, :], self.TRI[:], True, True,
                    [B("TRI"), B("GT")], [B(f"PS{p}")], skip_group_check=True)
        self.CP(V, self.GCF[0:12, :], self.PS[p][0:12, :], [B(f"PS{p}")], [B("GCF")])
        fl = lambda t_: t_.rearrange("p a h -> p (a h)")
        self.ACT(fl(self.SC1), fl(self.GCT), AF.Exp, [B("GCT")], [B("SC1")])
        self.TTo(V, fl(self.SC1), fl(self.SC1), fl(self.BT), ALU.mult, [B("SC1"), B("BT")], [B("SC1")])
        self.TTo(V, fl(self.SC2), fl(self.GLT), fl(self.GCT), ALU.subtract, [B("GLT"), B("GCT")], [B("SC2")])
        self.ACT(fl(self.SC2), fl(self.SC2), AF.Exp, [B("SC2")], [B("SC2")])

        def ev_qkv(c0):
            def f(i, ps, pbuf, m):
                ti = c0 // 128 + i
                xc, acc = self.XC[ti % 2], self.ACC[ti % 2]
                xb, ab = B(f"XC{ti % 2}"), B(f"ACC{ti % 2}")
                self.CP(G_, xc[:, 0:3], self.CONVST[:, j, ti, :], [B("CONVST")], [xb])
                self.CP("scalar", xc[:, 3:515], ps[:, :], [pbuf], [xb])
                self.CP(G_, self.CONVST[:, j, ti, :], xc[:, 512:515], [xb], [B("CONVST")])
                self.TS(V, acc, xc[:, 0:512], self.CW[:, j, ti, 0:1], None, ALU.mult, None, [xb, B("CW")], [ab])
                for k in range(1, 4):
                    self.STT(V, acc, xc[:, k:k + 512], self.CW[:, j, ti, k:k + 1], acc, ALU.mult, ALU.add,
                             [xb, B("CW"), ab], [ab])
                self.ACT(self.QKV[:, ti, :], acc, AF.Silu, [ab], [B(f"QKV{ti}")])
                if ti < 12:
                    self.ACT(self.SQ[:], self.QKV[:, ti, :], AF.Square, [B(f"QKV{ti}")], [B("SQ")])
                    pp = self.next_ps()
                    self.MM(self.PS[pp][:], self.ONES[:], self.SQ[:], True, True, [B("ONES"), B("SQ")], [B(f"PS{pp}")])
                    self.ACT(self.RSTD[:], self.PS[pp][:], AF.Sqrt, [B(f"PS{pp}"), B("EPSC")], [B("RSTD")], bias=self.eps_ap())
                    self.RCP(self.RSTD[:], self.RSTD[:], [B("RSTD")], [B("RSTD")])
                    if ti < 6:
                        self.STT(V, self.QKV[:, ti, :], self.QKV[:, ti, :], ISQ, self.RSTD[:], ALU.mult, ALU.mult,
                                 [B(f"QKV{ti}"), B("RSTD")], [B(f"QKV{ti}")])
                    else:
                        self.TTo(V, self.QKV[:, ti, :], self.QKV[:, ti, :], self.RSTD[:], ALU.mult,
                                 [B(f"QKV{ti}"), B("RSTD")], [B(f"QKV{ti}")])
            return f
        for c in range(6):
            self.proj_cols(wsrc, wbufs, c * 512, 512, ev_qkv(c * 512))

        def ev_gate(c):
            def f(i, ps, pbuf, m):
                self.ACT(self.SG[:, 4 * c + i, :], ps[:, :], AF.Silu, [pbuf], [B(f"SG{4 * c + i}")])
            return f
        for c in range(3):
            self.proj_cols(wsrc, wbufs, 3096 + c * 512, 512, ev_gate(c))
        self.proj_cols(wsrc, wbufs, 4632, 512,
                       lambda i, ps, pbuf, m: self.CP(V, self.QX[:, i, :], ps[:, :], [pbuf], [B(f"QX{i}")]))
        self.proj_cols(wsrc, wbufs, 5144, 512,
                       lambda i, ps, pbuf, m: self.ACT(self.SGX[:, i, :], ps[:, :], AF.Silu, [pbuf], [B(f"SGX{i}")]))
        self.xattn(l)
        side = []

        self.P.barrier()
        steps = 0
        for hp in range(3):
            gens = [self.gdn_head(j, 4 * hp + i_, self.slots[i_]) for i_ in range(4)]
            done = [False] * 4
            while not all(done):
                for gi, g in enumerate(gens):
                    if not done[gi]:
                        try:
                            next(g)
                        except StopIteration:
                            done[gi] = True
                steps += 1
                if side and steps % 40 == 20:
                    side.pop(0)()
        while side:
            side.pop(0)()
        self.out_proj(l)

    for k_, v_ in list(locals().items()):
        if callable(v_):
            setattr(Builder, k_, v_)


_gdn_methods()


SEQ_FULL = 8192
N_CORES = 8


def kernel(**inputs):
    L = SEQ_FULL
    bld = Builder(L, layers=(0, 1, 2, 3), do_final=True, mix=True)
    nc = bld.build()
    shared = host_layout_mix(inputs)
    base = host_layout(inputs, 0, 0, L)
    in_maps = []
    for c in range(N_CORES):
        b = c % 4
        m = dict(base)
        m.update(shared)
        m["xT"] = np.ascontiguousarray(np.asarray(inputs["x"], dtype=np.float32)[b].T)
        m["memT"] = np.ascontiguousarray(np.asarray(inputs["mem"], dtype=np.float32)[b].T)
        in_maps.append(m)
    res = run_bass_kernel_spmd(nc, in_maps, core_ids=list(range(N_CORES)))
    out = np.stack([np.ascontiguousarray(res.results[b]["outT"].T) for b in range(4)], axis=0)
    return out.astype(np.float32)
```

```python
import math
from contextlib import ExitStack

import numpy as np
import concourse.bass as bass
import concourse.mybir as mybir
from concourse.bass_utils import run_bass_kernel_spmd

F32 = mybir.dt.float32
BF16 = mybir.dt.bfloat16
AF = mybir.ActivationFunctionType
ALU = mybir.AluOpType
AX = mybir.AxisListType

EPOCH = 20000
DEBUG_NAMES = None
SAME_ENGINE_SYNC = True
N_DMA_SEMS = 32
N_SW_SEMS = 8
ENGINES = ("tensor", "vector", "scalar", "gpsimd", "sync")


class Buf:
    __slots__ = ("name", "w", "rs")

    def __init__(self, name):
        self.name = name
        self.w = None
        self.rs = []


class Op:
    __slots__ = ("eng", "fn", "reads", "writes", "dma", "waits", "signal", "tok", "idx", "dsem", "src")

    def __init__(self, eng, fn, reads, writes, dma):
        self.eng = eng
        self.fn = fn
        self.reads = reads
        self.writes = writes
        self.dma = dma
        self.waits = []
        self.signal = False
        self.tok = None
        self.dsem = None


class Prog:
    def __init__(self, nc):
        self.nc = nc
        self.ops = []

    def op(self, eng, fn, reads=(), writes=(), dma=False):
        o = Op(eng, fn, tuple(reads), tuple(writes), dma)
        o.idx = len(self.ops)
        import sys as _s
        fr = _s._getframe(1)
        o.src = []
        while fr is not None and len(o.src) < 4:
            o.src.append(fr.f_lineno)
            fr = fr.f_back
        self.ops.append(o)
        return o

    def mm(self, fn, reads, writes):
        return self.op("tensor", fn, reads, writes)

    def dve(self, fn, reads, writes):
        return self.op("vector", fn, reads, writes)

    def act(self, fn, reads, writes):
        return self.op("scalar", fn, reads, writes)

    def pool(self, fn, reads, writes):
        return self.op("gpsimd", fn, reads, writes)

    def dma(self, eng, out, in_, reads, writes):
        return self.op(eng, lambda e: e.dma_start(out=out, in_=in_), reads, writes, dma=True)

    def barrier(self):
        for e in ENGINES:
            o = self.op(e, None)
            o.dsem = "barrier"

    def analyse(self):
        ops = self.ops
        last_real = {}
        last_dmas = []
        seen = {e: {} for e in ENGINES}
        dma_rr = 0
        sw_rr = 0
        dma_last = [None] * N_DMA_SEMS
        for o in ops:
            deps = set()
            for b in o.reads:
                if b.w is not None:
                    deps.add(b.w)
            for b in o.writes:
                if b.w is not None:
                    deps.add(b.w)
                for r in b.rs:
                    deps.add(r)
            if o.dsem == "barrier":
                for e2, i2 in last_real.items():
                    if e2 != o.eng:
                        deps.add(i2)
                for i2 in last_dmas:
                    deps.add(i2)
            if o.dma:
                if o.eng == "gpsimd":
                    k = N_DMA_SEMS - N_SW_SEMS + sw_rr % N_SW_SEMS
                    sw_rr += 1
                else:
                    k = dma_rr % (N_DMA_SEMS - N_SW_SEMS)
                    dma_rr += 1
                o.dsem = k
                if dma_last[k] is not None:
                    deps.add(dma_last[k])
                dma_last[k] = o.idx
            sn = seen[o.eng]
            best = {}
            dl = []
            for d in deps:
                p = ops[d]
                if p.dma:
                    dl.append(d)
                elif best.get(p.eng, -1) < d:
                    best[p.eng] = d
            for d in sorted(dl + list(best.values())):
                p = ops[d]
                if p.dma:
                    key = ("dma", p.idx)
                    if key in sn:
                        continue
                    sn[key] = True
                    o.waits.append(d)
                else:
                    if p.eng == o.eng and (p.eng == "tensor" or not SAME_ENGINE_SYNC):
                        continue
                    if sn.get(p.eng, -1) >= d:
                        continue
                    sn[p.eng] = d
                    p.signal = True
                    o.waits.append(d)
            if o.fn is None:
                continue
            if o.dma:
                last_dmas.append(o.idx)
                if len(last_dmas) > N_DMA_SEMS:
                    last_dmas.pop(0)
            else:
                last_real[o.eng] = o.idx
            for b in o.reads:
                b.rs.append(o.idx)
            for b in o.writes:
                b.w = o.idx
                b.rs = []
        cnt = {e: 0 for e in ENGINES}
        dcnt = [0] * N_DMA_SEMS
        self.n_epochs = {e: 1 for e in ENGINES}
        for o in ops:
            if o.dma:
                dcnt[o.dsem] += 16
                o.tok = ("d", o.dsem, dcnt[o.dsem])
            elif o.signal:
                c = cnt[o.eng]
                cnt[o.eng] += 1
                ep = c // EPOCH
                self.n_epochs[o.eng] = max(self.n_epochs[o.eng], ep + 1)
                o.tok = ("e", o.eng, ep, c % EPOCH + 1)

    def emit(self):
        nc = self.nc
        self.analyse()
        with ExitStack() as es:
            esem = {}
            for e in ENGINES:
                esem[e] = [es.enter_context(nc.semaphore(f"s_{e}_{i}")) for i in range(self.n_epochs[e])]
            dsem = [es.enter_context(nc.semaphore(f"s_dma_{i}")) for i in range(N_DMA_SEMS)]
            block = es.enter_context(nc.Block())
            by_eng = {e: [o for o in self.ops if o.eng == e] for e in ENGINES}
            ops = self.ops

            def run(eng_name):
                def body(eng):
                    for o in by_eng[eng_name]:
                        for d in o.waits:
                            t = ops[d].tok
                            if t[0] == "d":
                                eng.wait_ge(dsem[t[1]], t[2])
                            else:
                                eng.wait_ge(esem[t[1]][t[2]], t[3])
                        if o.fn is None:
                            continue
                        try:
                            inst = o.fn(eng)
                        except Exception:
                            print("FAILED OP at lines", o.src, "engine", o.eng)
                            raise
                        if DEBUG_NAMES is not None:
                            try:
                                DEBUG_NAMES.append((str(getattr(inst, "name", None) or getattr(getattr(inst, "ins", None), "name", None)), o.src, o.eng))
                            except Exception:
                                pass
                        if o.dma:
                            inst.then_inc(dsem[o.dsem], 16)
                        elif o.signal:
                            inst.then_inc(esem[o.eng][o.tok[2]], 1)
                return body

            block.tensor(run("tensor"))
            block.vector(run("vector"))
            block.scalar(run("scalar"))
            block.gpsimd(run("gpsimd"))
            block.sync(run("sync"))


D = 1024
KT = 8
MEM = 256
TW = 1536
XW = 512
S5_IN = 4096
GDN_IN = 5656
TT = 512
NCH = 64
EPS = 1e-6
ISQ = 1.0 / math.sqrt(128.0)


class Builder:
    def __init__(self, L, layers=(0, 1, 2, 3), do_final=True, mix=True):
        self.L = L
        self.n_tiles = L // TT
        self.layers = tuple(layers)
        self.do_final = do_final
        self.mix = mix
        self.truncate = None
        self.nc = bass.Bass("TRN2", target_bir_lowering=False)
        self.P = Prog(self.nc)
        self.es = ExitStack()
        self.bufs = {}

    def din(self, name, shape):
        return self.nc.dram_tensor(name, list(shape), F32, kind="ExternalInput").ap()

    def dscr(self, name, shape, dt=BF16):
        return self.nc.dram_tensor(name, list(shape), dt, kind="Internal").ap()

    def sb(self, name, shape, dt):
        return self.es.enter_context(self.nc.sbuf_tensor(name, list(shape), dt))

    def ps(self, name, shape, dt=F32):
        return self.es.enter_context(self.nc.psum_tensor(name, list(shape), dt))

    def B(self, name):
        b = self.bufs.get(name)
        if b is None:
            b = self.bufs[name] = Buf(name)
        return b

    def MM(self, out, lhsT, rhs, start, stop, reads, writes, **kw):
        self.P.mm(lambda e: e.matmul(out, lhsT, rhs, start=start, stop=stop, **kw), reads, writes)

    def TR(self, out, in_, ident, reads, writes):
        self.P.mm(lambda e: e.transpose(out, in_, ident), reads, writes)

    def ACT(self, out, in_, func, reads, writes, **kw):
        self.P.act(lambda e: e.activation(out=out, in_=in_, func=func, **kw), reads, writes)

    def TTo(self, eng, out, in0, in1, op, reads, writes):
        self.P.op(eng, lambda e: e.tensor_tensor(out=out, in0=in0, in1=in1, op=op), reads, writes)

    def TS(self, eng, out, in0, s1, s2, op0, op1, reads, writes):
        if op1 is None:
            self.P.op(eng, lambda e: e.tensor_scalar(out=out, in0=in0, scalar1=s1, scalar2=None, op0=op0), reads, writes)
        else:
            self.P.op(eng, lambda e: e.tensor_scalar(out=out, in0=in0, scalar1=s1, scalar2=s2, op0=op0, op1=op1), reads, writes)

    def STT(self, eng, out, in0, scalar, in1, op0, op1, reads, writes):
        self.P.op(eng, lambda e: e.scalar_tensor_tensor(out=out, in0=in0, scalar=scalar, in1=in1, op0=op0, op1=op1),
                  reads, writes)

    def CP(self, eng, out, in_, reads, writes):
        if eng == "scalar":
            self.P.op(eng, lambda e: e.activation(out=out, in_=in_, func=AF.Copy), reads, writes)
        else:
            self.P.op(eng, lambda e: e.tensor_copy(out=out, in_=in_), reads, writes)

    def MS(self, eng, ap, val, writes):
        self.P.op(eng, lambda e: e.memset(ap, val), [], writes)

    def RCP(self, out, in_, reads, writes):
        self.P.dve(lambda e: e.reciprocal(out=out, in_=in_), reads, writes)

    def DMA(self, eng, out, in_, reads, writes):
        self.P.dma(eng, out, in_, reads, writes)

    def declare(self):
        L = self.L
        d = self.dr = {}
        d["xT"] = self.din("xT", [D, L])
        d["memT"] = self.din("memT", [D, MEM])
        d["nw"] = self.din("nw", [128, 4, KT])
        d["mnw"] = self.din("mnw", [128, 4, KT])
        d["fnw"] = self.din("fnw", [128, KT])
        d["win_s5"] = self.din("win_s5", [2, 128, KT, S5_IN])
        d["win_gdn"] = self.din("win_gdn", [2, 128, KT, GDN_IN])
        d["wout"] = self.din("wout", [4, 128, 16, D])
        d["wkv"] = self.din("wkv", [4, 128, KT, D])
        d["wglu"] = self.din("wglu", [2, 128, 12, TW])
        d["bglu"] = self.din("bglu", [128, 2, 12])
        self.outT = self.nc.dram_tensor("outT", [D, L], F32, kind="ExternalOutput").ap()
        s = self.sc = {}
        s["win_s5"] = self.dscr("win_s5_b", [2, 128, KT, S5_IN])
        s["win_gdn"] = self.dscr("win_gdn_b", [2, 128, KT, GDN_IN])
        s["wout"] = self.dscr("wout_b", [4, 128, 16, D])
        s["wkv"] = self.dscr("wkv_b", [4, 128, KT, D])
        s["wglu"] = self.dscr("wglu_b", [2, 128, 12, TW])

    def alloc_common(self):
        sb, ps = self.sb, self.ps
        self.X = sb("X", [128, KT, TT], F32)
        self.H = sb("H", [128, KT, TT], BF16)
        self.SQ = sb("SQ", [128, TT], BF16)
        self.RSTD = sb("RSTD", [128, TT], F32)
        self.YM = sb("YM", [128, 12, TT], BF16)
        self.YX = sb("YX", [128, 4, TT], BF16)
        self.QX = sb("QX", [128, 4, TT], BF16)
        self.SGX = sb("SGX", [128, 4, TT], BF16)
        self.SG = sb("SG", [128, 12, TT], BF16)
        self.EX = sb("EX", [128, 2, TT], BF16)
        if not self.mix:
            self.RDEN = sb("RDEN", [128, TT], F32)
        self.OTMP = self.RSTD
        self.KTs = sb("KTs", [128, 4, 4, MEM], BF16)
        self.Vs = sb("Vs", [128, 4, 2, XW], BF16)
        self.ONES = sb("ONES", [128, 128], BF16)
        self.NW = sb("NW", [128, 4, KT], F32)
        self.MNW = sb("MNW", [128, 4, KT], F32)
        self.FNW = sb("FNW", [128, KT], F32)
        self.BGLU = sb("BGLU", [128, 2, 12], F32)
        self.NWB = 4
        self.WB = [sb(f"WB{i}", [128, 8 * 512], BF16) for i in range(self.NWB)]
        self.wb_rr = 0
        self.PS = [ps(f"PS{i}", [128, 512]) for i in range(8)]
        self.ps_rr = 0

    def next_ps(self, lo=0, hi=4):
        i = lo + self.ps_rr % (hi - lo)
        self.ps_rr += 1
        return i

    def load_w(self, src, K, ncols, rbufs):
        i = self.wb_rr % self.NWB
        self.wb_rr += 1
        wb = self.WB[i][:, 0:K * ncols].rearrange("p (k n) -> p k n", k=K)
        eng = "sync"
        self.DMA(eng, wb, src, rbufs, [self.B(f"WB{i}")])
        return wb, self.B(f"WB{i}")

    def prologue(self):
        d, s, B = self.dr, self.sc, self.B
        for name, n0, K in (("win_s5", 2, KT), ("win_gdn", 2, KT), ("wout", 4, 16), ("wkv", 4, KT), ("wglu", 2, 12)):
            for j in range(n0):
                for k in range(K):
                    self.DMA("gpsimd", s[name][j, :, k, :], d[name][j, :, k, :], [], [B(f"{name}_b{j}_{k}")])
        self.DMA("sync", self.NW[:], d["nw"], [], [B("NW")])
        self.DMA("sync", self.MNW[:], d["mnw"], [], [B("MNW")])
        self.DMA("sync", self.FNW[:], d["fnw"], [], [B("FNW")])
        self.DMA("sync", self.BGLU[:], d["bglu"], [], [B("BGLU")])
        self.MS("vector", self.ONES[:], 1.0, [B("ONES")])
        if not self.mix:
            self.MS("vector", self.YM[:], 0.0, [B(f"YM{j}") for j in range(12)])
        memT = self.X[:, :, 0:MEM]
        for k in range(KT):
            self.DMA("sync", self.X[:, k, 0:MEM], d["memT"][k * 128:(k + 1) * 128, :], [], [B(f"X{k}")])
        pb = 4
        for k in range(KT):
            self.ACT(self.SQ[:, 0:MEM], self.X[:, k, 0:MEM], AF.Square, [B(f"X{k}")], [B("SQ")])
            self.MM(self.PS[pb][:, 0:MEM], self.ONES[:], self.SQ[:, 0:MEM], k == 0, k == KT - 1,
                    [B("ONES"), B("SQ")], [B(f"PS{pb}")])
        self.rstd_from(self.RSTD[:, 0:MEM], self.PS[pb][:, 0:MEM], 1.0 / D, [B(f"PS{pb}")], [B("RSTD")])
        for l in self.layers:
            for k in range(KT):
                self.STT("vector", self.H[:, k, 0:MEM], self.X[:, k, 0:MEM], self.MNW[:, l, k:k + 1],
                         self.RSTD[:, 0:MEM], ALU.mult, ALU.mult,
                         [B(f"X{k}"), B("MNW"), B("RSTD")], [B(f"H{k}")])
            for half in range(2):
                wb, wbuf = self.load_w(s["wkv"][l, :, :, half * 512:(half + 1) * 512], KT, 512,
                                       [B(f"wkv_b{l}_{k}") for k in range(KT)])
                if half == 0:
                    for h in range(4):
                        p = self.next_ps()
                        for k in range(KT):
                            self.MM(self.PS[p][:, 0:MEM], wb[:, k, h * 128:(h + 1) * 128], self.H[:, k, 0:MEM],
                                    k == 0, k == KT - 1, [wbuf, B(f"H{k}")], [B(f"PS{p}")])
                        self.CP("scalar", self.KTs[:, l, h, :], self.PS[p][:, 0:MEM], [B(f"PS{p}")], [B("KTs")])
                else:
                    for mt in range(2):
                        p = self.next_ps()
                        for k in range(KT):
                            self.MM(self.PS[p][:, :], self.H[:, k, mt * 128:(mt + 1) * 128], wb[:, k, :],
                                    k == 0, k == KT - 1, [wbuf, B(f"H{k}")], [B(f"PS{p}")])
                        self.CP("vector", self.Vs[:, l, mt, :], self.PS[p][:, :], [B(f"PS{p}")], [B("Vs")])

    def rstd_from(self, out, in_, scale, reads, writes, eps=None):
        B = self.B
        self.ACT(out, in_, AF.Ln, list(reads) + [B("EPSC")], writes, scale=scale, bias=(self.eps_ap() if eps is None else eps))
        self.ACT(out, out, AF.Exp, writes, writes, scale=-0.5)

    def eps_ap(self):
        if not hasattr(self, "_eps"):
            self._eps = self.sb("EPSC", [128, 1], F32)
            self.MS("vector", self._eps[:], EPS, [self.B("EPSC")])
        return self._eps[:, 0:1]

    def rms_stats(self):
        B = self.B
        pb = 4
        for k in range(KT):
            self.ACT(self.SQ[:], self.X[:, k, :], AF.Square, [B(f"X{k}")], [B("SQ")])
            self.MM(self.PS[pb][:], self.ONES[:], self.SQ[:], k == 0, k == KT - 1, [B("ONES"), B("SQ")], [B(f"PS{pb}")])
        self.rstd_from(self.RSTD[:], self.PS[pb][:], 1.0 / D, [B(f"PS{pb}")], [B("RSTD")])

    def norm_in(self, l):
        B = self.B
        self.rms_stats()
        for k in range(KT):
            self.STT("vector", self.H[:, k, :], self.X[:, k, :], self.NW[:, l, k:k + 1], self.RSTD[:],
                     ALU.mult, ALU.mult, [B(f"X{k}"), B("NW"), B("RSTD")], [B(f"H{k}")])

    def proj_cols(self, wsrc, wbufs, col0, ncols, evac):
        B = self.B
        wb, wbuf = self.load_w(wsrc[:, :, col0:col0 + ncols], KT, ncols, wbufs)
        nt = (ncols + 127) // 128
        for i in range(nt):
            m = min(128, ncols - i * 128)
            p = self.next_ps()
            for k in range(KT):
                self.MM(self.PS[p][0:m, :], wb[:, k, i * 128:i * 128 + m], self.H[:, k, :], k == 0, k == KT - 1,
                        [wbuf, B(f"H{k}")], [B(f"PS{p}")])
            evac(i, self.PS[p], B(f"PS{p}"), m)

    def xattn(self, l):
        B = self.B
        for h in range(4):
            sp = [5, 6]
            for mt in range(2):
                self.MM(self.PS[sp[mt]][:], self.KTs[:, l, h, mt * 128:(mt + 1) * 128], self.QX[:, h, :], True, True,
                        [B("KTs"), B(f"QX{h}")], [B(f"PS{sp[mt]}")])
                self.ACT(self.EX[:, mt, :], self.PS[sp[mt]][:], AF.Exp, [B(f"PS{sp[mt]}")], [B(f"EX{mt}")], scale=ISQ)
            for mt in range(2):
                self.MM(self.PS[7][:], self.ONES[:], self.EX[:, mt, :], mt == 0, mt == 1,
                        [B("ONES"), B(f"EX{mt}")], [B("PS7")])
            p = self.next_ps()
            for mt in range(2):
                self.MM(self.PS[p][:], self.Vs[:, l, mt, h * 128:(h + 1) * 128], self.EX[:, mt, :], mt == 0, mt == 1,
                        [B("Vs"), B(f"EX{mt}")], [B(f"PS{p}")])
            self.RCP(self.RDEN[:], self.PS[7][:], [B("PS7")], [B("RDEN")])
            self.TTo("vector", self.OTMP[:], self.PS[p][:], self.RDEN[:], ALU.mult, [B(f"PS{p}"), B("RDEN")], [B("RSTD")])
            self.TTo("gpsimd", self.YX[:, h, :], self.OTMP[:], self.SGX[:, h, :], ALU.mult,
                     [B("RSTD"), B(f"SGX{h}")], [B(f"YX{h}")])

    def out_proj(self, l):
        B, s = self.B, self.sc
        rb = [B(f"YM{j}") for j in range(12)] + [B(f"YX{h}") for h in range(4)]
        for half in range(4):
            wb, wbuf = self.load_w(s["wout"][l, :, :, half * 256:(half + 1) * 256], 16, 256,
                                   [B(f"wout_b{l}_{k}") for k in range(16)])
            for i in range(2):
                m = half * 2 + i
                p = self.next_ps()
                for k in range(16):
                    rhs = self.YM[:, k, :] if k < 12 else self.YX[:, k - 12, :]
                    self.MM(self.PS[p][:], wb[:, k, i * 128:(i + 1) * 128], rhs, k == 0, k == 15,
                            [wbuf, rb[k]], [B(f"PS{p}")])
                self.TTo("vector", self.X[:, m, :], self.X[:, m, :], self.PS[p][:], ALU.add,
                         [B(f"X{m}"), B(f"PS{p}")], [B(f"X{m}")])

    def final_norm_store(self, t):
        B = self.B
        self.rms_stats()
        OUT = self.H_as_f32()
        for k in range(KT):
            ob = [B(f"H{2 * (k % 2) + q_}") for q_ in range(2)]
            self.STT("vector", OUT[k % 2], self.X[:, k, :], self.FNW[:, k:k + 1], self.RSTD[:], ALU.mult, ALU.mult,
                     [B(f"X{k}"), B("FNW"), B("RSTD")], ob)
            self.DMA("sync", self.outT[k * 128:(k + 1) * 128, t * TT:(t + 1) * TT], OUT[k % 2], ob,
                     [B(f"out_{t}_{k}")])
            self.out_bufs.append(B(f"out_{t}_{k}"))

    def H_as_f32(self):
        if not hasattr(self, "_outf"):
            self._outf = self.H[:, 0:4, :].rearrange("p k n -> p (k n)").bitcast(F32).rearrange("p (a n) -> p a n", a=2)
        return [self._outf[:, 0, :], self._outf[:, 1, :]]

    def alloc_s5(self):
        sb = self.sb
        self.U = self.YM

    def s5_layer(self, l):
        B, s = self.B, self.sc
        j = l // 2
        self.norm_in(l)
        wbufs = [B(f"win_s5_b{j}_{k}") for k in range(KT)]

        def evac(c0):
            def f(i, ps, pbuf, m):
                gi = c0 // 128 + i
                if gi < 12:
                    self.CP("vector" if gi % 2 else "scalar",
                            self.U[:, gi, :].rearrange("p (s c) -> p s c", s=8),
                            ps[:, :].rearrange("p (c s) -> p s c", s=8), [pbuf], [B(f"YM{gi}")])
                elif gi < 24:
                    self.ACT(self.SG[:, gi - 12, :], ps[:, :], AF.Silu, [pbuf], [B(f"SG{gi - 12}")])
                elif gi < 28:
                    self.CP("vector", self.QX[:, gi - 24, :], ps[:, :], [pbuf], [B(f"QX{gi - 24}")])
                else:
                    self.ACT(self.SGX[:, gi - 28, :], ps[:, :], AF.Silu, [pbuf], [B(f"SGX{gi - 28}")])
            return f

        order = [0, 512, 1024, 3072, 3584, 1536, 2048, 2560]
        for c0 in order[:3]:
            self.proj_cols(s["win_s5"][j], wbufs, c0, 512, evac(c0))
        if self.mix:
            self.s5_state_part(j)
        for c0 in order[3:]:
            self.proj_cols(s["win_s5"][j], wbufs, c0, 512, evac(c0))
        self.xattn(l)
        if self.mix:
            self.s5_out_part(j)
        self.out_proj(l)

    def build(self):
        B = self.B
        self.declare()
        self.alloc_common()
        self.alloc_s5()
        if self.mix:
            self.alloc_s5_mix()
            self.alloc_gdn()
        self.out_bufs = []
        self.marks = {}
        self.prologue()
        self.marks["prologue"] = len(self.P.ops)
        if self.mix:
            self.s5_precompute()
            self.gdn_consts()
        self.marks["precompute"] = len(self.P.ops)
        for t in range(self.n_tiles):
            for k in range(KT):
                self.DMA("sync", self.X[:, k, :], self.dr["xT"][k * 128:(k + 1) * 128, t * TT:(t + 1) * TT],
                         [], [B(f"X{k}")])
            for l in self.layers:
                if l % 2 == 0:
                    self.s5_layer(l)
                else:
                    self.gdn_layer(l)
                self.P.barrier()
            if self.do_final:
                self.final_norm_store(t)
            else:
                for k in range(KT):
                    self.DMA("sync", self.outT[k * 128:(k + 1) * 128, t * TT:(t + 1) * TT], self.X[:, k, :],
                             [B(f"X{k}")], [B(f"out_{t}_{k}")])
                    self.out_bufs.append(B(f"out_{t}_{k}"))
        self.marks["end"] = len(self.P.ops)
        if self.truncate is not None:
            del self.P.ops[self.truncate:]
            self.out_bufs = []
            self.DMA("sync", self.outT[0:128, 0:TT], self.X[:, 0, :], [B("X0")], [B("out_dbg")])
            self.out_bufs.append(B("out_dbg"))
        self.P.op("sync", None, reads=self.out_bufs)
        self.P.emit()
        self.es.close()
        return self.nc


def _kt(w, K):
    return np.ascontiguousarray(w.reshape(K, 128, -1).transpose(1, 0, 2))


def host_layout(inp, b, t0, L):
    f = lambda a: np.ascontiguousarray(np.asarray(a, dtype=np.float32))
    m = {}
    m["xT"] = f(np.asarray(inp["x"])[b, t0:t0 + L, :].T)
    m["memT"] = f(np.asarray(inp["mem"])[b].T)
    m["nw"] = f(np.asarray(inp["norm_w"]).reshape(4, KT, 128).transpose(2, 0, 1))
    m["mnw"] = f(np.asarray(inp["mem_norm_w"]).reshape(4, KT, 128).transpose(2, 0, 1))
    m["fnw"] = f(np.asarray(inp["final_norm_w"]).reshape(KT, 128).T)
    m["win_s5"] = f(np.stack([_kt(np.asarray(inp["s5_w_in"])[j], KT) for j in range(2)]))
    m["win_gdn"] = f(np.stack([_kt(np.asarray(inp["gdn_w_in"])[j], KT) for j in range(2)]))
    m["wout"] = f(np.stack([_kt(np.asarray(inp["w_out"])[i], 16) for i in range(4)]))
    m["wkv"] = f(np.stack([_kt(np.asarray(inp["w_mem_kv"])[i], KT) for i in range(4)]))
    m["wglu"] = f(np.stack([_kt(np.asarray(inp["s5_w_glu"])[j], 12) for j in range(2)]))
    m["bglu"] = f(np.asarray(inp["s5_b_glu"]).reshape(2, 12, 128).transpose(2, 0, 1))
    return m


def host_layout_mix(inp):
    f = lambda a: np.ascontiguousarray(np.asarray(a, dtype=np.float32))
    m = {}
    lr = np.asarray(inp["s5_lambda_re"]); li = np.asarray(inp["s5_lambda_im"]); ls = np.asarray(inp["s5_log_step"])
    sp = lambda a: a.reshape(2, 48, 2, 64).transpose(2, 3, 0, 1).reshape(128, 2, 48)
    m["lamr_sp"] = f(sp(lr)); m["lami_sp"] = f(sp(li))
    m["lst_sp"] = f(sp(np.broadcast_to(ls[:, :, None], (2, 96, 64))))
    spc = lambda a: a.reshape(2, 48, 2, 16, 64).transpose(2, 4, 0, 1, 3).reshape(128, 2, 48, 16)
    m["cre_sp"] = f(spc(np.asarray(inp["s5_c_re"]))); m["cim_sp"] = f(spc(np.asarray(inp["s5_c_im"])))
    spb = lambda a: a.reshape(2, 48, 2, 64, 16).transpose(2, 3, 0, 1, 4).reshape(128, 2, 48, 16)
    m["bre_sp"] = f(spb(np.asarray(inp["s5_b_re"]))); m["bim_sp"] = f(spb(np.asarray(inp["s5_b_im"])))
    cpl = lambda a: np.broadcast_to(a.reshape(2, 12, 8, 1, 64), (2, 12, 8, 16, 64)).transpose(2, 3, 0, 1, 4).reshape(128, 2, 12, 64)
    m["lamr_cp"] = f(cpl(lr)); m["lami_cp"] = f(cpl(li))
    m["lst_cp"] = f(cpl(np.broadcast_to(ls[:, :, None], (2, 96, 64))))
    cpb = lambda a: a.reshape(2, 12, 8, 64, 16).transpose(2, 4, 0, 1, 3).reshape(128, 2, 12, 64)
    m["bre_cp"] = f(cpb(np.asarray(inp["s5_b_re"]))); m["bim_cp"] = f(cpb(np.asarray(inp["s5_b_im"])))
    m["d_cp"] = f(np.asarray(inp["s5_d"]).reshape(2, 12, 128).transpose(2, 0, 1))
    m.update(host_layout_gdn(inp))
    return m


PI = math.pi


def _s5_methods():
    def declare_mix(self):
        d = self.dr
        for n, shp in (("lamr_sp", [128, 2, 48]), ("lami_sp", [128, 2, 48]), ("lst_sp", [128, 2, 48]),
                       ("cre_sp", [128, 2, 48, 16]), ("cim_sp", [128, 2, 48, 16]),
                       ("bre_sp", [128, 2, 48, 16]), ("bim_sp", [128, 2, 48, 16]),
                       ("lamr_cp", [128, 2, 12, 64]), ("lami_cp", [128, 2, 12, 64]), ("lst_cp", [128, 2, 12, 64]),
                       ("bre_cp", [128, 2, 12, 64]), ("bim_cp", [128, 2, 12, 64]), ("d_cp", [128, 2, 12])):
            d[n] = self.din(n, shp)
        self.sc["Ka"] = self.dscr("Ka_d", [2, 12, 128, 8 * 128])
        self.sc["Wb"] = self.dscr("Wb_d", [2, 12, 128, 8 * 2 * 128])
        self.sc["Wc"] = self.dscr("Wc_d", [2, 12, 128, 4 * 8 * 2 * 32])
        self.declare_gdn()

    def alloc_s5_mix(self):
        sb = self.sb
        self.declare_mix()
        A = self.ARENA = sb("ARENA", [128, 36 * 1024], BF16)
        o = 0

        def carve(nbytes):
            nonlocal o
            v = A[:, o // 2:(o + nbytes) // 2]
            o += nbytes
            return v
        self.RDEN = A[:, 35 * 1024:36 * 1024].bitcast(F32)
        self.Y = carve(12 * TT * 2).rearrange("p (j n) -> p j n", j=12)
        self.XS = carve(2 * 48 * 65 * 4).bitcast(F32).rearrange("p (r g c) -> p r g c", r=2, g=48)
        self.XPb = carve(2 * 48 * 64 * 2).rearrange("p (r g c) -> p r g c", r=2, g=48)
        self.TA = carve(2 * 48 * 4).bitcast(F32).rearrange("p (r g) -> p r g", r=2)
        self.TB = carve(2 * 48 * 4).bitcast(F32).rearrange("p (r g) -> p r g", r=2)
        self.GTS5 = carve(TT * 2)
        self.s5_arena_end = o
        self.U = self.YM
        self.AR2 = sb("AR2", [128, 2, 2, 48], F32)
        self.AIS = sb("AIS", [128, 2, 2, 48], F32)
        self.ST5 = sb("ST5", [128, 2, 2, 48], F32)
        self.IDENT = sb("IDENT", [128, 128], F32)
        self.IDENTB = sb("IDENTB", [128, 128], BF16)
        self.MSP = sb("MSP", [128, 2], F32)
        self.MCP = sb("MCP", [128, 2], F32)
        self.DCP = sb("DCP", [128, 2, 12], F32)

    def s5_precompute(self):
        B, d, s = self.B, self.dr, self.sc
        nc = self.nc
        A = self.ARENA
        o = 0

        def carve(shape, dt=F32):
            nonlocal o
            n = int(np.prod(shape))
            nb = n * (4 if dt == F32 else 2)
            v = A[:, o // 2:(o + nb) // 2]
            o += nb
            if dt == F32:
                v = v.bitcast(F32)
            if len(shape) > 1:
                names = " ".join(f"a{i}" for i in range(len(shape)))
                kw = {f"a{i}": shape[i] for i in range(len(shape) - 1)}
                v = v.rearrange(f"p ({names}) -> p {names}", **kw)
            return v
        ar = B("ARENA")
        V = "vector"
        self.P.pool(lambda e: e.iota(self.IDENT[:].bitcast(mybir.dt.int32), pattern=[[1, 128]], base=0, channel_multiplier=-1),
                    [], [B("IDENT")])
        self.CP(V, self.IDENT[:], self.IDENT[:].bitcast(mybir.dt.int32), [B("IDENT")], [B("IDENT")])
        self.TS(V, self.IDENT[:], self.IDENT[:], 0.0, None, ALU.is_equal, None, [B("IDENT")], [B("IDENT")])
        self.CP(V, self.IDENTB[:], self.IDENT[:], [B("IDENT")], [B("IDENTB")])
        self.MS(V, self.MSP[:], 0.0, [B("MSP")])
        self.MS(V, self.MSP[0:64, 0:1], 1.0, [B("MSP")])
        self.MS(V, self.MSP[64:128, 1:2], 1.0, [B("MSP")])
        onesf = carve([1])
        self.MS(V, onesf, 1.0, [ar])
        self.MS(V, self.MCP[:], 0.0, [B("MCP")])
        for blk in range(8):
            g2 = blk % 2
            self.DMA("sync", self.MCP[blk * 16:(blk + 1) * 16, g2:g2 + 1], onesf[0:16, :], [ar, B("MCP")], [B("MCP")])
        self.DMA("sync", self.DCP[:], d["d_cp"], [], [B("DCP")])
        self.MS(V, self.ST5[:], 0.0, [B("ST5")])
        base_o = o

        def chain(lamr, lami, lst, n, npow, rev):
            lr_ = carve([n]); li_ = carve([n]); dt = carve([n]); t1 = carve([n]); t2 = carve([n]); mag = carve([n])
            sn = carve([n]); cs = carve([n]); cr = carve([n]); ci = carve([n])
            Pr = carve([npow, n]); Pi = carve([npow, n])
            self.DMA("sync", lr_, lamr, [], [ar]); self.DMA("sync", li_, lami, [], [ar]); self.DMA("sync", dt, lst, [], [ar])
            self.ACT(dt, dt, AF.Exp, [ar], [ar])
            self.TTo(V, t1, lr_, dt, ALU.mult, [ar], [ar])
            self.TTo(V, t2, li_, dt, ALU.mult, [ar], [ar])
            self.ACT(mag, t1, AF.Exp, [ar], [ar])
            ni_ = carve([n]); mk = carve([n])

            def sin_of(dst, src, shift):
                self.TS(V, dst, src, shift, None, ALU.add, None, [ar], [ar])
                self.TS(V, mk, dst, 1.0 / (2 * PI), None, ALU.mult, None, [ar], [ar])
                self.CP(V, ni_.bitcast(mybir.dt.int32), mk, [ar], [ar])
                self.CP(V, mk, ni_.bitcast(mybir.dt.int32), [ar], [ar])
                self.STT(V, dst, mk, -2 * PI, dst, ALU.mult, ALU.add, [ar], [ar])
                self.TS(V, mk, dst, PI, None, ALU.is_gt, None, [ar], [ar])
                self.STT(V, dst, mk, -2 * PI, dst, ALU.mult, ALU.add, [ar], [ar])
                self.TS(V, mk, dst, -PI, None, ALU.is_lt, None, [ar], [ar])
                self.STT(V, dst, mk, 2 * PI, dst, ALU.mult, ALU.add, [ar], [ar])
                self.ACT(dst, dst, AF.Sin, [ar], [ar])
            sin_of(sn, t2, 0.0)
            sin_of(cs, t2, 0.5 * PI)
            a1r, a1i = t1, t2
            self.TTo(V, a1r, mag, cs, ALU.mult, [ar], [ar])
            self.TTo(V, a1i, mag, sn, ALU.mult, [ar], [ar])
            nr = carve([n]); den = carve([n]); tt = carve([n])
            self.TS(V, nr, a1r, -1.0, None, ALU.add, None, [ar], [ar])
            self.TTo(V, den, lr_, lr_, ALU.mult, [ar], [ar])
            self.TTo(V, tt, li_, li_, ALU.mult, [ar], [ar])
            self.TTo(V, den, den, tt, ALU.add, [ar], [ar])
            self.RCP(den, den, [ar], [ar])
            self.TTo(V, cr, nr, lr_, ALU.mult, [ar], [ar])
            self.TTo(V, tt, a1i, li_, ALU.mult, [ar], [ar])
            self.TTo(V, cr, cr, tt, ALU.add, [ar], [ar])
            self.TTo(V, cr, cr, den, ALU.mult, [ar], [ar])
            self.TTo(V, ci, a1i, lr_, ALU.mult, [ar], [ar])
            self.TTo(V, tt, nr, li_, ALU.mult, [ar], [ar])
            self.TTo(V, ci, ci, tt, ALU.subtract, [ar], [ar])
            self.TTo(V, ci, ci, den, ALU.mult, [ar], [ar])
            ix = (lambda k: npow - 1 - k) if rev else (lambda k: k)
            self.MS(V, Pr[:, ix(0), :], 1.0, [ar]); self.MS(V, Pi[:, ix(0), :], 0.0, [ar])
            for k in range(1, npow):
                p, q = ix(k - 1), ix(k)
                self.TTo(V, Pr[:, q, :], Pr[:, p, :], a1r, ALU.mult, [ar], [ar])
                self.TTo(V, tt, Pi[:, p, :], a1i, ALU.mult, [ar], [ar])
                self.TTo(V, Pr[:, q, :], Pr[:, q, :], tt, ALU.subtract, [ar], [ar])
                self.TTo(V, Pi[:, q, :], Pr[:, p, :], a1i, ALU.mult, [ar], [ar])
                self.TTo(V, tt, Pi[:, p, :], a1r, ALU.mult, [ar], [ar])
                self.TTo(V, Pi[:, q, :], Pi[:, q, :], tt, ALU.add, [ar], [ar])
            return Pr, Pi, cr, ci

        for j in sorted(set(l // 2 for l in self.layers if l % 2 == 0)):
            o = base_o
            Pr, Pi, cr, ci = chain(d["lamr_sp"][:, j, :], d["lami_sp"][:, j, :], d["lst_sp"][:, j, :], 48, 9, False)
            self.CP(V, self.AR2[:, j, 0, :], Pr[:, 8, :], [ar], [B("AR2")])
            self.CP(V, self.AR2[:, j, 1, :], Pr[:, 8, :], [ar], [B("AR2")])
            self.CP(V, self.AIS[:, j, 1, :], Pi[:, 8, :], [ar], [B("AIS")])
            self.TS(V, self.AIS[:, j, 0, :], Pi[:, 8, :], -1.0, None, ALU.mult, None, [ar], [B("AIS")])
            cre = carve([48, 16]); cim = carve([48, 16]); bre = carve([48, 16]); bim = carve([48, 16])
            for t_, n_ in ((cre, "cre_sp"), (cim, "cim_sp"), (bre, "bre_sp"), (bim, "bim_sp")):
                self.DMA("sync", t_, d[n_][:, j, :, :], [], [ar])
            BBr = carve([48, 16]); BBi = carve([48, 16]); tq = carve([48, 16])
            crb = cr.unsqueeze(2).broadcast_to([128, 48, 16]); cib = ci.unsqueeze(2).broadcast_to([128, 48, 16])
            self.TTo(V, BBr, bre, crb, ALU.mult, [ar], [ar]); self.TTo(V, tq, bim, cib, ALU.mult, [ar], [ar])
            self.TTo(V, BBr, BBr, tq, ALU.subtract, [ar], [ar])
            self.TTo(V, BBi, bim, crb, ALU.mult, [ar], [ar]); self.TTo(V, tq, bre, cib, ALU.mult, [ar], [ar])
            self.TTo(V, BBi, BBi, tq, ALU.add, [ar], [ar])
            BBr32 = carve([48, 32]); BBiN32 = carve([48, 32])
            BBiN = tq
            self.TS(V, BBiN, BBi, -1.0, None, ALU.mult, None, [ar], [ar])
            for g2 in range(2):
                self.TS(V, BBr32[:, :, g2 * 16:(g2 + 1) * 16], BBr, self.MSP[:, g2:g2 + 1], None, ALU.mult, None,
                        [ar, B("MSP")], [ar])
                self.TS(V, BBiN32[:, :, g2 * 16:(g2 + 1) * 16], BBiN, self.MSP[:, g2:g2 + 1], None, ALU.mult, None,
                        [ar, B("MSP")], [ar])
            Er = carve([9, 4, 16]); Ei = carve([9, 4, 16]); te = carve([9, 4, 16])
            Er32 = carve([9, 4, 32]); Ei32 = carve([9, 4, 32])
            WcB = carve([4, 8, 2, 32], BF16)
            KaF = carve([8, 128]); KaB = carve([8, 128], BF16)
            self.MS(V, KaF, 0.0, [ar])
            for jt in range(12):
                g0 = jt * 4
                shp = [128, 9, 4, 16]
                cb = lambda c_: c_[:, g0:g0 + 4, :].unsqueeze(1).broadcast_to(shp)
                pb_ = lambda p_: p_[:, :, g0:g0 + 4].unsqueeze(3).broadcast_to(shp)
                self.TTo(V, Er, cb(cre), pb_(Pr), ALU.mult, [ar], [ar]); self.TTo(V, te, cb(cim), pb_(Pi), ALU.mult, [ar], [ar])
                self.TTo(V, Er, Er, te, ALU.subtract, [ar], [ar])
                self.TTo(V, Ei, cb(cre), pb_(Pi), ALU.mult, [ar], [ar]); self.TTo(V, te, cb(cim), pb_(Pr), ALU.mult, [ar], [ar])
                self.TTo(V, Ei, Ei, te, ALU.add, [ar], [ar])
                for g2 in range(2):
                    self.TS(V, Er32[:, :, :, g2 * 16:(g2 + 1) * 16], Er, self.MSP[:, g2:g2 + 1], None, ALU.mult, None,
                            [ar, B("MSP")], [ar])
                    self.TS(V, Ei32[:, :, :, g2 * 16:(g2 + 1) * 16], Ei, self.MSP[:, g2:g2 + 1], None, ALU.mult, None,
                            [ar, B("MSP")], [ar])
                self.CP(V, WcB[:, :, :, 0, :], Er32[:, 1:9, :, :].rearrange("p k g c -> p g k c"), [ar], [B("WcB")])
                self.TS(V, WcB[:, :, :, 1, :], Ei32[:, 1:9, :, :].rearrange("p k g c -> p g k c"), -1.0, None, ALU.mult, None,
                        [ar], [B("WcB")])
                self.DMA("sync", s["Wc"][j, jt], WcB.rearrange("p g t r c -> p (g t r c)"), [B("WcB")], [B(f"Wc_d{j}_{jt}")])
                pk = 6
                PSK = self.PS[pk][:, 0:256].rearrange("p (t c) -> p t c", t=8)
                for q in range(4):
                    for tau in range(8):
                        self.MM(PSK[32 * q:32 * q + 32, tau, :], BBr32[:, g0 + q, :], Er32[:, tau, q, :], True, False,
                                [ar], [B(f"PS{pk}")], tile_position=(0, 32 * q))
                        self.MM(PSK[32 * q:32 * q + 32, tau, :], BBiN32[:, g0 + q, :], Ei32[:, tau, q, :], False, True,
                                [ar], [B(f"PS{pk}")], tile_position=(0, 32 * q))
                for q in range(4):
                    self.CP(V, KaF[32 * q:32 * q + 32, :, 32 * q:32 * q + 32], PSK[32 * q:32 * q + 32, :, :],
                            [B(f"PS{pk}")], [B("KaF")])
                self.STT(V, KaB[:, 0, :], self.IDENT[:], self.DCP[:, j, jt:jt + 1], KaF[:, 0, :], ALU.mult, ALU.add,
                         [B("IDENT"), B("DCP"), B("KaF")], [B("KaB")])
                self.CP(V, KaB[:, 1:8, :], KaF[:, 1:8, :], [B("KaF")], [B("KaB")])
                self.DMA("sync", s["Ka"][j, jt], KaB.rearrange("p t c -> p (t c)"), [B("KaB")], [B(f"Ka_d{j}_{jt}")])
                assert o <= 72 * 1024, o
            for jt in range(12):
                o = base_o
                n = 64
                Pr, Pi, cr, ci = chain(d["lamr_cp"][:, j, jt, :], d["lami_cp"][:, j, jt, :], d["lst_cp"][:, j, jt, :], n, 8, True)
                bre = carve([n]); bim = carve([n]); BBr = carve([n]); BBi = carve([n]); tq = carve([n])
                self.DMA("sync", bre, d["bre_cp"][:, j, jt, :], [], [ar]); self.DMA("sync", bim, d["bim_cp"][:, j, jt, :], [], [ar])
                self.TTo(V, BBr, bre, cr, ALU.mult, [ar], [ar]); self.TTo(V, tq, bim, ci, ALU.mult, [ar], [ar])
                self.TTo(V, BBr, BBr, tq, ALU.subtract, [ar], [ar])
                self.TTo(V, BBi, bim, cr, ALU.mult, [ar], [ar]); self.TTo(V, tq, bre, ci, ALU.mult, [ar], [ar])
                self.TTo(V, BBi, BBi, tq, ALU.add, [ar], [ar])
                Wr = carve([8, n]); Wi = carve([8, n]); tw = carve([8, n])
                bb = lambda a_: a_.unsqueeze(1).broadcast_to([128, 8, n])
                self.TTo(V, Wr, Pr, bb(BBr), ALU.mult, [ar], [ar]); self.TTo(V, tw, Pi, bb(BBi), ALU.mult, [ar], [ar])
                self.TTo(V, Wr, Wr, tw, ALU.subtract, [ar], [ar])
                self.TTo(V, Wi, Pr, bb(BBi), ALU.mult, [ar], [ar]); self.TTo(V, tw, Pi, bb(BBr), ALU.mult, [ar], [ar])
                self.TTo(V, Wi, Wi, tw, ALU.add, [ar], [ar])
                WbB = carve([8, 2, 128], BF16)
                for ri, W3 in ((0, Wr), (1, Wi)):
                    for g2 in range(2):
                        self.TS(V, WbB[:, :, ri, g2 * 64:(g2 + 1) * 64], W3, self.MCP[:, g2:g2 + 1], None,
                                ALU.mult, None, [ar, B("MCP")], [ar])
                self.DMA("sync", s["Wb"][j, jt], WbB.rearrange("p s r c -> p (s r c)"), [ar], [B(f"Wb_d{j}_{jt}")])
                assert o <= 72 * 1024, o
        fence = [ar, B("WcB"), B("KaB"), B("KaF")]
        for j in (0, 1):
            for jt in range(12):
                for nm in ("Ka_d", "Wb_d", "Wc_d"):
                    if f"{nm}{j}_{jt}" in self.bufs:
                        fence.append(B(f"{nm}{j}_{jt}"))
        fence += [B("MCP"), B("MSP"), B("IDENT"), B("IDENTB")]
        for e_ in ("tensor", "vector", "scalar", "gpsimd", "sync"):
            self.P.op(e_, None, writes=fence)

    def s5_state_part(self, j):
        B, s = self.B, self.sc
        self.CP("gpsimd", self.XS[:, :, :, 0], self.ST5[:, j, :, :], [B("ST5")], [B("XS")])
        for jt in range(12):
            wb, wbuf = self.load_w(s["Wb"][j, jt].rearrange("p (a n) -> p a n", a=1), 1, 2048, [B(f"Wb_d{j}_{jt}")])
            W = wb[:, 0, :].rearrange("p (s r c) -> p s r c", s=8, r=2)
            for ri in range(2):
                for sidx in range(8):
                    for q in range(4):
                        pq = 4 + q
                        PZ = self.PS[pq][:, :].rearrange("p (a r c) -> p a r c", a=4, r=2)
                        self.MM(PZ[:, jt % 4, ri, :], W[32 * q:32 * q + 32, sidx, ri, :],
                                self.U[32 * q:32 * q + 32, jt, sidx * 64:(sidx + 1) * 64], sidx == 0, sidx == 7,
                                [wbuf, B(f"YM{jt}")], [B(f"PS{pq}")], tile_position=(32 * q, 0), skip_group_check=True)
            if jt % 4 == 3:
                jt0 = jt - 3
                for q in range(4):
                    pq = 4 + q
                    PZ = self.PS[pq][:, :].rearrange("p (a r c) -> p a r c", a=4, r=2)
                    self.CP("scalar" if q % 2 else "vector", self.XS[:, :, 4 * jt0 + q:4 * jt0 + q + 13:4, 1:65],
                            PZ.rearrange("p a r c -> p r a c"), [B(f"PS{pq}")], [B("XS")])
        self.marks.setdefault("st_mm", len(self.P.ops))
        E = "gpsimd"
        xs = B("XS")
        for c in range(NCH):
            self.TTo(E, self.TA[:], self.AR2[:, j], self.XS[:, :, :, c], ALU.mult, [B("AR2"), xs], [B("TA")])
            self.TTo(E, self.TB[:, 0, :], self.AIS[:, j, 0, :], self.XS[:, 1, :, c], ALU.mult, [B("AIS"), xs], [B("TB")])
            self.TTo(E, self.TB[:, 1, :], self.AIS[:, j, 1, :], self.XS[:, 0, :, c], ALU.mult, [B("AIS"), xs], [B("TB")])
            self.TTo(E, self.TA[:], self.TA[:], self.TB[:], ALU.add, [B("TA"), B("TB")], [B("TA")])
            self.TTo(E, self.XS[:, :, :, c + 1], self.XS[:, :, :, c + 1], self.TA[:], ALU.add, [xs, B("TA")], [xs])
        self.marks.setdefault("st_scan", len(self.P.ops))
        self.CP("scalar", self.XPb[:], self.XS[:, :, :, 0:64], [xs], [B("XPb")])
        self.CP("gpsimd", self.ST5[:, j, :, :], self.XS[:, :, :, 64], [xs], [B("ST5")])

    def s5_out_part(self, j):
        B, s = self.B, self.sc
        for jt in range(12):
            i = self.wb_rr % self.NWB
            self.wb_rr += 1
            wbuf = B(f"WB{i}")
            KA = self.WB[i][:, 0:1024].rearrange("p (t c) -> p t c", t=8)
            WC = self.WB[i][:, 1024:3072].rearrange("p (g t r c) -> p g t r c", g=4, t=8, r=2)
            self.DMA("sync", self.WB[i][:, 0:1024], s["Ka"][j, jt], [B(f"Ka_d{j}_{jt}")], [wbuf])
            self.DMA("sync", self.WB[i][:, 1024:3072], s["Wc"][j, jt], [B(f"Wc_d{j}_{jt}")], [wbuf])
            p = self.next_ps()
            PY = self.PS[p][:, :].rearrange("p (t c) -> p t c", t=8)
            PYf = self.PS[p][:, :]
            for tau in range(8):
                self.MM(PYf[:, tau * 64:512], KA[:, tau, :], self.U[:, jt, 0:(8 - tau) * 64], tau == 0, False,
                        [wbuf, B(f"YM{jt}")], [B(f"PS{p}")], skip_group_check=True)
            for t in range(8):
                for ri in range(2):
                    for q in range(4):
                        self.MM(PY[32 * q:32 * q + 32, t, :], WC[:, q, t, ri, :], self.XPb[:, ri, 4 * jt + q, :], False,
                                ri == 1, [wbuf, B("XPb")], [B(f"PS{p}")], tile_position=(0, 32 * q), skip_group_check=True)
            self.ACT(self.Y[:, jt, :].rearrange("p (c s) -> p s c", s=8), PY, AF.Gelu_apprx_tanh,
                     [B(f"PS{p}")], [B(f"Y{jt}")])
        self.marks.setdefault("out_mm", len(self.P.ops))
        yb = [B(f"Y{k}") for k in range(12)]
        for ch in range(6):
            wb, wbuf = self.load_w(s["wglu"][j, :, :, ch * 256:(ch + 1) * 256], 12, 256,
                                   [B(f"wglu_b{j}_{k}") for k in range(12)])
            for i in range(2):
                m = ch * 2 + i
                p = self.next_ps()
                for k in range(12):
                    self.MM(self.PS[p][:], wb[:, k, i * 128:(i + 1) * 128], self.Y[:, k, :], k == 0, k == 11,
                            [wbuf, yb[k]], [B(f"PS{p}")])
                self.ACT(self.GTS5, self.PS[p][:], AF.Sigmoid, [B(f"PS{p}"), B("BGLU")], [B("GTS5")],
                         bias=self.BGLU[:, j, m:m + 1])
                self.TTo("vector", self.GTS5, self.GTS5, self.Y[:, m, :], ALU.mult, [B("GTS5"), yb[m]], [B("GTS5")])
                self.TTo("gpsimd", self.YM[:, m, :], self.GTS5, self.SG[:, m, :], ALU.mult, [B("GTS5"), B(f"SG{m}")],
                         [B(f"YM{m}")])

    for k_, v_ in list(locals().items()):
        if callable(v_):
            setattr(Builder, k_, v_)


_s5_methods()


def host_layout_gdn(inp):
    f = lambda a: np.ascontiguousarray(np.asarray(a, dtype=np.float32))
    m = {}
    m["g_alog"] = f(np.asarray(inp["gdn_a_log"]).T)
    m["g_dtb"] = f(np.asarray(inp["gdn_dt_bias"]).T)
    m["g_nw"] = f(np.asarray(inp["gdn_norm_w"]).T)
    cw = np.asarray(inp["gdn_conv_w"])
    m["g_cw"] = f(cw.reshape(2, 4, 24, 128).transpose(3, 0, 2, 1))
    return m


def _gdn_methods():
    def declare_gdn(self):
        d = self.dr
        d["g_alog"] = self.din("g_alog", [12, 2])
        d["g_dtb"] = self.din("g_dtb", [12, 2])
        d["g_nw"] = self.din("g_nw", [128, 2])
        d["g_cw"] = self.din("g_cw", [128, 2, 24, 4])

    def alloc_gdn(self):
        sb = self.sb
        A = self.ARENA
        o = 0

        def carve(nbytes):
            nonlocal o
            v = A[:, o // 2:(o + nbytes) // 2]
            o += nbytes
            return v
        f32v = lambda v, *shape: (v.bitcast(F32) if not shape else v.bitcast(F32))
        self.QKV = carve(24 * TT * 2).rearrange("p (j n) -> p j n", j=24)
        self.XC = [carve(516 * 4).bitcast(F32) for _ in range(2)]
        self.ACC = [carve(TT * 4).bitcast(F32) for _ in range(2)]
        self.Gg = carve(TT * 4).bitcast(F32)
        self.BETA = carve(TT * 4).bitcast(F32)
        self.GCF = carve(TT * 4).bitcast(F32)
        self.E1 = carve(TT * 4).bitcast(F32)
        tm = lambda: carve(48 * 4).bitcast(F32).rearrange("p (a h) -> p a h", a=4)
        self.GT, self.BT, self.GCT, self.GLT, self.SC1, self.SC2 = tm(), tm(), tm(), tm(), tm(), tm()
        m32 = lambda cv: cv(128 * 4).bitcast(F32)
        m16 = lambda cv: cv(128 * 2)

        class Slot:
            pass

        def mk_slot(cv, tag):
            S = Slot()
            S.tag = tag
            S.GRs = cv(TT * 4).bitcast(F32)
            S.EGR = cv(TT * 4).bitcast(F32)
            S.KBT = cv(TT * 2)
            S.QG = cv(TT * 2)
            S.TMP1, S.DEC, S.DL, S.DA = m32(cv), m32(cv), m32(cv), m32(cv)
            S.LTm, S.L0 = m16(cv), m16(cv)
            S.Pb = [m16(cv) for _ in range(2)]
            S.PTb = [m16(cv) for _ in range(2)]
            S.RTb = [m16(cv) for _ in range(2)]
            S.Us = m32(cv)
            S.TMT, S.AT, S.VB, S.KBG, S.KDEC, S.WTs, S.VN, S.ON, S.Sb = (m16(cv) for _ in range(9))
            S.SSQ = cv(4 * 4).bitcast(F32)
            S.JUNK = m16(cv)
            return S
        self.slots = [mk_slot(carve, "a")]
        dead_bf = [v.bitcast(BF16) for v in (self.XC[0], self.XC[1], self.ACC[0], self.ACC[1], self.E1)]
        dpos = [0, 0]

        def carve2(nbytes):
            n = nbytes // 2
            while dpos[0] < len(dead_bf):
                v = dead_bf[dpos[0]]
                if dpos[1] + n <= v.shape[1]:
                    out = v[:, dpos[1]:dpos[1] + n]
                    dpos[1] += n
                    return out
                dpos[0] += 1
                dpos[1] = 0
            return carve(nbytes)
        self.slots.append(mk_slot(carve2, "b"))
        dead_c = [self.QX[:, :, :].rearrange("p a n -> p (a n)"), self.SGX[:, :, :].rearrange("p a n -> p (a n)"),
                  self.EX[:, :, :].rearrange("p a n -> p (a n)"), self.RSTD[:, :].bitcast(BF16), self.SQ[:, :]]
        cpos = [0, 0]

        def carve3(nbytes):
            n = nbytes // 2
            while cpos[0] < len(dead_c):
                v = dead_c[cpos[0]]
                if cpos[1] + n <= v.shape[1]:
                    out = v[:, cpos[1]:cpos[1] + n]
                    cpos[1] += n
                    return out
                cpos[0] += 1
                cpos[1] = 0
            return carve(nbytes)
        self.slots.append(mk_slot(carve3, "c"))
        dead_d = [self.H[:, :, :].rearrange("p a n -> p (a n)")]
        dq = [0, 0]

        def carve4(nbytes):
            n = nbytes // 2
            while dq[0] < len(dead_d):
                v = dead_d[dq[0]]
                if dq[1] + n <= v.shape[1]:
                    out = v[:, dq[1]:dq[1] + n]
                    dq[1] += n
                    return out
                dq[0] += 1
                dq[1] = 0
            return carve(nbytes)
        self.slots.append(mk_slot(carve4, "d"))
        assert o <= 64 * 1024, o
        self.SEL = A[0:12, 32 * 1024:32 * 1024 + 12 * 128 * 2].bitcast(F32).rearrange("p (h m) -> p h m", h=12)
        self.SST = sb("SST", [128, 2, 12, 128], F32)
        self.CONVST = sb("CONVST", [128, 2, 24, 3], F32)
        self.CW = sb("CW", [128, 2, 24, 4], F32)
        self.TRI = sb("TRI", [128, 128], F32)
        self.MSU = sb("MSU", [128, 128], F32)
        self.BLK = sb("BLK", [128, 128], F32)
        self.ONESF = sb("ONESF", [128, 128], F32)
        self.GNW = sb("GNW", [128, 2], F32)
        self.NA = sb("NA", [12, 2], F32)
        self.DTB = sb("DTB", [12, 2], F32)

    def gdn_consts(self):
        B, d = self.B, self.dr
        V = "vector"
        self.MS(V, self.SST[:], 0.0, [B("SST")])
        self.MS(V, self.CONVST[:], 0.0, [B("CONVST")])
        self.MS(V, self.ONESF[:], 1.0, [B("ONESF")])
        self.DMA("sync", self.CW[:], d["g_cw"], [], [B("CW")])
        self.DMA("sync", self.GNW[:], d["g_nw"], [], [B("GNW")])
        self.DMA("sync", self.NA[:], d["g_alog"], [], [B("NA")])
        self.DMA("sync", self.DTB[:], d["g_dtb"], [], [B("DTB")])
        self.ACT(self.NA[:], self.NA[:], AF.Exp, [B("NA")], [B("NA")])
        self.TS(V, self.NA[:], self.NA[:], -1.0, None, ALU.mult, None, [B("NA")], [B("NA")])
        self.P.pool(lambda e: e.iota(self.TRI[:].bitcast(mybir.dt.int32), pattern=[[1, 128]], base=0, channel_multiplier=-1),
                    [], [B("TRI")])
        self.CP(V, self.TRI[:], self.TRI[:].bitcast(mybir.dt.int32), [B("TRI")], [B("TRI")])
        self.TS(V, self.TRI[:], self.TRI[:], 0.0, None, ALU.is_ge, None, [B("TRI")], [B("TRI")])
        self.MS(V, self.TRI[0:64, 64:128], 0.0, [B("TRI")])
        self.TTo(V, self.MSU[:], self.TRI[:], self.IDENT[:], ALU.subtract, [B("TRI"), B("IDENT")], [B("MSU")])
        self.MS(V, self.BLK[:], 0.0, [B("BLK")])
        self.MS(V, self.BLK[0:64, 0:64], 1.0, [B("BLK")])
        self.MS(V, self.BLK[64:128, 64:128], 1.0, [B("BLK")])
        for h in range(12):
            self.TS(V, self.SEL[:, h, :], self.ONESF[0:12, :], self.IDENT[0:12, h:h + 1], None, ALU.mult, None,
                    [B("ONESF"), B("IDENT")], [B("SEL")])

    def gdn_head(self, j, h, S):
        B = self.B
        V, G_ = "vector", "gpsimd"
        gh = B("GH" + S.tag)
        gp_ = B("GP" + S.tag)
        hq = h // 2
        QT, KTt, VT = self.QKV[:, hq, :], self.QKV[:, 6 + hq, :], self.QKV[:, 12 + h, :]
        bq, bk, bv = B(f"QKV{hq}"), B(f"QKV{6 + hq}"), B(f"QKV{12 + h}")
        Sst = self.SST[:, j, h, :]
        bs = B(f"SST{j}_{h}")
        p = self.next_ps(0, 8)
        self.MM(self.PS[p][:], self.SEL[:, h, :], self.GCF[0:12, :], True, True, [B("SEL"), B("GCF")], [B(f"PS{p}")])
        yield
        self.CP(V, S.GRs, self.PS[p][:], [B(f"PS{p}")], [gh])
        yield
        self.ACT(S.EGR, self.PS[p][:], AF.Exp, [B(f"PS{p}")], [gh])
        yield
        p = self.next_ps(0, 8)
        self.MM(self.PS[p][:], self.SEL[:, h, :], self.BETA[0:12, :], True, True, [B("SEL"), B("BETA")], [B(f"PS{p}")])
        yield
        self.TTo(V, S.KBT, KTt, self.PS[p][:], ALU.mult, [bk, B(f"PS{p}")], [gh])
        yield
        self.TTo(G_, S.QG, QT, S.EGR, ALU.mult, [bq, gh], [gh])
        yield
        self.CP("scalar", S.Sb, Sst, [bs], [B("Sb" + S.tag)])
        yield
        for pr in range(4):
            cs = slice(pr * 128, (pr + 1) * 128)
            p = self.next_ps(0, 8)
            self.MM(self.PS[p][:, 0:128], KTt[:, cs], S.KBT[:, cs], True, True, [bk, gh], [B(f"PS{p}")])
            yield
            self.MM(self.PS[p][:, 128:256], KTt[:, cs], QT[:, cs], True, True, [bk, bq], [B(f"PS{p}")], skip_group_check=True)
            yield
            self.TS(V, S.TMP1, S.GRs[:, cs], self.GCT[:, pr, h:h + 1], 0.0, ALU.subtract, ALU.min,
                    [gh, B("GCT")], [gp_])
            yield
            self.ACT(S.DEC, S.TMP1, AF.Exp, [gp_], [gp_])
            yield
            self.TTo(G_, S.DL, S.DEC, self.MSU[:], ALU.mult, [gp_, B("MSU")], [gp_])
            yield
            self.TTo(G_, S.DA, S.DEC, self.TRI[:], ALU.mult, [gp_, B("TRI")], [gp_])
            yield
            self.TTo(V, S.LTm, self.PS[p][:, 0:128], S.DL, ALU.mult, [B(f"PS{p}"), gp_], [gp_])
            yield
            self.TTo(V, S.AT, self.PS[p][:, 128:256], S.DA, ALU.mult, [B(f"PS{p}"), gp_], [gp_])
            yield
            p = self.next_ps(0, 8)
            pl16 = self.PS[p][:, :].bitcast(BF16)
            self.TR(pl16[:, 0:128], S.LTm, self.IDENTB[:], [gp_, B("IDENTB")], [B(f"PS{p}")])
            yield
            self.CP("scalar", S.L0, pl16[:, 0:128], [B(f"PS{p}")], [gp_])
            yield
            self.TTo(V, S.RTb[0], self.IDENTB[:], S.LTm, ALU.subtract, [B("IDENTB"), gp_], [gp_])
            yield
            Pc, PTc, RTc = S.L0, S.LTm, S.RTb[0]
            for k in range(5):
                Pn, PTn, RTn = S.Pb[k % 2], S.PTb[k % 2], S.RTb[(k + 1) % 2]
                p = self.next_ps(0, 8)
                self.MM(self.PS[p][:, 0:128], PTc, Pc, True, True, [gp_], [B(f"PS{p}")])
                yield
                if k < 4:
                    self.MM(self.PS[p][:, 128:256], Pc, PTc, True, True, [gp_], [B(f"PS{p}")], skip_group_check=True)
                    yield
                self.CP("scalar", Pn, self.PS[p][:, 0:128], [B(f"PS{p}")], [gp_])
                yield
                if k < 4:
                    self.CP(V, PTn, self.PS[p][:, 128:256], [B(f"PS{p}")], [gp_])
                    yield
                p2 = self.next_ps(0, 8)
                self.MM(self.PS[p2][:, 0:128], Pn, RTc, True, True, [gp_], [B(f"PS{p2}")])
                yield
                if k < 4:
                    self.TTo(V, RTn, RTc, self.PS[p2][:, 0:128], ALU.add, [gp_, B(f"PS{p2}")], [gp_])
                    yield
                else:
                    self.TTo(V, S.TMT, RTc, self.PS[p2][:, 0:128], ALU.add, [gp_, B(f"PS{p2}")], [gp_])
                    yield
                Pc, PTc, RTc = Pn, PTn, RTn
            p = self.next_ps(0, 8)
            pb16 = self.PS[p][:, :].bitcast(BF16)
            self.TR(pb16[:, 0:128], VT[:, cs], self.IDENTB[:], [bv, B("IDENTB")], [B(f"PS{p}")])
            yield
            self.TR(pb16[:, 128:256], KTt[:, cs], self.IDENTB[:], [bk, B("IDENTB")], [B(f"PS{p}")])
            yield
            self.TS(V, S.VB, pb16[:, 0:128], self.BT[:, pr, h:h + 1], None, ALU.mult, None, [B(f"PS{p}"), B("BT")], [gp_])
            yield
            self.TS(V, S.KBG, pb16[:, 128:256], self.SC1[:, pr, h:h + 1], None, ALU.mult, None,
                    [B(f"PS{p}"), B("SC1")], [gp_])
            yield
            self.TS(V, S.KDEC, pb16[:, 128:256], self.SC2[:, pr, h:h + 1], None, ALU.mult, None,
                    [B(f"PS{p}"), B("SC2")], [gp_])
            yield
            p = self.next_ps(0, 8)
            self.MM(self.PS[p][:, 0:128], S.TMT, S.VB, True, True, [gp_], [B(f"PS{p}")])
            yield
            self.MM(self.PS[p][:, 128:256], S.KBG, S.TMT, True, True, [gp_], [B(f"PS{p}")], skip_group_check=True)
            yield
            self.CP("scalar", S.Us, self.PS[p][:, 0:128], [B(f"PS{p}")], [gp_])
            yield
            self.CP(V, S.WTs, self.PS[p][:, 128:256], [B(f"PS{p}")], [gp_])
            yield
            for c in range(2):
                rs = slice(c * 64, (c + 1) * 64)
                gcol = pr * 128 + c * 64
                pw = self.next_ps(0, 8)
                self.MM(self.PS[pw][rs, 0:128], S.WTs[:, rs], S.Sb, True, True, [gp_, B("Sb" + S.tag)], [B(f"PS{pw}")],
                        tile_position=(0, 64 * c))
                yield
                self.TTo(V, S.VN[rs, :], S.Us[rs, :], self.PS[pw][rs, 0:128], ALU.subtract,
                         [gp_, B(f"PS{pw}")], [B("VN" + S.tag)])
                yield
                po = self.next_ps(0, 8)
                self.MM(self.PS[po][rs, 0:128], S.QG[:, gcol:gcol + 64], S.Sb, True, False, [gh, B("Sb" + S.tag)],
                        [B(f"PS{po}")], tile_position=(0, 64 * c))
                yield
                self.MM(self.PS[po][rs, 0:128], S.AT[rs, rs], S.VN[rs, :], False, True, [gp_, B("VN" + S.tag)],
                        [B(f"PS{po}")], tile_position=(64 * c, 64 * c))
                yield
                pu = self.next_ps(0, 8)
                self.MM(self.PS[pu][:, 0:128], S.KDEC[rs, :], S.VN[rs, :], True, True, [gp_, B("VN" + S.tag)],
                        [B(f"PS{pu}")], tile_position=(64 * c, 0))
                yield
                self.STT(V, Sst, Sst, S.EGR[:, gcol + 63:gcol + 64], self.PS[pu][:, 0:128], ALU.mult, ALU.add,
                         [bs, gh, B(f"PS{pu}")], [bs])
                yield
                self.CP("scalar", S.Sb, Sst, [bs], [B("Sb" + S.tag)])
                yield
                self.ACT(S.JUNK[rs, :], self.PS[po][rs, 0:128], AF.Square, [B(f"PS{po}")], [B("JUNK" + S.tag), B("SSQ" + S.tag)],
                         accum_out=S.SSQ[rs, 0:1])
                yield
                self.ACT(S.SSQ[rs, 2:3], S.SSQ[rs, 0:1], AF.Ln, [B("SSQ" + S.tag), B("EPSC")], [B("SSQ" + S.tag)],
                         scale=1.0 / 128, bias=self._eps[rs, 0:1])
                yield
                self.ACT(S.SSQ[rs, 2:3], S.SSQ[rs, 2:3], AF.Exp, [B("SSQ" + S.tag)], [B("SSQ" + S.tag)], scale=-0.5)
                yield
                self.TS(V, S.ON[rs, :], self.PS[po][rs, 0:128], S.SSQ[rs, 2:3], None, ALU.mult, None,
                        [B(f"PS{po}"), B("SSQ" + S.tag)], [B("ON" + S.tag)])
                yield
            p = self.next_ps(0, 8)
            pb16 = self.PS[p][:, :].bitcast(BF16)
            self.TR(pb16[:, 0:128], S.ON, self.IDENTB[:], [B("ON" + S.tag), B("IDENTB")], [B(f"PS{p}")])
            yield
            self.STT(V, self.YM[:, h, cs], pb16[:, 0:128], self.GNW[:, j:j + 1], self.SG[:, h, cs], ALU.mult, ALU.mult,
                     [B(f"PS{p}"), B("GNW"), B(f"SG{h}")], [B(f"YM{h}")])
            yield


    def gdn_layer(self, l):
        B, s = self.B, self.sc
        j = l // 2
        V, G_ = "vector", "gpsimd"
        self.norm_in(l)
        wsrc = s["win_gdn"][j]
        wbufs = [B(f"win_gdn_b{j}_{k}") for k in range(KT)]
        ga = B("GA")

        def ev_a(i, ps, pbuf, m):
            self.ACT(self.E1[0:12, :], ps[0:12, :], AF.Exp, [pbuf, B("DTB")], [ga], bias=self.DTB[:, j:j + 1])
            self.ACT(self.E1[0:12, :], self.E1[0:12, :], AF.Ln, [ga, B("ONESF")], [ga], bias=self.ONESF[0:12, 0:1])
            self.TS(V, self.Gg[0:12, :], self.E1[0:12, :], self.NA[:, j:j + 1], None, ALU.mult, None, [ga, B("NA")], [B("Gg")])

        def ev_b(i, ps, pbuf, m):
            self.ACT(self.BETA[0:12, :], ps[0:12, :], AF.Sigmoid, [pbuf], [B("BETA")])
        self.proj_cols(wsrc, wbufs, 3072, 12, ev_a)
        self.proj_cols(wsrc, wbufs, 3084, 12, ev_b)
        p = self.next_ps()
        for pr in range(4):
            self.TR(self.PS[p][:, pr * 12:(pr + 1) * 12], self.Gg[0:12, pr * 128:(pr + 1) * 128], self.IDENT[0:12, 0:12],
                    [B("Gg"), B("IDENT")], [B(f"PS{p}")])
            self.TR(self.PS[p][:, 48 + pr * 12:48 + (pr + 1) * 12], self.BETA[0:12, pr * 128:(pr + 1) * 128],
                    self.IDENT[0:12, 0:12], [B("BETA"), B("IDENT")], [B(f"PS{p}")])
        self.CP(V, self.GT.rearrange("p a h -> p (a h)"), self.PS[p][:, 0:48], [B(f"PS{p}")], [B("GT")])
        self.CP(V, self.BT.rearrange("p a h -> p (a h)"), self.PS[p][:, 48:96], [B(f"PS{p}")], [B("BT")])
        p = self.next_ps()
        for pr in range(4):
            self.MM(self.PS[p][:, pr * 12:(pr + 1) * 12], self.TRI[:], self.GT[:, pr, :], True, True,
                    [B("TRI"), B("GT")], [B(f"PS{p}")])
            self.MM(self.PS[p][:, 48 + pr * 12:48 + (pr + 1) * 12], self.BLK[:], self.GT[:, pr, :], True, True,
                    [B("BLK"), B("GT")], [B(f"PS{p}")], skip_group_check=True)
        self.CP(V, self.GCT.rearrange("p a h -> p (a h)"), self.PS[p][:, 0:48], [B(f"PS{p}")], [B("GCT")])
        self.CP(V, self.GLT.rearrange("p a h -> p (a h)"), self.PS[p][:, 48:96], [B(f"PS{p}")], [B("GLT")])
        p = self.next_ps()
        for pr in range(4):
            self.MM(self.PS[p][0:12, pr * 128:(pr + 1) * 128], self.GT[:, pr, :], self.TRI[:], True, True,
                    [B("TRI"), B("GT")], [B(f"PS{p}")], skip_group_check=True)
        self.CP(V, self.GCF[0:12, :], self.PS[p][0:12, :], [B(f"PS{p}")], [B("GCF")])
        fl = lambda t_: t_.rearrange("p a h -> p (a h)")
        self.ACT(fl(self.SC1), fl(self.GCT), AF.Exp, [B("GCT")], [B("SC1")])
        self.TTo(V, fl(self.SC1), fl(self.SC1), fl(self.BT), ALU.mult, [B("SC1"), B("BT")], [B("SC1")])
        self.TTo(V, fl(self.SC2), fl(self.GLT), fl(self.GCT), ALU.subtract, [B("GLT"), B("GCT")], [B("SC2")])
        self.ACT(fl(self.SC2), fl(self.SC2), AF.Exp, [B("SC2")], [B("SC2")])

        def ev_qkv(c0):
            def f(i, ps, pbuf, m):
                ti = c0 // 128 + i
                xc, acc = self.XC[ti % 2], self.ACC[ti % 2]
                xb, ab = B(f"XC{ti % 2}"), B(f"ACC{ti % 2}")
                self.CP(G_, xc[:, 0:3], self.CONVST[:, j, ti, :], [B("CONVST")], [xb])
                self.CP("scalar", xc[:, 3:515], ps[:, :], [pbuf], [xb])
                self.CP(G_, self.CONVST[:, j, ti, :], xc[:, 512:515], [xb], [B("CONVST")])
                self.TS(V, acc, xc[:, 0:512], self.CW[:, j, ti, 0:1], None, ALU.mult, None, [xb, B("CW")], [ab])
                for k in range(1, 4):
                    self.STT(V, acc, xc[:, k:k + 512], self.CW[:, j, ti, k:k + 1], acc, ALU.mult, ALU.add,
                             [xb, B("CW"), ab], [ab])
                self.ACT(self.QKV[:, ti, :], acc, AF.Silu, [ab], [B(f"QKV{ti}")])
            return f
        for c in range(6):
            self.proj_cols(wsrc, wbufs, c * 512, 512, ev_qkv(c * 512))
            if c == 2:
                for ti in range(12):
                    self.ACT(self.SQ[:], self.QKV[:, ti, :], AF.Square, [B(f"QKV{ti}")], [B("SQ")])
                    pp = self.next_ps()
                    self.MM(self.PS[pp][:], self.ONES[:], self.SQ[:], True, True, [B("ONES"), B("SQ")], [B(f"PS{pp}")])
                    self.rstd_from(self.RSTD[:], self.PS[pp][:], 1.0, [B(f"PS{pp}")], [B("RSTD")])
                    if ti < 6:
                        self.STT(V, self.QKV[:, ti, :], self.QKV[:, ti, :], ISQ, self.RSTD[:], ALU.mult, ALU.mult,
                                 [B(f"QKV{ti}"), B("RSTD")], [B(f"QKV{ti}")])
                    else:
                        self.TTo(V, self.QKV[:, ti, :], self.QKV[:, ti, :], self.RSTD[:], ALU.mult,
                                 [B(f"QKV{ti}"), B("RSTD")], [B(f"QKV{ti}")])

        def ev_gate(c):
            def f(i, ps, pbuf, m):
                self.ACT(self.SG[:, 4 * c + i, :], ps[:, :], AF.Silu, [pbuf], [B(f"SG{4 * c + i}")])
            return f
        for c in range(3):
            self.proj_cols(wsrc, wbufs, 3096 + c * 512, 512, ev_gate(c))
        self.proj_cols(wsrc, wbufs, 4632, 512,
                       lambda i, ps, pbuf, m: self.CP(V, self.QX[:, i, :], ps[:, :], [pbuf], [B(f"QX{i}")]))
        self.proj_cols(wsrc, wbufs, 5144, 512,
                       lambda i, ps, pbuf, m: self.ACT(self.SGX[:, i, :], ps[:, :], AF.Silu, [pbuf], [B(f"SGX{i}")]))
        self.xattn(l)
        side = []

        self.P.barrier()
        steps = 0
        for hp in range(3):
            gens = [self.gdn_head(j, 4 * hp + i_, self.slots[i_]) for i_ in range(4)]
            done = [False] * 4
            while not all(done):
                for gi, g in enumerate(gens):
                    if not done[gi]:
                        try:
                            next(g)
                        except StopIteration:
                            done[gi] = True
                steps += 1
                if side and steps % 40 == 20:
                    side.pop(0)()
        while side:
            side.pop(0)()
        self.out_proj(l)

    for k_, v_ in list(locals().items()):
        if callable(v_):
            setattr(Builder, k_, v_)


_gdn_methods()


SEQ_FULL = 8192
N_CORES = 8


def kernel(**inputs):
    L = SEQ_FULL
    bld = Builder(L, layers=(0, 1, 2, 3), do_final=True, mix=True)
    nc = bld.build()
    shared = host_layout_mix(inputs)
    base = host_layout(inputs, 0, 0, L)
    in_maps = []
    for c in range(N_CORES):
        b = c % 4
        m = dict(base)
        m.update(shared)
        m["xT"] = np.ascontiguousarray(np.asarray(inputs["x"], dtype=np.float32)[b].T)
        m["memT"] = np.ascontiguousarray(np.asarray(inputs["mem"], dtype=np.float32)[b].T)
        in_maps.append(m)
    res = run_bass_kernel_spmd(nc, in_maps, core_ids=list(range(N_CORES)))
    out = np.stack([np.ascontiguousarray(res.results[b]["outT"].T) for b in range(4)], axis=0)
    return out.astype(np.float32)
```

```python
import math
from contextlib import ExitStack

import numpy as np
import concourse.bass as bass
import concourse.mybir as mybir
from concourse.bass_utils import run_bass_kernel_spmd

F32 = mybir.dt.float32
BF16 = mybir.dt.bfloat16
AF = mybir.ActivationFunctionType
ALU = mybir.AluOpType
AX = mybir.AxisListType

EPOCH = 20000
DEBUG_NAMES = None
SAME_ENGINE_SYNC = True
N_DMA_SEMS = 32
N_SW_SEMS = 8
ENGINES = ("tensor", "vector", "scalar", "gpsimd", "sync")


class Buf:
    __slots__ = ("name", "w", "rs")

    def __init__(self, name):
        self.name = name
        self.w = None
        self.rs = []


class Op:
    __slots__ = ("eng", "fn", "reads", "writes", "dma", "waits", "signal", "tok", "idx", "dsem", "src")

    def __init__(self, eng, fn, reads, writes, dma):
        self.eng = eng
        self.fn = fn
        self.reads = reads
        self.writes = writes
        self.dma = dma
        self.waits = []
        self.signal = False
        self.tok = None
        self.dsem = None


class Prog:
    def __init__(self, nc):
        self.nc = nc
        self.ops = []

    def op(self, eng, fn, reads=(), writes=(), dma=False):
        o = Op(eng, fn, tuple(reads), tuple(writes), dma)
        o.idx = len(self.ops)
        import sys as _s
        fr = _s._getframe(1)
        o.src = []
        while fr is not None and len(o.src) < 4:
            o.src.append(fr.f_lineno)
            fr = fr.f_back
        self.ops.append(o)
        return o

    def mm(self, fn, reads, writes):
        return self.op("tensor", fn, reads, writes)

    def dve(self, fn, reads, writes):
        return self.op("vector", fn, reads, writes)

    def act(self, fn, reads, writes):
        return self.op("scalar", fn, reads, writes)

    def pool(self, fn, reads, writes):
        return self.op("gpsimd", fn, reads, writes)

    def dma(self, eng, out, in_, reads, writes):
        return self.op(eng, lambda e: e.dma_start(out=out, in_=in_), reads, writes, dma=True)

    def barrier(self):
        for e in ENGINES:
            o = self.op(e, None)
            o.dsem = "barrier"

    def analyse(self):
        ops = self.ops
        last_real = {}
        last_dmas = []
        seen = {e: {} for e in ENGINES}
        dma_rr = 0
        sw_rr = 0
        dma_last = [None] * N_DMA_SEMS
        for o in ops:
            deps = set()
            for b in o.reads:
                if b.w is not None:
                    deps.add(b.w)
            for b in o.writes:
                if b.w is not None:
                    deps.add(b.w)
                for r in b.rs:
                    deps.add(r)
            if o.dsem == "barrier":
                for e2, i2 in last_real.items():
                    if e2 != o.eng:
                        deps.add(i2)
                for i2 in last_dmas:
                    deps.add(i2)
            if o.dma:
                if o.eng == "gpsimd":
                    k = N_DMA_SEMS - N_SW_SEMS + sw_rr % N_SW_SEMS
                    sw_rr += 1
                else:
                    k = dma_rr % (N_DMA_SEMS - N_SW_SEMS)
                    dma_rr += 1
                o.dsem = k
                if dma_last[k] is not None:
                    deps.add(dma_last[k])
                dma_last[k] = o.idx
            sn = seen[o.eng]
            best = {}
            dl = []
            for d in deps:
                p = ops[d]
                if p.dma:
                    dl.append(d)
                elif best.get(p.eng, -1) < d:
                    best[p.eng] = d
            for d in sorted(dl + list(best.values())):
                p = ops[d]
                if p.dma:
                    key = ("dma", p.idx)
                    if key in sn:
                        continue
                    sn[key] = True
                    o.waits.append(d)
                else:
                    if p.eng == o.eng and (p.eng == "tensor" or not SAME_ENGINE_SYNC):
                        continue
                    if sn.get(p.eng, -1) >= d:
                        continue
                    sn[p.eng] = d
                    p.signal = True
                    o.waits.append(d)
            if o.fn is None:
                continue
            if o.dma:
                last_dmas.append(o.idx)
                if len(last_dmas) > N_DMA_SEMS:
                    last_dmas.pop(0)
            else:
                last_real[o.eng] = o.idx
            for b in o.reads:
                b.rs.append(o.idx)
            for b in o.writes:
                b.w = o.idx
                b.rs = []
        cnt = {e: 0 for e in ENGINES}
        dcnt = [0] * N_DMA_SEMS
        self.n_epochs = {e: 1 for e in ENGINES}
        for o in ops:
            if o.dma:
                dcnt[o.dsem] += 16
                o.tok = ("d", o.dsem, dcnt[o.dsem])
            elif o.signal:
                c = cnt[o.eng]
                cnt[o.eng] += 1
                ep = c // EPOCH
                self.n_epochs[o.eng] = max(self.n_epochs[o.eng], ep + 1)
                o.tok = ("e", o.eng, ep, c % EPOCH + 1)

    def emit(self):
        nc = self.nc
        self.analyse()
        with ExitStack() as es:
            esem = {}
            for e in ENGINES:
                esem[e] = [es.enter_context(nc.semaphore(f"s_{e}_{i}")) for i in range(self.n_epochs[e])]
            dsem = [es.enter_context(nc.semaphore(f"s_dma_{i}")) for i in range(N_DMA_SEMS)]
            block = es.enter_context(nc.Block())
            by_eng = {e: [o for o in self.ops if o.eng == e] for e in ENGINES}
            ops = self.ops

            def run(eng_name):
                def body(eng):
                    for o in by_eng[eng_name]:
                        for d in o.waits:
                            t = ops[d].tok
                            if t[0] == "d":
                                eng.wait_ge(dsem[t[1]], t[2])
                            else:
                                eng.wait_ge(esem[t[1]][t[2]], t[3])
                        if o.fn is None:
                            continue
                        try:
                            inst = o.fn(eng)
                        except Exception:
                            print("FAILED OP at lines", o.src, "engine", o.eng)
                            raise
                        if DEBUG_NAMES is not None:
                            try:
                                DEBUG_NAMES.append((str(getattr(inst, "name", None) or getattr(getattr(inst, "ins", None), "name", None)), o.src, o.eng))
                            except Exception:
                                pass
                        if o.dma:
                            inst.then_inc(dsem[o.dsem], 16)
                        elif o.signal:
                            inst.then_inc(esem[o.eng][o.tok[2]], 1)
                return body

            block.tensor(run("tensor"))
            block.vector(run("vector"))
            block.scalar(run("scalar"))
            block.gpsimd(run("gpsimd"))
            block.sync(run("sync"))


D = 1024
KT = 8
MEM = 256
TW = 1536
XW = 512
S5_IN = 4096
GDN_IN = 5656
TT = 512
NCH = 64
EPS = 1e-6
ISQ = 1.0 / math.sqrt(128.0)
STAGGER = 25


class Builder:
    def __init__(self, L, layers=(0, 1, 2, 3), do_final=True, mix=True):
        self.L = L
        self.n_tiles = L // TT
        self.layers = tuple(layers)
        self.do_final = do_final
        self.mix = mix
        self.truncate = None
        self.nc = bass.Bass("TRN2", target_bir_lowering=False)
        self.P = Prog(self.nc)
        self.es = ExitStack()
        self.bufs = {}

    def din(self, name, shape):
        return self.nc.dram_tensor(name, list(shape), F32, kind="ExternalInput").ap()

    def dscr(self, name, shape, dt=BF16):
        return self.nc.dram_tensor(name, list(shape), dt, kind="Internal").ap()

    def sb(self, name, shape, dt):
        return self.es.enter_context(self.nc.sbuf_tensor(name, list(shape), dt))

    def ps(self, name, shape, dt=F32):
        return self.es.enter_context(self.nc.psum_tensor(name, list(shape), dt))

    def B(self, name):
        b = self.bufs.get(name)
        if b is None:
            b = self.bufs[name] = Buf(name)
        return b

    def MM(self, out, lhsT, rhs, start, stop, reads, writes, **kw):
        self.P.mm(lambda e: e.matmul(out, lhsT, rhs, start=start, stop=stop, **kw), reads, writes)

    def TR(self, out, in_, ident, reads, writes):
        self.P.mm(lambda e: e.transpose(out, in_, ident), reads, writes)

    def ACT(self, out, in_, func, reads, writes, **kw):
        self.P.act(lambda e: e.activation(out=out, in_=in_, func=func, **kw), reads, writes)

    def TTo(self, eng, out, in0, in1, op, reads, writes):
        self.P.op(eng, lambda e: e.tensor_tensor(out=out, in0=in0, in1=in1, op=op), reads, writes)

    def TS(self, eng, out, in0, s1, s2, op0, op1, reads, writes):
        if op1 is None:
            self.P.op(eng, lambda e: e.tensor_scalar(out=out, in0=in0, scalar1=s1, scalar2=None, op0=op0), reads, writes)
        else:
            self.P.op(eng, lambda e: e.tensor_scalar(out=out, in0=in0, scalar1=s1, scalar2=s2, op0=op0, op1=op1), reads, writes)

    def STT(self, eng, out, in0, scalar, in1, op0, op1, reads, writes):
        self.P.op(eng, lambda e: e.scalar_tensor_tensor(out=out, in0=in0, scalar=scalar, in1=in1, op0=op0, op1=op1),
                  reads, writes)

    def CP(self, eng, out, in_, reads, writes):
        if eng == "scalar":
            self.P.op(eng, lambda e: e.activation(out=out, in_=in_, func=AF.Copy), reads, writes)
        else:
            self.P.op(eng, lambda e: e.tensor_copy(out=out, in_=in_), reads, writes)

    def MS(self, eng, ap, val, writes):
        self.P.op(eng, lambda e: e.memset(ap, val), [], writes)

    def RCP(self, out, in_, reads, writes):
        self.P.dve(lambda e: e.reciprocal(out=out, in_=in_), reads, writes)

    def DMA(self, eng, out, in_, reads, writes):
        self.P.dma(eng, out, in_, reads, writes)

    def declare(self):
        L = self.L
        d = self.dr = {}
        d["xT"] = self.din("xT", [D, L])
        d["memT"] = self.din("memT", [D, MEM])
        d["nw"] = self.din("nw", [128, 4, KT])
        d["mnw"] = self.din("mnw", [128, 4, KT])
        d["fnw"] = self.din("fnw", [128, KT])
        d["win_s5"] = self.din("win_s5", [2, 128, KT, S5_IN])
        d["win_gdn"] = self.din("win_gdn", [2, 128, KT, GDN_IN])
        d["wout"] = self.din("wout", [4, 128, 16, D])
        d["wkv"] = self.din("wkv", [4, 128, KT, D])
        d["wglu"] = self.din("wglu", [2, 128, 12, TW])
        d["bglu"] = self.din("bglu", [128, 2, 12])
        self.outT = self.nc.dram_tensor("outT", [D, L], F32, kind="ExternalOutput").ap()
        s = self.sc = {}
        s["win_s5"] = self.dscr("win_s5_b", [2, 128, KT, S5_IN])
        s["win_gdn"] = self.dscr("win_gdn_b", [2, 128, KT, GDN_IN])
        s["wout"] = self.dscr("wout_b", [4, 128, 16, D])
        s["wkv"] = self.dscr("wkv_b", [4, 128, KT, D])
        s["wglu"] = self.dscr("wglu_b", [2, 128, 12, TW])

    def alloc_common(self):
        sb, ps = self.sb, self.ps
        self.X = sb("X", [128, KT, TT], F32)
        self.H = sb("H", [128, KT, TT], BF16)
        self.SQ = sb("SQ", [128, TT], BF16)
        self.RSTD = sb("RSTD", [128, TT], F32)
        self.YM = sb("YM", [128, 12, TT], BF16)
        self.YX = sb("YX", [128, 4, TT], BF16)
        self.QX = sb("QX", [128, 4, TT], BF16)
        self.SGX = sb("SGX", [128, 4, TT], BF16)
        self.SG = sb("SG", [128, 12, TT], BF16)
        self.EX = sb("EX", [128, 2, TT], BF16)
        if not self.mix:
            self.RDEN = sb("RDEN", [128, TT], F32)
        self.OTMP = self.RSTD
        self.KTs = sb("KTs", [128, 4, 4, MEM], BF16)
        self.Vs = sb("Vs", [128, 4, 2, XW], BF16)
        self.ONES = sb("ONES", [128, 128], BF16)
        self.NW = sb("NW", [128, 4, KT], F32)
        self.MNW = sb("MNW", [128, 4, KT], F32)
        self.FNW = sb("FNW", [128, KT], F32)
        self.BGLU = sb("BGLU", [128, 2, 12], F32)
        self.NWB = 4
        self.WB = [sb(f"WB{i}", [128, 8 * 512], BF16) for i in range(self.NWB)]
        self.wb_rr = 0
        self.PS = [ps(f"PS{i}", [128, 512]) for i in range(8)]
        self.ps_rr = 0

    def next_ps(self, lo=0, hi=4):
        i = lo + self.ps_rr % (hi - lo)
        self.ps_rr += 1
        return i

    def load_w(self, src, K, ncols, rbufs):
        i = self.wb_rr % self.NWB
        self.wb_rr += 1
        wb = self.WB[i][:, 0:K * ncols].rearrange("p (k n) -> p k n", k=K)
        eng = "sync"
        self.DMA(eng, wb, src, rbufs, [self.B(f"WB{i}")])
        return wb, self.B(f"WB{i}")

    def prologue(self):
        d, s, B = self.dr, self.sc, self.B
        order = [("wkv", 0), ("wkv", 1), ("wkv", 2), ("wkv", 3), ("win_s5", 0), ("wglu", 0), ("wout", 0),
                 ("win_gdn", 0), ("wout", 1), ("win_s5", 1), ("wglu", 1), ("wout", 2), ("win_gdn", 1), ("wout", 3)]
        kdim = {"win_s5": KT, "win_gdn": KT, "wout": 16, "wkv": KT, "wglu": 12}
        for name, j in order:
            for k in range(kdim[name]):
                self.DMA("gpsimd", s[name][j, :, k, :], d[name][j, :, k, :], [], [B(f"{name}_b{j}_{k}")])
        self.DMA("sync", self.NW[:], d["nw"], [], [B("NW")])
        self.DMA("sync", self.MNW[:], d["mnw"], [], [B("MNW")])
        self.DMA("sync", self.FNW[:], d["fnw"], [], [B("FNW")])
        self.DMA("sync", self.BGLU[:], d["bglu"], [], [B("BGLU")])
        self.MS("vector", self.ONES[:], 1.0, [B("ONES")])
        if not self.mix:
            self.MS("vector", self.YM[:], 0.0, [B(f"YM{j}") for j in range(12)])
        memT = self.X[:, :, 0:MEM]
        for k in range(KT):
            self.DMA("sync", self.X[:, k, 0:MEM], d["memT"][k * 128:(k + 1) * 128, :], [], [B(f"X{k}")])
        pb = 4
        for k in range(KT):
            self.ACT(self.SQ[:, 0:MEM], self.X[:, k, 0:MEM], AF.Square, [B(f"X{k}")], [B("SQ")])
            self.MM(self.PS[pb][:, 0:MEM], self.ONES[:], self.SQ[:, 0:MEM], k == 0, k == KT - 1,
                    [B("ONES"), B("SQ")], [B(f"PS{pb}")])
        self.rstd_from(self.RSTD[:, 0:MEM], self.PS[pb][:, 0:MEM], 1.0 / D, [B(f"PS{pb}")], [B("RSTD")])
        for l in self.layers:
            for k in range(KT):
                self.STT("vector", self.H[:, k, 0:MEM], self.X[:, k, 0:MEM], self.MNW[:, l, k:k + 1],
                         self.RSTD[:, 0:MEM], ALU.mult, ALU.mult,
                         [B(f"X{k}"), B("MNW"), B("RSTD")], [B(f"H{k}")])
            for half in range(2):
                wb, wbuf = self.load_w(s["wkv"][l, :, :, half * 512:(half + 1) * 512], KT, 512,
                                       [B(f"wkv_b{l}_{k}") for k in range(KT)])
                if half == 0:
                    for h in range(4):
                        p = self.next_ps()
                        for k in range(KT):
                            self.MM(self.PS[p][:, 0:MEM], wb[:, k, h * 128:(h + 1) * 128], self.H[:, k, 0:MEM],
                                    k == 0, k == KT - 1, [wbuf, B(f"H{k}")], [B(f"PS{p}")])
                        self.CP("scalar", self.KTs[:, l, h, :], self.PS[p][:, 0:MEM], [B(f"PS{p}")], [B("KTs")])
                else:
                    for mt in range(2):
                        p = self.next_ps()
                        for k in range(KT):
                            self.MM(self.PS[p][:, :], self.H[:, k, mt * 128:(mt + 1) * 128], wb[:, k, :],
                                    k == 0, k == KT - 1, [wbuf, B(f"H{k}")], [B(f"PS{p}")])
                        self.CP("vector", self.Vs[:, l, mt, :], self.PS[p][:, :], [B(f"PS{p}")], [B("Vs")])

    def rstd_from(self, out, in_, scale, reads, writes, eps=None):
        B = self.B
        self.ACT(out, in_, AF.Ln, list(reads) + [B("EPSC")], writes, scale=scale, bias=(self.eps_ap() if eps is None else eps))
        self.ACT(out, out, AF.Exp, writes, writes, scale=-0.5)

    def eps_ap(self):
        if not hasattr(self, "_eps"):
            self._eps = self.sb("EPSC", [128, 1], F32)
            self.MS("vector", self._eps[:], EPS, [self.B("EPSC")])
        return self._eps[:, 0:1]

    def rms_stats(self):
        B = self.B
        pb = 4
        for k in range(KT):
            self.ACT(self.SQ[:], self.X[:, k, :], AF.Square, [B(f"X{k}")], [B("SQ")])
            self.MM(self.PS[pb][:], self.ONES[:], self.SQ[:], k == 0, k == KT - 1, [B("ONES"), B("SQ")], [B(f"PS{pb}")])
        self.rstd_from(self.RSTD[:], self.PS[pb][:], 1.0 / D, [B(f"PS{pb}")], [B("RSTD")])

    def norm_in(self, l):
        B = self.B
        self.rms_stats()
        for k in range(KT):
            self.STT("vector", self.H[:, k, :], self.X[:, k, :], self.NW[:, l, k:k + 1], self.RSTD[:],
                     ALU.mult, ALU.mult, [B(f"X{k}"), B("NW"), B("RSTD")], [B(f"H{k}")])

    def proj_cols(self, wsrc, wbufs, col0, ncols, evac):
        B = self.B
        wb, wbuf = self.load_w(wsrc[:, :, col0:col0 + ncols], KT, ncols, wbufs)
        nt = (ncols + 127) // 128
        for i in range(nt):
            m = min(128, ncols - i * 128)
            p = self.next_ps()
            for k in range(KT):
                self.MM(self.PS[p][0:m, :], wb[:, k, i * 128:i * 128 + m], self.H[:, k, :], k == 0, k == KT - 1,
                        [wbuf, B(f"H{k}")], [B(f"PS{p}")])
            evac(i, self.PS[p], B(f"PS{p}"), m)

    def xattn(self, l):
        B = self.B
        for h in range(4):
            sp = [5, 6]
            for mt in range(2):
                self.MM(self.PS[sp[mt]][:], self.KTs[:, l, h, mt * 128:(mt + 1) * 128], self.QX[:, h, :], True, True,
                        [B("KTs"), B(f"QX{h}")], [B(f"PS{sp[mt]}")])
                self.ACT(self.EX[:, mt, :], self.PS[sp[mt]][:], AF.Exp, [B(f"PS{sp[mt]}")], [B(f"EX{mt}")], scale=ISQ)
            for mt in range(2):
                self.MM(self.PS[7][:], self.ONES[:], self.EX[:, mt, :], mt == 0, mt == 1,
                        [B("ONES"), B(f"EX{mt}")], [B("PS7")])
            p = self.next_ps()
            for mt in range(2):
                self.MM(self.PS[p][:], self.Vs[:, l, mt, h * 128:(h + 1) * 128], self.EX[:, mt, :], mt == 0, mt == 1,
                        [B("Vs"), B(f"EX{mt}")], [B(f"PS{p}")])
            self.RCP(self.RDEN[:], self.PS[7][:], [B("PS7")], [B("RDEN")])
            self.TTo("vector", self.OTMP[:], self.PS[p][:], self.RDEN[:], ALU.mult, [B(f"PS{p}"), B("RDEN")], [B("RSTD")])
            self.TTo("gpsimd", self.YX[:, h, :], self.OTMP[:], self.SGX[:, h, :], ALU.mult,
                     [B("RSTD"), B(f"SGX{h}")], [B(f"YX{h}")])

    def out_proj(self, l):
        B, s = self.B, self.sc
        rb = [B(f"YM{j}") for j in range(12)] + [B(f"YX{h}") for h in range(4)]
        for half in range(4):
            wb, wbuf = self.load_w(s["wout"][l, :, :, half * 256:(half + 1) * 256], 16, 256,
                                   [B(f"wout_b{l}_{k}") for k in range(16)])
            for i in range(2):
                m = half * 2 + i
                p = self.next_ps()
                for k in range(16):
                    rhs = self.YM[:, k, :] if k < 12 else self.YX[:, k - 12, :]
                    self.MM(self.PS[p][:], wb[:, k, i * 128:(i + 1) * 128], rhs, k == 0, k == 15,
                            [wbuf, rb[k]], [B(f"PS{p}")])
                self.TTo("vector", self.X[:, m, :], self.X[:, m, :], self.PS[p][:], ALU.add,
                         [B(f"X{m}"), B(f"PS{p}")], [B(f"X{m}")])

    def final_norm_store(self, t):
        B = self.B
        self.rms_stats()
        OUT = self.H_as_f32()
        for k in range(KT):
            ob = [B(f"H{2 * (k % 2) + q_}") for q_ in range(2)]
            self.STT("vector", OUT[k % 2], self.X[:, k, :], self.FNW[:, k:k + 1], self.RSTD[:], ALU.mult, ALU.mult,
                     [B(f"X{k}"), B("FNW"), B("RSTD")], ob)
            self.DMA("sync", self.outT[k * 128:(k + 1) * 128, t * TT:(t + 1) * TT], OUT[k % 2], ob,
                     [B(f"out_{t}_{k}")])
            self.out_bufs.append(B(f"out_{t}_{k}"))

    def H_as_f32(self):
        if not hasattr(self, "_outf"):
            self._outf = self.H[:, 0:4, :].rearrange("p k n -> p (k n)").bitcast(F32).rearrange("p (a n) -> p a n", a=2)
        return [self._outf[:, 0, :], self._outf[:, 1, :]]

    def alloc_s5(self):
        sb = self.sb
        self.U = self.YM

    def s5_layer(self, l):
        B, s = self.B, self.sc
        j = l // 2
        self.norm_in(l)
        wbufs = [B(f"win_s5_b{j}_{k}") for k in range(KT)]

        def evac(c0):
            def f(i, ps, pbuf, m):
                gi = c0 // 128 + i
                if gi < 12:
                    self.CP("vector" if gi % 2 else "scalar",
                            self.U[:, gi, :].rearrange("p (s c) -> p s c", s=8),
                            ps[:, :].rearrange("p (c s) -> p s c", s=8), [pbuf], [B(f"YM{gi}")])
                elif gi < 24:
                    self.ACT(self.SG[:, gi - 12, :], ps[:, :], AF.Silu, [pbuf], [B(f"SG{gi - 12}")])
                elif gi < 28:
                    self.CP("vector", self.QX[:, gi - 24, :], ps[:, :], [pbuf], [B(f"QX{gi - 24}")])
                else:
                    self.ACT(self.SGX[:, gi - 28, :], ps[:, :], AF.Silu, [pbuf], [B(f"SGX{gi - 28}")])
            return f

        order = [0, 512, 1024, 3072, 3584, 1536, 2048, 2560]
        for c0 in order[:3]:
            self.proj_cols(s["win_s5"][j], wbufs, c0, 512, evac(c0))
        if self.mix:
            self.s5_state_part(j)
        for c0 in order[3:]:
            self.proj_cols(s["win_s5"][j], wbufs, c0, 512, evac(c0))
        self.xattn(l)
        if self.mix:
            self.s5_out_part(j)
        self.out_proj(l)

    def build(self):
        B = self.B
        self.declare()
        self.alloc_common()
        self.alloc_s5()
        if self.mix:
            self.alloc_s5_mix()
            self.alloc_gdn()
        self.out_bufs = []
        self.marks = {}
        self.prologue()
        self.marks["prologue"] = len(self.P.ops)
        if self.mix:
            self.s5_precompute()
            self.gdn_consts()
        self.marks["precompute"] = len(self.P.ops)
        for t in range(self.n_tiles):
            for k in range(KT):
                self.DMA("sync", self.X[:, k, :], self.dr["xT"][k * 128:(k + 1) * 128, t * TT:(t + 1) * TT],
                         [], [B(f"X{k}")])
            for l in self.layers:
                if l % 2 == 0:
                    self.s5_layer(l)
                else:
                    self.gdn_layer(l)
                self.P.barrier()
            if self.do_final:
                self.final_norm_store(t)
            else:
                for k in range(KT):
                    self.DMA("sync", self.outT[k * 128:(k + 1) * 128, t * TT:(t + 1) * TT], self.X[:, k, :],
                             [B(f"X{k}")], [B(f"out_{t}_{k}")])
                    self.out_bufs.append(B(f"out_{t}_{k}"))
        self.marks["end"] = len(self.P.ops)
        if self.truncate is not None:
            del self.P.ops[self.truncate:]
            self.out_bufs = []
            self.DMA("sync", self.outT[0:128, 0:TT], self.X[:, 0, :], [B("X0")], [B("out_dbg")])
            self.out_bufs.append(B("out_dbg"))
        self.P.op("sync", None, reads=self.out_bufs)
        self.P.emit()
        self.es.close()
        return self.nc


def _kt(w, K):
    return np.ascontiguousarray(w.reshape(K, 128, -1).transpose(1, 0, 2))


def host_layout(inp, b, t0, L):
    f = lambda a: np.ascontiguousarray(np.asarray(a, dtype=np.float32))
    m = {}
    m["xT"] = f(np.asarray(inp["x"])[b, t0:t0 + L, :].T)
    m["memT"] = f(np.asarray(inp["mem"])[b].T)
    m["nw"] = f(np.asarray(inp["norm_w"]).reshape(4, KT, 128).transpose(2, 0, 1))
    m["mnw"] = f(np.asarray(inp["mem_norm_w"]).reshape(4, KT, 128).transpose(2, 0, 1))
    m["fnw"] = f(np.asarray(inp["final_norm_w"]).reshape(KT, 128).T)
    m["win_s5"] = f(np.stack([_kt(np.asarray(inp["s5_w_in"])[j], KT) for j in range(2)]))
    m["win_gdn"] = f(np.stack([_kt(np.asarray(inp["gdn_w_in"])[j], KT) for j in range(2)]))
    m["wout"] = f(np.stack([_kt(np.asarray(inp["w_out"])[i], 16) for i in range(4)]))
    m["wkv"] = f(np.stack([_kt(np.asarray(inp["w_mem_kv"])[i], KT) for i in range(4)]))
    m["wglu"] = f(np.stack([_kt(np.asarray(inp["s5_w_glu"])[j], 12) for j in range(2)]))
    m["bglu"] = f(np.asarray(inp["s5_b_glu"]).reshape(2, 12, 128).transpose(2, 0, 1))
    return m


def host_layout_mix(inp):
    f = lambda a: np.ascontiguousarray(np.asarray(a, dtype=np.float32))
    m = {}
    lr = np.asarray(inp["s5_lambda_re"]); li = np.asarray(inp["s5_lambda_im"]); ls = np.asarray(inp["s5_log_step"])
    sp = lambda a: a.reshape(2, 48, 2, 64).transpose(2, 3, 0, 1).reshape(128, 2, 48)
    m["lamr_sp"] = f(sp(lr)); m["lami_sp"] = f(sp(li))
    m["lst_sp"] = f(sp(np.broadcast_to(ls[:, :, None], (2, 96, 64))))
    spc = lambda a: a.reshape(2, 48, 2, 16, 64).transpose(2, 4, 0, 1, 3).reshape(128, 2, 48, 16)
    m["cre_sp"] = f(spc(np.asarray(inp["s5_c_re"]))); m["cim_sp"] = f(spc(np.asarray(inp["s5_c_im"])))
    spb = lambda a: a.reshape(2, 48, 2, 64, 16).transpose(2, 3, 0, 1, 4).reshape(128, 2, 48, 16)
    m["bre_sp"] = f(spb(np.asarray(inp["s5_b_re"]))); m["bim_sp"] = f(spb(np.asarray(inp["s5_b_im"])))
    cpl = lambda a: np.broadcast_to(a.reshape(2, 12, 8, 1, 64), (2, 12, 8, 16, 64)).transpose(2, 3, 0, 1, 4).reshape(128, 2, 12, 64)
    m["lamr_cp"] = f(cpl(lr)); m["lami_cp"] = f(cpl(li))
    m["lst_cp"] = f(cpl(np.broadcast_to(ls[:, :, None], (2, 96, 64))))
    cpb = lambda a: a.reshape(2, 12, 8, 64, 16).transpose(2, 4, 0, 1, 3).reshape(128, 2, 12, 64)
    m["bre_cp"] = f(cpb(np.asarray(inp["s5_b_re"]))); m["bim_cp"] = f(cpb(np.asarray(inp["s5_b_im"])))
    m["d_cp"] = f(np.asarray(inp["s5_d"]).reshape(2, 12, 128).transpose(2, 0, 1))
    m.update(host_layout_gdn(inp))
    return m


PI = math.pi


def _s5_methods():
    def declare_mix(self):
        d = self.dr
        for n, shp in (("lamr_sp", [128, 2, 48]), ("lami_sp", [128, 2, 48]), ("lst_sp", [128, 2, 48]),
                       ("cre_sp", [128, 2, 48, 16]), ("cim_sp", [128, 2, 48, 16]),
                       ("bre_sp", [128, 2, 48, 16]), ("bim_sp", [128, 2, 48, 16]),
                       ("lamr_cp", [128, 2, 12, 64]), ("lami_cp", [128, 2, 12, 64]), ("lst_cp", [128, 2, 12, 64]),
                       ("bre_cp", [128, 2, 12, 64]), ("bim_cp", [128, 2, 12, 64]), ("d_cp", [128, 2, 12])):
            d[n] = self.din(n, shp)
        self.sc["Ka"] = self.dscr("Ka_d", [2, 12, 128, 8 * 128])
        self.sc["Wb"] = self.dscr("Wb_d", [2, 12, 128, 8 * 2 * 128])
        self.sc["Wc"] = self.dscr("Wc_d", [2, 12, 128, 4 * 8 * 2 * 32])
        self.declare_gdn()

    def alloc_s5_mix(self):
        sb = self.sb
        self.declare_mix()
        A = self.ARENA = sb("ARENA", [128, 36 * 1024], BF16)
        o = 0

        def carve(nbytes):
            nonlocal o
            v = A[:, o // 2:(o + nbytes) // 2]
            o += nbytes
            return v
        self.RDEN = A[:, 35 * 1024:36 * 1024].bitcast(F32)
        self.Y = carve(12 * TT * 2).rearrange("p (j n) -> p j n", j=12)
        self.XS = carve(2 * 48 * 65 * 4).bitcast(F32).rearrange("p (r g c) -> p r g c", r=2, g=48)
        self.XPb = carve(2 * 48 * 64 * 2).rearrange("p (r g c) -> p r g c", r=2, g=48)
        self.TA = carve(2 * 48 * 4).bitcast(F32).rearrange("p (r g) -> p r g", r=2)
        self.TB = carve(2 * 48 * 4).bitcast(F32).rearrange("p (r g) -> p r g", r=2)
        self.GTS5 = carve(TT * 2)
        self.s5_arena_end = o
        self.U = self.YM
        self.AR2 = sb("AR2", [128, 2, 2, 48], F32)
        self.AIS = sb("AIS", [128, 2, 2, 48], F32)
        self.ST5 = sb("ST5", [128, 2, 2, 48], F32)
        self.IDENT = sb("IDENT", [128, 128], F32)
        self.IDENTB = sb("IDENTB", [128, 128], BF16)
        self.MSP = sb("MSP", [128, 2], F32)
        self.MCP = sb("MCP", [128, 2], F32)
        self.DCP = sb("DCP", [128, 2, 12], F32)

    def s5_precompute(self):
        B, d, s = self.B, self.dr, self.sc
        nc = self.nc
        A = self.ARENA
        o = 0

        def carve(shape, dt=F32):
            nonlocal o
            n = int(np.prod(shape))
            nb = n * (4 if dt == F32 else 2)
            v = A[:, o // 2:(o + nb) // 2]
            o += nb
            if dt == F32:
                v = v.bitcast(F32)
            if len(shape) > 1:
                names = " ".join(f"a{i}" for i in range(len(shape)))
                kw = {f"a{i}": shape[i] for i in range(len(shape) - 1)}
                v = v.rearrange(f"p ({names}) -> p {names}", **kw)
            return v
        ar = B("ARENA")
        V = "vector"
        self.P.pool(lambda e: e.iota(self.IDENT[:].bitcast(mybir.dt.int32), pattern=[[1, 128]], base=0, channel_multiplier=-1),
                    [], [B("IDENT")])
        self.CP(V, self.IDENT[:], self.IDENT[:].bitcast(mybir.dt.int32), [B("IDENT")], [B("IDENT")])
        self.TS(V, self.IDENT[:], self.IDENT[:], 0.0, None, ALU.is_equal, None, [B("IDENT")], [B("IDENT")])
        self.CP(V, self.IDENTB[:], self.IDENT[:], [B("IDENT")], [B("IDENTB")])
        self.MS(V, self.MSP[:], 0.0, [B("MSP")])
        self.MS(V, self.MSP[0:64, 0:1], 1.0, [B("MSP")])
        self.MS(V, self.MSP[64:128, 1:2], 1.0, [B("MSP")])
        onesf = carve([1])
        self.MS(V, onesf, 1.0, [ar])
        self.MS(V, self.MCP[:], 0.0, [B("MCP")])
        for blk in range(8):
            g2 = blk % 2
            self.DMA("sync", self.MCP[blk * 16:(blk + 1) * 16, g2:g2 + 1], onesf[0:16, :], [ar, B("MCP")], [B("MCP")])
        self.DMA("sync", self.DCP[:], d["d_cp"], [], [B("DCP")])
        self.MS(V, self.ST5[:], 0.0, [B("ST5")])
        base_o = o

        def chain(lamr, lami, lst, n, npow, rev):
            lr_ = carve([n]); li_ = carve([n]); dt = carve([n]); t1 = carve([n]); t2 = carve([n]); mag = carve([n])
            sn = carve([n]); cs = carve([n]); cr = carve([n]); ci = carve([n])
            Pr = carve([npow, n]); Pi = carve([npow, n])
            self.DMA("sync", lr_, lamr, [], [ar]); self.DMA("sync", li_, lami, [], [ar]); self.DMA("sync", dt, lst, [], [ar])
            self.ACT(dt, dt, AF.Exp, [ar], [ar])
            self.TTo(V, t1, lr_, dt, ALU.mult, [ar], [ar])
            self.TTo(V, t2, li_, dt, ALU.mult, [ar], [ar])
            self.ACT(mag, t1, AF.Exp, [ar], [ar])
            ni_ = carve([n]); mk = carve([n])

            def sin_of(dst, src, shift):
                self.TS(V, dst, src, shift, None, ALU.add, None, [ar], [ar])
                self.TS(V, mk, dst, 1.0 / (2 * PI), None, ALU.mult, None, [ar], [ar])
                self.CP(V, ni_.bitcast(mybir.dt.int32), mk, [ar], [ar])
                self.CP(V, mk, ni_.bitcast(mybir.dt.int32), [ar], [ar])
                self.STT(V, dst, mk, -2 * PI, dst, ALU.mult, ALU.add, [ar], [ar])
                self.TS(V, mk, dst, PI, None, ALU.is_gt, None, [ar], [ar])
                self.STT(V, dst, mk, -2 * PI, dst, ALU.mult, ALU.add, [ar], [ar])
                self.TS(V, mk, dst, -PI, None, ALU.is_lt, None, [ar], [ar])
                self.STT(V, dst, mk, 2 * PI, dst, ALU.mult, ALU.add, [ar], [ar])
                self.ACT(dst, dst, AF.Sin, [ar], [ar])
            sin_of(sn, t2, 0.0)
            sin_of(cs, t2, 0.5 * PI)
            a1r, a1i = t1, t2
            self.TTo(V, a1r, mag, cs, ALU.mult, [ar], [ar])
            self.TTo(V, a1i, mag, sn, ALU.mult, [ar], [ar])
            nr = carve([n]); den = carve([n]); tt = carve([n])
            self.TS(V, nr, a1r, -1.0, None, ALU.add, None, [ar], [ar])
            self.TTo(V, den, lr_, lr_, ALU.mult, [ar], [ar])
            self.TTo(V, tt, li_, li_, ALU.mult, [ar], [ar])
            self.TTo(V, den, den, tt, ALU.add, [ar], [ar])
            self.RCP(den, den, [ar], [ar])
            self.TTo(V, cr, nr, lr_, ALU.mult, [ar], [ar])
            self.TTo(V, tt, a1i, li_, ALU.mult, [ar], [ar])
            self.TTo(V, cr, cr, tt, ALU.add, [ar], [ar])
            self.TTo(V, cr, cr, den, ALU.mult, [ar], [ar])
            self.TTo(V, ci, a1i, lr_, ALU.mult, [ar], [ar])
            self.TTo(V, tt, nr, li_, ALU.mult, [ar], [ar])
            self.TTo(V, ci, ci, tt, ALU.subtract, [ar], [ar])
            self.TTo(V, ci, ci, den, ALU.mult, [ar], [ar])
            ix = (lambda k: npow - 1 - k) if rev else (lambda k: k)
            self.MS(V, Pr[:, ix(0), :], 1.0, [ar]); self.MS(V, Pi[:, ix(0), :], 0.0, [ar])
            for k in range(1, npow):
                p, q = ix(k - 1), ix(k)
                self.TTo(V, Pr[:, q, :], Pr[:, p, :], a1r, ALU.mult, [ar], [ar])
                self.TTo(V, tt, Pi[:, p, :], a1i, ALU.mult, [ar], [ar])
                self.TTo(V, Pr[:, q, :], Pr[:, q, :], tt, ALU.subtract, [ar], [ar])
                self.TTo(V, Pi[:, q, :], Pr[:, p, :], a1i, ALU.mult, [ar], [ar])
                self.TTo(V, tt, Pi[:, p, :], a1r, ALU.mult, [ar], [ar])
                self.TTo(V, Pi[:, q, :], Pi[:, q, :], tt, ALU.add, [ar], [ar])
            return Pr, Pi, cr, ci

        for j in sorted(set(l // 2 for l in self.layers if l % 2 == 0)):
            o = base_o
            Pr, Pi, cr, ci = chain(d["lamr_sp"][:, j, :], d["lami_sp"][:, j, :], d["lst_sp"][:, j, :], 48, 9, False)
            self.CP(V, self.AR2[:, j, 0, :], Pr[:, 8, :], [ar], [B("AR2")])
            self.CP(V, self.AR2[:, j, 1, :], Pr[:, 8, :], [ar], [B("AR2")])
            self.CP(V, self.AIS[:, j, 1, :], Pi[:, 8, :], [ar], [B("AIS")])
            self.TS(V, self.AIS[:, j, 0, :], Pi[:, 8, :], -1.0, None, ALU.mult, None, [ar], [B("AIS")])
            cre = carve([48, 16]); cim = carve([48, 16]); bre = carve([48, 16]); bim = carve([48, 16])
            for t_, n_ in ((cre, "cre_sp"), (cim, "cim_sp"), (bre, "bre_sp"), (bim, "bim_sp")):
                self.DMA("sync", t_, d[n_][:, j, :, :], [], [ar])
            BBr = carve([48, 16]); BBi = carve([48, 16]); tq = carve([48, 16])
            crb = cr.unsqueeze(2).broadcast_to([128, 48, 16]); cib = ci.unsqueeze(2).broadcast_to([128, 48, 16])
            self.TTo(V, BBr, bre, crb, ALU.mult, [ar], [ar]); self.TTo(V, tq, bim, cib, ALU.mult, [ar], [ar])
            self.TTo(V, BBr, BBr, tq, ALU.subtract, [ar], [ar])
            self.TTo(V, BBi, bim, crb, ALU.mult, [ar], [ar]); self.TTo(V, tq, bre, cib, ALU.mult, [ar], [ar])
            self.TTo(V, BBi, BBi, tq, ALU.add, [ar], [ar])
            BBr32 = carve([48, 32]); BBiN32 = carve([48, 32])
            BBiN = tq
            self.TS(V, BBiN, BBi, -1.0, None, ALU.mult, None, [ar], [ar])
            for g2 in range(2):
                self.TS(V, BBr32[:, :, g2 * 16:(g2 + 1) * 16], BBr, self.MSP[:, g2:g2 + 1], None, ALU.mult, None,
                        [ar, B("MSP")], [ar])
                self.TS(V, BBiN32[:, :, g2 * 16:(g2 + 1) * 16], BBiN, self.MSP[:, g2:g2 + 1], None, ALU.mult, None,
                        [ar, B("MSP")], [ar])
            Er = carve([9, 4, 16]); Ei = carve([9, 4, 16]); te = carve([9, 4, 16])
            Er32 = carve([9, 4, 32]); Ei32 = carve([9, 4, 32])
            WcB = carve([4, 8, 2, 32], BF16)
            KaF = carve([8, 128]); KaB = carve([8, 128], BF16)
            self.MS(V, KaF, 0.0, [ar])
            for jt in range(12):
                g0 = jt * 4
                shp = [128, 9, 4, 16]
                cb = lambda c_: c_[:, g0:g0 + 4, :].unsqueeze(1).broadcast_to(shp)
                pb_ = lambda p_: p_[:, :, g0:g0 + 4].unsqueeze(3).broadcast_to(shp)
                self.TTo(V, Er, cb(cre), pb_(Pr), ALU.mult, [ar], [ar]); self.TTo(V, te, cb(cim), pb_(Pi), ALU.mult, [ar], [ar])
                self.TTo(V, Er, Er, te, ALU.subtract, [ar], [ar])
                self.TTo(V, Ei, cb(cre), pb_(Pi), ALU.mult, [ar], [ar]); self.TTo(V, te, cb(cim), pb_(Pr), ALU.mult, [ar], [ar])
                self.TTo(V, Ei, Ei, te, ALU.add, [ar], [ar])
                for g2 in range(2):
                    self.TS(V, Er32[:, :, :, g2 * 16:(g2 + 1) * 16], Er, self.MSP[:, g2:g2 + 1], None, ALU.mult, None,
                            [ar, B("MSP")], [ar])
                    self.TS(V, Ei32[:, :, :, g2 * 16:(g2 + 1) * 16], Ei, self.MSP[:, g2:g2 + 1], None, ALU.mult, None,
                            [ar, B("MSP")], [ar])
                self.CP(V, WcB[:, :, :, 0, :], Er32[:, 1:9, :, :].rearrange("p k g c -> p g k c"), [ar], [B("WcB")])
                self.TS(V, WcB[:, :, :, 1, :], Ei32[:, 1:9, :, :].rearrange("p k g c -> p g k c"), -1.0, None, ALU.mult, None,
                        [ar], [B("WcB")])
                self.DMA("sync", s["Wc"][j, jt], WcB.rearrange("p g t r c -> p (g t r c)"), [B("WcB")], [B(f"Wc_d{j}_{jt}")])
                pk = 6
                PSK = self.PS[pk][:, 0:256].rearrange("p (t c) -> p t c", t=8)
                for q in range(4):
                    for tau in range(8):
                        self.MM(PSK[32 * q:32 * q + 32, tau, :], BBr32[:, g0 + q, :], Er32[:, tau, q, :], True, False,
                                [ar], [B(f"PS{pk}")], tile_position=(0, 32 * q))
                        self.MM(PSK[32 * q:32 * q + 32, tau, :], BBiN32[:, g0 + q, :], Ei32[:, tau, q, :], False, True,
                                [ar], [B(f"PS{pk}")], tile_position=(0, 32 * q))
                for q in range(4):
                    self.CP(V, KaF[32 * q:32 * q + 32, :, 32 * q:32 * q + 32], PSK[32 * q:32 * q + 32, :, :],
                            [B(f"PS{pk}")], [B("KaF")])
                self.STT(V, KaB[:, 0, :], self.IDENT[:], self.DCP[:, j, jt:jt + 1], KaF[:, 0, :], ALU.mult, ALU.add,
                         [B("IDENT"), B("DCP"), B("KaF")], [B("KaB")])
                self.CP(V, KaB[:, 1:8, :], KaF[:, 1:8, :], [B("KaF")], [B("KaB")])
                self.DMA("sync", s["Ka"][j, jt], KaB.rearrange("p t c -> p (t c)"), [B("KaB")], [B(f"Ka_d{j}_{jt}")])
                assert o <= 72 * 1024, o
            for jt in range(12):
                o = base_o
                n = 64
                Pr, Pi, cr, ci = chain(d["lamr_cp"][:, j, jt, :], d["lami_cp"][:, j, jt, :], d["lst_cp"][:, j, jt, :], n, 8, True)
                bre = carve([n]); bim = carve([n]); BBr = carve([n]); BBi = carve([n]); tq = carve([n])
                self.DMA("sync", bre, d["bre_cp"][:, j, jt, :], [], [ar]); self.DMA("sync", bim, d["bim_cp"][:, j, jt, :], [], [ar])
                self.TTo(V, BBr, bre, cr, ALU.mult, [ar], [ar]); self.TTo(V, tq, bim, ci, ALU.mult, [ar], [ar])
                self.TTo(V, BBr, BBr, tq, ALU.subtract, [ar], [ar])
                self.TTo(V, BBi, bim, cr, ALU.mult, [ar], [ar]); self.TTo(V, tq, bre, ci, ALU.mult, [ar], [ar])
                self.TTo(V, BBi, BBi, tq, ALU.add, [ar], [ar])
                Wr = carve([8, n]); Wi = carve([8, n]); tw = carve([8, n])
                bb = lambda a_: a_.unsqueeze(1).broadcast_to([128, 8, n])
                self.TTo(V, Wr, Pr, bb(BBr), ALU.mult, [ar], [ar]); self.TTo(V, tw, Pi, bb(BBi), ALU.mult, [ar], [ar])
                self.TTo(V, Wr, Wr, tw, ALU.subtract, [ar], [ar])
                self.TTo(V, Wi, Pr, bb(BBi), ALU.mult, [ar], [ar]); self.TTo(V, tw, Pi, bb(BBr), ALU.mult, [ar], [ar])
                self.TTo(V, Wi, Wi, tw, ALU.add, [ar], [ar])
                WbB = carve([8, 2, 128], BF16)
                for ri, W3 in ((0, Wr), (1, Wi)):
                    for g2 in range(2):
                        self.TS(V, WbB[:, :, ri, g2 * 64:(g2 + 1) * 64], W3, self.MCP[:, g2:g2 + 1], None,
                                ALU.mult, None, [ar, B("MCP")], [ar])
                self.DMA("sync", s["Wb"][j, jt], WbB.rearrange("p s r c -> p (s r c)"), [ar], [B(f"Wb_d{j}_{jt}")])
                assert o <= 72 * 1024, o
        fence = [ar, B("WcB"), B("KaB"), B("KaF")]
        for j in (0, 1):
            for jt in range(12):
                for nm in ("Ka_d", "Wb_d", "Wc_d"):
                    if f"{nm}{j}_{jt}" in self.bufs:
                        fence.append(B(f"{nm}{j}_{jt}"))
        fence += [B("MCP"), B("MSP"), B("IDENT"), B("IDENTB")]
        for e_ in ("tensor", "vector", "scalar", "gpsimd", "sync"):
            self.P.op(e_, None, writes=fence)

    def s5_state_part(self, j):
        B, s = self.B, self.sc
        self.CP("gpsimd", self.XS[:, :, :, 0], self.ST5[:, j, :, :], [B("ST5")], [B("XS")])
        for jt in range(12):
            wb, wbuf = self.load_w(s["Wb"][j, jt].rearrange("p (a n) -> p a n", a=1), 1, 2048, [B(f"Wb_d{j}_{jt}")])
            W = wb[:, 0, :].rearrange("p (s r c) -> p s r c", s=8, r=2)
            for ri in range(2):
                for sidx in range(8):
                    for q in range(4):
                        pq = 4 + q
                        PZ = self.PS[pq][:, :].rearrange("p (a r c) -> p a r c", a=4, r=2)
                        self.MM(PZ[:, jt % 4, ri, :], W[32 * q:32 * q + 32, sidx, ri, :],
                                self.U[32 * q:32 * q + 32, jt, sidx * 64:(sidx + 1) * 64], sidx == 0, sidx == 7,
                                [wbuf, B(f"YM{jt}")], [B(f"PS{pq}")], tile_position=(32 * q, 0), skip_group_check=True)
            if jt % 4 == 3:
                jt0 = jt - 3
                for q in range(4):
                    pq = 4 + q
                    PZ = self.PS[pq][:, :].rearrange("p (a r c) -> p a r c", a=4, r=2)
                    self.CP("scalar" if q % 2 else "vector", self.XS[:, :, 4 * jt0 + q:4 * jt0 + q + 13:4, 1:65],
                            PZ.rearrange("p a r c -> p r a c"), [B(f"PS{pq}")], [B("XS")])
        self.marks.setdefault("st_mm", len(self.P.ops))
        E = "gpsimd"
        xs = B("XS")
        for c in range(NCH):
            self.TTo(E, self.TA[:], self.AR2[:, j], self.XS[:, :, :, c], ALU.mult, [B("AR2"), xs], [B("TA")])
            self.TTo(E, self.TB[:, 0, :], self.AIS[:, j, 0, :], self.XS[:, 1, :, c], ALU.mult, [B("AIS"), xs], [B("TB")])
            self.TTo(E, self.TB[:, 1, :], self.AIS[:, j, 1, :], self.XS[:, 0, :, c], ALU.mult, [B("AIS"), xs], [B("TB")])
            self.TTo(E, self.TA[:], self.TA[:], self.TB[:], ALU.add, [B("TA"), B("TB")], [B("TA")])
            self.TTo(E, self.XS[:, :, :, c + 1], self.XS[:, :, :, c + 1], self.TA[:], ALU.add, [xs, B("TA")], [xs])
        self.marks.setdefault("st_scan", len(self.P.ops))
        self.CP("scalar", self.XPb[:], self.XS[:, :, :, 0:64], [xs], [B("XPb")])
        self.CP("gpsimd", self.ST5[:, j, :, :], self.XS[:, :, :, 64], [xs], [B("ST5")])

    def s5_out_part(self, j):
        B, s = self.B, self.sc
        for jt in range(12):
            i = self.wb_rr % self.NWB
            self.wb_rr += 1
            wbuf = B(f"WB{i}")
            KA = self.WB[i][:, 0:1024].rearrange("p (t c) -> p t c", t=8)
            WC = self.WB[i][:, 1024:3072].rearrange("p (g t r c) -> p g t r c", g=4, t=8, r=2)
            self.DMA("sync", self.WB[i][:, 0:1024], s["Ka"][j, jt], [B(f"Ka_d{j}_{jt}")], [wbuf])
            self.DMA("sync", self.WB[i][:, 1024:3072], s["Wc"][j, jt], [B(f"Wc_d{j}_{jt}")], [wbuf])
            p = self.next_ps()
            PY = self.PS[p][:, :].rearrange("p (t c) -> p t c", t=8)
            PYf = self.PS[p][:, :]
            for tau in range(8):
                self.MM(PYf[:, tau * 64:512], KA[:, tau, :], self.U[:, jt, 0:(8 - tau) * 64], tau == 0, False,
                        [wbuf, B(f"YM{jt}")], [B(f"PS{p}")], skip_group_check=True)
            for t in range(8):
                for ri in range(2):
                    for q in range(4):
                        self.MM(PY[32 * q:32 * q + 32, t, :], WC[:, q, t, ri, :], self.XPb[:, ri, 4 * jt + q, :], False,
                                ri == 1, [wbuf, B("XPb")], [B(f"PS{p}")], tile_position=(0, 32 * q), skip_group_check=True)
            self.ACT(self.Y[:, jt, :].rearrange("p (c s) -> p s c", s=8), PY, AF.Gelu_apprx_tanh,
                     [B(f"PS{p}")], [B(f"Y{jt}")])
        self.marks.setdefault("out_mm", len(self.P.ops))
        yb = [B(f"Y{k}") for k in range(12)]
        for ch in range(6):
            wb, wbuf = self.load_w(s["wglu"][j, :, :, ch * 256:(ch + 1) * 256], 12, 256,
                                   [B(f"wglu_b{j}_{k}") for k in range(12)])
            for i in range(2):
                m = ch * 2 + i
                p = self.next_ps()
                for k in range(12):
                    self.MM(self.PS[p][:], wb[:, k, i * 128:(i + 1) * 128], self.Y[:, k, :], k == 0, k == 11,
                            [wbuf, yb[k]], [B(f"PS{p}")])
                self.ACT(self.GTS5, self.PS[p][:], AF.Sigmoid, [B(f"PS{p}"), B("BGLU")], [B("GTS5")],
                         bias=self.BGLU[:, j, m:m + 1])
                self.TTo("vector", self.GTS5, self.GTS5, self.Y[:, m, :], ALU.mult, [B("GTS5"), yb[m]], [B("GTS5")])
                self.TTo("gpsimd", self.YM[:, m, :], self.GTS5, self.SG[:, m, :], ALU.mult, [B("GTS5"), B(f"SG{m}")],
                         [B(f"YM{m}")])

    for k_, v_ in list(locals().items()):
        if callable(v_):
            setattr(Builder, k_, v_)


_s5_methods()


def host_layout_gdn(inp):
    f = lambda a: np.ascontiguousarray(np.asarray(a, dtype=np.float32))
    m = {}
    m["g_alog"] = f(np.asarray(inp["gdn_a_log"]).T)
    m["g_dtb"] = f(np.asarray(inp["gdn_dt_bias"]).T)
    m["g_nw"] = f(np.asarray(inp["gdn_norm_w"]).T)
    cw = np.asarray(inp["gdn_conv_w"])
    m["g_cw"] = f(cw.reshape(2, 4, 24, 128).transpose(3, 0, 2, 1))
    return m


def _gdn_methods():
    def declare_gdn(self):
        d = self.dr
        d["g_alog"] = self.din("g_alog", [12, 2])
        d["g_dtb"] = self.din("g_dtb", [12, 2])
        d["g_nw"] = self.din("g_nw", [128, 2])
        d["g_cw"] = self.din("g_cw", [128, 2, 24, 4])

    def alloc_gdn(self):
        sb = self.sb
        A = self.ARENA
        o = 0

        def carve(nbytes):
            nonlocal o
            v = A[:, o // 2:(o + nbytes) // 2]
            o += nbytes
            return v
        f32v = lambda v, *shape: (v.bitcast(F32) if not shape else v.bitcast(F32))
        self.QKV = carve(24 * TT * 2).rearrange("p (j n) -> p j n", j=24)
        self.XC = [carve(516 * 4).bitcast(F32) for _ in range(2)]
        self.ACC = [carve(TT * 4).bitcast(F32) for _ in range(2)]
        self.Gg = carve(TT * 4).bitcast(F32)
        self.BETA = carve(TT * 4).bitcast(F32)
        self.GCF = carve(TT * 4).bitcast(F32)
        self.E1 = carve(TT * 4).bitcast(F32)
        tm = lambda: carve(48 * 4).bitcast(F32).rearrange("p (a h) -> p a h", a=4)
        self.GT, self.BT, self.GCT, self.GLT, self.SC1, self.SC2 = tm(), tm(), tm(), tm(), tm(), tm()
        m32 = lambda cv: cv(128 * 4).bitcast(F32)
        m16 = lambda cv: cv(128 * 2)

        class Slot:
            pass

        def mk_slot(cv, tag):
            S = Slot()
            S.tag = tag
            S.GRs = cv(TT * 4).bitcast(F32)
            S.EGR = cv(TT * 4).bitcast(F32)
            S.KBT = cv(TT * 2)
            S.QG = cv(TT * 2)
            S.TMP1, S.DEC, S.DL, S.DA = m32(cv), m32(cv), m32(cv), m32(cv)
            S.LTm, S.L0 = m16(cv), m16(cv)
            S.Pb = [m16(cv) for _ in range(2)]
            S.PTb = [m16(cv) for _ in range(2)]
            S.RTb = [m16(cv) for _ in range(2)]
            S.Us = m32(cv)
            S.TMT, S.AT, S.VB, S.KBG, S.KDEC, S.WTs, S.VN, S.ON, S.Sb = (m16(cv) for _ in range(9))
            S.SSQ = cv(4 * 4).bitcast(F32)
            S.JUNK = m16(cv)
            return S
        self.slots = [mk_slot(carve, "a")]
        dead_bf = [v.bitcast(BF16) for v in (self.XC[0], self.XC[1], self.ACC[0], self.ACC[1], self.E1)]
        dpos = [0, 0]

        def carve2(nbytes):
            n = nbytes // 2
            while dpos[0] < len(dead_bf):
                v = dead_bf[dpos[0]]
                if dpos[1] + n <= v.shape[1]:
                    out = v[:, dpos[1]:dpos[1] + n]
                    dpos[1] += n
                    return out
                dpos[0] += 1
                dpos[1] = 0
            return carve(nbytes)
        self.slots.append(mk_slot(carve2, "b"))
        dead_c = [self.QX[:, :, :].rearrange("p a n -> p (a n)"), self.SGX[:, :, :].rearrange("p a n -> p (a n)"),
                  self.EX[:, :, :].rearrange("p a n -> p (a n)"), self.RSTD[:, :].bitcast(BF16), self.SQ[:, :]]
        cpos = [0, 0]

        def carve3(nbytes):
            n = nbytes // 2
            while cpos[0] < len(dead_c):
                v = dead_c[cpos[0]]
                if cpos[1] + n <= v.shape[1]:
                    out = v[:, cpos[1]:cpos[1] + n]
                    cpos[1] += n
                    return out
                cpos[0] += 1
                cpos[1] = 0
            return carve(nbytes)
        self.slots.append(mk_slot(carve3, "c"))
        dead_d = [self.H[:, :, :].rearrange("p a n -> p (a n)")]
        dq = [0, 0]

        def carve4(nbytes):
            n = nbytes // 2
            while dq[0] < len(dead_d):
                v = dead_d[dq[0]]
                if dq[1] + n <= v.shape[1]:
                    out = v[:, dq[1]:dq[1] + n]
                    dq[1] += n
                    return out
                dq[0] += 1
                dq[1] = 0
            return carve(nbytes)
        self.slots.append(mk_slot(carve4, "d"))
        assert o <= 64 * 1024, o
        self.SEL = A[0:12, 32 * 1024:32 * 1024 + 12 * 128 * 2].bitcast(F32).rearrange("p (h m) -> p h m", h=12)
        self.SST = sb("SST", [128, 2, 12, 128], F32)
        self.CONVST = sb("CONVST", [128, 2, 24, 3], F32)
        self.CW = sb("CW", [128, 2, 24, 4], F32)
        self.TRI = sb("TRI", [128, 128], F32)
        self.MSU = sb("MSU", [128, 128], F32)
        self.BLK = sb("BLK", [128, 128], F32)
        self.ONESF = sb("ONESF", [128, 128], F32)
        self.GNW = sb("GNW", [128, 2], F32)
        self.NA = sb("NA", [12, 2], F32)
        self.DTB = sb("DTB", [12, 2], F32)

    def gdn_consts(self):
        B, d = self.B, self.dr
        V = "vector"
        self.MS(V, self.SST[:], 0.0, [B("SST")])
        self.MS(V, self.CONVST[:], 0.0, [B("CONVST")])
        self.MS(V, self.ONESF[:], 1.0, [B("ONESF")])
        self.DMA("sync", self.CW[:], d["g_cw"], [], [B("CW")])
        self.DMA("sync", self.GNW[:], d["g_nw"], [], [B("GNW")])
        self.DMA("sync", self.NA[:], d["g_alog"], [], [B("NA")])
        self.DMA("sync", self.DTB[:], d["g_dtb"], [], [B("DTB")])
        self.ACT(self.NA[:], self.NA[:], AF.Exp, [B("NA")], [B("NA")])
        self.TS(V, self.NA[:], self.NA[:], -1.0, None, ALU.mult, None, [B("NA")], [B("NA")])
        self.P.pool(lambda e: e.iota(self.TRI[:].bitcast(mybir.dt.int32), pattern=[[1, 128]], base=0, channel_multiplier=-1),
                    [], [B("TRI")])
        self.CP(V, self.TRI[:], self.TRI[:].bitcast(mybir.dt.int32), [B("TRI")], [B("TRI")])
        self.TS(V, self.TRI[:], self.TRI[:], 0.0, None, ALU.is_ge, None, [B("TRI")], [B("TRI")])
        self.MS(V, self.TRI[0:64, 64:128], 0.0, [B("TRI")])
        self.TTo(V, self.MSU[:], self.TRI[:], self.IDENT[:], ALU.subtract, [B("TRI"), B("IDENT")], [B("MSU")])
        self.MS(V, self.BLK[:], 0.0, [B("BLK")])
        self.MS(V, self.BLK[0:64, 0:64], 1.0, [B("BLK")])
        self.MS(V, self.BLK[64:128, 64:128], 1.0, [B("BLK")])
        for h in range(12):
            self.TS(V, self.SEL[:, h, :], self.ONESF[0:12, :], self.IDENT[0:12, h:h + 1], None, ALU.mult, None,
                    [B("ONESF"), B("IDENT")], [B("SEL")])

    def slot_ps(self, S):
        k = "abcd".index(S.tag)
        S.ps_n = getattr(S, "ps_n", 0) + 1
        return 2 * k + S.ps_n % 2

    def gdn_head(self, j, h, S):
        B = self.B
        V, G_ = "vector", "gpsimd"
        gh = B("GH" + S.tag)
        gp_ = B("GP" + S.tag)
        hq = h // 2
        QT, KTt, VT = self.QKV[:, hq, :], self.QKV[:, 6 + hq, :], self.QKV[:, 12 + h, :]
        bq, bk, bv = B(f"QKV{hq}"), B(f"QKV{6 + hq}"), B(f"QKV{12 + h}")
        Sst = self.SST[:, j, h, :]
        bs = B(f"SST{j}_{h}")
        p = self.slot_ps(S)
        self.MM(self.PS[p][:], self.SEL[:, h, :], self.GCF[0:12, :], True, True, [B("SEL"), B("GCF")], [B(f"PS{p}")])
        yield
        self.CP(V, S.GRs, self.PS[p][:], [B(f"PS{p}")], [gh])
        yield
        self.ACT(S.EGR, self.PS[p][:], AF.Exp, [B(f"PS{p}")], [gh])
        yield
        p = self.slot_ps(S)
        self.MM(self.PS[p][:], self.SEL[:, h, :], self.BETA[0:12, :], True, True, [B("SEL"), B("BETA")], [B(f"PS{p}")])
        yield
        self.TTo(V, S.KBT, KTt, self.PS[p][:], ALU.mult, [bk, B(f"PS{p}")], [gh])
        yield
        self.TTo(G_, S.QG, QT, S.EGR, ALU.mult, [bq, gh], [gh])
        yield
        self.CP("scalar", S.Sb, Sst, [bs], [B("Sb" + S.tag)])
        yield
        for pr in range(4):
            cs = slice(pr * 128, (pr + 1) * 128)
            p = self.slot_ps(S)
            self.MM(self.PS[p][:, 0:128], KTt[:, cs], S.KBT[:, cs], True, True, [bk, gh], [B(f"PS{p}")])
            yield
            self.MM(self.PS[p][:, 128:256], KTt[:, cs], QT[:, cs], True, True, [bk, bq], [B(f"PS{p}")], skip_group_check=True)
            yield
            self.TS(V, S.TMP1, S.GRs[:, cs], self.GCT[:, pr, h:h + 1], 0.0, ALU.subtract, ALU.min,
                    [gh, B("GCT")], [gp_])
            yield
            self.ACT(S.DEC, S.TMP1, AF.Exp, [gp_], [gp_])
            yield
            self.TTo(G_, S.DL, S.DEC, self.MSU[:], ALU.mult, [gp_, B("MSU")], [gp_])
            yield
            self.TTo(G_, S.DA, S.DEC, self.TRI[:], ALU.mult, [gp_, B("TRI")], [gp_])
            yield
            self.TTo(V, S.LTm, self.PS[p][:, 0:128], S.DL, ALU.mult, [B(f"PS{p}"), gp_], [gp_])
            yield
            self.TTo(V, S.AT, self.PS[p][:, 128:256], S.DA, ALU.mult, [B(f"PS{p}"), gp_], [gp_])
            yield
            p = self.slot_ps(S)
            pl16 = self.PS[p][:, :].bitcast(BF16)
            self.TR(pl16[:, 0:128], S.LTm, self.IDENTB[:], [gp_, B("IDENTB")], [B(f"PS{p}")])
            yield
            self.CP("scalar", S.L0, pl16[:, 0:128], [B(f"PS{p}")], [gp_])
            yield
            self.TTo(V, S.RTb[0], self.IDENTB[:], S.LTm, ALU.subtract, [B("IDENTB"), gp_], [gp_])
            yield
            Pc, PTc, RTc = S.L0, S.LTm, S.RTb[0]
            for k in range(5):
                Pn, PTn, RTn = S.Pb[k % 2], S.PTb[k % 2], S.RTb[(k + 1) % 2]
                p = self.slot_ps(S)
                self.MM(self.PS[p][:, 0:128], PTc, Pc, True, True, [gp_], [B(f"PS{p}")])
                yield
                if k < 4:
                    self.MM(self.PS[p][:, 128:256], Pc, PTc, True, True, [gp_], [B(f"PS{p}")], skip_group_check=True)
                    yield
                self.CP("scalar", Pn, self.PS[p][:, 0:128], [B(f"PS{p}")], [gp_])
                yield
                if k < 4:
                    self.CP(V, PTn, self.PS[p][:, 128:256], [B(f"PS{p}")], [gp_])
                    yield
                p2 = self.slot_ps(S)
                self.MM(self.PS[p2][:, 0:128], Pn, RTc, True, True, [gp_], [B(f"PS{p2}")])
                yield
                if k < 4:
                    self.TTo(V, RTn, RTc, self.PS[p2][:, 0:128], ALU.add, [gp_, B(f"PS{p2}")], [gp_])
                    yield
                else:
                    self.TTo(V, S.TMT, RTc, self.PS[p2][:, 0:128], ALU.add, [gp_, B(f"PS{p2}")], [gp_])
                    yield
                Pc, PTc, RTc = Pn, PTn, RTn
            p = self.slot_ps(S)
            pb16 = self.PS[p][:, :].bitcast(BF16)
            self.TR(pb16[:, 0:128], VT[:, cs], self.IDENTB[:], [bv, B("IDENTB")], [B(f"PS{p}")])
            yield
            self.TR(pb16[:, 128:256], KTt[:, cs], self.IDENTB[:], [bk, B("IDENTB")], [B(f"PS{p}")])
            yield
            self.TS(V, S.VB, pb16[:, 0:128], self.BT[:, pr, h:h + 1], None, ALU.mult, None, [B(f"PS{p}"), B("BT")], [gp_])
            yield
            self.TS(V, S.KBG, pb16[:, 128:256], self.SC1[:, pr, h:h + 1], None, ALU.mult, None,
                    [B(f"PS{p}"), B("SC1")], [gp_])
            yield
            self.TS(V, S.KDEC, pb16[:, 128:256], self.SC2[:, pr, h:h + 1], None, ALU.mult, None,
                    [B(f"PS{p}"), B("SC2")], [gp_])
            yield
            p = self.slot_ps(S)
            self.MM(self.PS[p][:, 0:128], S.TMT, S.VB, True, True, [gp_], [B(f"PS{p}")])
            yield
            self.MM(self.PS[p][:, 128:256], S.KBG, S.TMT, True, True, [gp_], [B(f"PS{p}")], skip_group_check=True)
            yield
            self.CP("scalar", S.Us, self.PS[p][:, 0:128], [B(f"PS{p}")], [gp_])
            yield
            self.CP(V, S.WTs, self.PS[p][:, 128:256], [B(f"PS{p}")], [gp_])
            yield
            for c in range(2):
                rs = slice(c * 64, (c + 1) * 64)
                gcol = pr * 128 + c * 64
                pw = self.slot_ps(S)
                self.MM(self.PS[pw][rs, 0:128], S.WTs[:, rs], S.Sb, True, True, [gp_, B("Sb" + S.tag)], [B(f"PS{pw}")],
                        tile_position=(0, 64 * c))
                yield
                self.TTo(V, S.VN[rs, :], S.Us[rs, :], self.PS[pw][rs, 0:128], ALU.subtract,
                         [gp_, B(f"PS{pw}")], [B("VN" + S.tag)])
                yield
                po = self.slot_ps(S)
                self.MM(self.PS[po][rs, 0:128], S.QG[:, gcol:gcol + 64], S.Sb, True, False, [gh, B("Sb" + S.tag)],
                        [B(f"PS{po}")], tile_position=(0, 64 * c))
                yield
                self.MM(self.PS[po][rs, 0:128], S.AT[rs, rs], S.VN[rs, :], False, True, [gp_, B("VN" + S.tag)],
                        [B(f"PS{po}")], tile_position=(64 * c, 64 * c))
                yield
                pu = self.slot_ps(S)
                self.MM(self.PS[pu][:, 0:128], S.KDEC[rs, :], S.VN[rs, :], True, True, [gp_, B("VN" + S.tag)],
                        [B(f"PS{pu}")], tile_position=(64 * c, 0))
                yield
                self.STT(V, Sst, Sst, S.EGR[:, gcol + 63:gcol + 64], self.PS[pu][:, 0:128], ALU.mult, ALU.add,
                         [bs, gh, B(f"PS{pu}")], [bs])
                yield
                self.CP("scalar", S.Sb, Sst, [bs], [B("Sb" + S.tag)])
                yield
                self.ACT(S.JUNK[rs, :], self.PS[po][rs, 0:128], AF.Square, [B(f"PS{po}")], [B("JUNK" + S.tag), B("SSQ" + S.tag)],
                         accum_out=S.SSQ[rs, 0:1])
                yield
                self.ACT(S.SSQ[rs, 2:3], S.SSQ[rs, 0:1], AF.Ln, [B("SSQ" + S.tag), B("EPSC")], [B("SSQ" + S.tag)],
                         scale=1.0 / 128, bias=self._eps[rs, 0:1])
                yield
                self.ACT(S.SSQ[rs, 2:3], S.SSQ[rs, 2:3], AF.Exp, [B("SSQ" + S.tag)], [B("SSQ" + S.tag)], scale=-0.5)
                yield
                self.TS(V, S.ON[rs, :], self.PS[po][rs, 0:128], S.SSQ[rs, 2:3], None, ALU.mult, None,
                        [B(f"PS{po}"), B("SSQ" + S.tag)], [B("ON" + S.tag)])
                yield
            p = self.slot_ps(S)
            pb16 = self.PS[p][:, :].bitcast(BF16)
            self.TR(pb16[:, 0:128], S.ON, self.IDENTB[:], [B("ON" + S.tag), B("IDENTB")], [B(f"PS{p}")])
            yield
            self.STT(V, self.YM[:, h, cs], pb16[:, 0:128], self.GNW[:, j:j + 1], self.SG[:, h, cs], ALU.mult, ALU.mult,
                     [B(f"PS{p}"), B("GNW"), B(f"SG{h}")], [B(f"YM{h}")])
            yield


    def gdn_layer(self, l):
        B, s = self.B, self.sc
        j = l // 2
        V, G_ = "vector", "gpsimd"
        self.norm_in(l)
        wsrc = s["win_gdn"][j]
        wbufs = [B(f"win_gdn_b{j}_{k}") for k in range(KT)]
        ga = B("GA")

        def ev_a(i, ps, pbuf, m):
            self.ACT(self.E1[0:12, :], ps[0:12, :], AF.Exp, [pbuf, B("DTB")], [ga], bias=self.DTB[:, j:j + 1])
            self.ACT(self.E1[0:12, :], self.E1[0:12, :], AF.Ln, [ga, B("ONESF")], [ga], bias=self.ONESF[0:12, 0:1])
            self.TS(V, self.Gg[0:12, :], self.E1[0:12, :], self.NA[:, j:j + 1], None, ALU.mult, None, [ga, B("NA")], [B("Gg")])

        def ev_b(i, ps, pbuf, m):
            self.ACT(self.BETA[0:12, :], ps[0:12, :], AF.Sigmoid, [pbuf], [B("BETA")])
        self.proj_cols(wsrc, wbufs, 3072, 12, ev_a)
        self.proj_cols(wsrc, wbufs, 3084, 12, ev_b)
        p = self.next_ps()
        for pr in range(4):
            self.TR(self.PS[p][:, pr * 12:(pr + 1) * 12], self.Gg[0:12, pr * 128:(pr + 1) * 128], self.IDENT[0:12, 0:12],
                    [B("Gg"), B("IDENT")], [B(f"PS{p}")])
            self.TR(self.PS[p][:, 48 + pr * 12:48 + (pr + 1) * 12], self.BETA[0:12, pr * 128:(pr + 1) * 128],
                    self.IDENT[0:12, 0:12], [B("BETA"), B("IDENT")], [B(f"PS{p}")])
        self.CP(V, self.GT.rearrange("p a h -> p (a h)"), self.PS[p][:, 0:48], [B(f"PS{p}")], [B("GT")])
        self.CP(V, self.BT.rearrange("p a h -> p (a h)"), self.PS[p][:, 48:96], [B(f"PS{p}")], [B("BT")])
        p = self.next_ps()
        for pr in range(4):
            self.MM(self.PS[p][:, pr * 12:(pr + 1) * 12], self.TRI[:], self.GT[:, pr, :], True, True,
                    [B("TRI"), B("GT")], [B(f"PS{p}")])
            self.MM(self.PS[p][:, 48 + pr * 12:48 + (pr + 1) * 12], self.BLK[:], self.GT[:, pr, :], True, True,
                    [B("BLK"), B("GT")], [B(f"PS{p}")], skip_group_check=True)
        self.CP(V, self.GCT.rearrange("p a h -> p (a h)"), self.PS[p][:, 0:48], [B(f"PS{p}")], [B("GCT")])
        self.CP(V, self.GLT.rearrange("p a h -> p (a h)"), self.PS[p][:, 48:96], [B(f"PS{p}")], [B("GLT")])
        p = self.next_ps()
        for pr in range(4):
            self.MM(self.PS[p][0:12, pr * 128:(pr + 1) * 128], self.GT[:, pr, :], self.TRI[:], True, True,
                    [B("TRI"), B("GT")], [B(f"PS{p}")], skip_group_check=True)
        self.CP(V, self.GCF[0:12, :], self.PS[p][0:12, :], [B(f"PS{p}")], [B("GCF")])
        fl = lambda t_: t_.rearrange("p a h -> p (a h)")
        self.ACT(fl(self.SC1), fl(self.GCT), AF.Exp, [B("GCT")], [B("SC1")])
        self.TTo(V, fl(self.SC1), fl(self.SC1), fl(self.BT), ALU.mult, [B("SC1"), B("BT")], [B("SC1")])
        self.TTo(V, fl(self.SC2), fl(self.GLT), fl(self.GCT), ALU.subtract, [B("GLT"), B("GCT")], [B("SC2")])
        self.ACT(fl(self.SC2), fl(self.SC2), AF.Exp, [B("SC2")], [B("SC2")])

        def ev_qkv(c0):
            def f(i, ps, pbuf, m):
                ti = c0 // 128 + i
                xc, acc = self.XC[ti % 2], self.ACC[ti % 2]
                xb, ab = B(f"XC{ti % 2}"), B(f"ACC{ti % 2}")
                self.CP(G_, xc[:, 0:3], self.CONVST[:, j, ti, :], [B("CONVST")], [xb])
                self.CP("scalar", xc[:, 3:515], ps[:, :], [pbuf], [xb])
                self.CP(G_, self.CONVST[:, j, ti, :], xc[:, 512:515], [xb], [B("CONVST")])
                self.TS(V, acc, xc[:, 0:512], self.CW[:, j, ti, 0:1], None, ALU.mult, None, [xb, B("CW")], [ab])
                for k in range(1, 4):
                    self.STT(V, acc, xc[:, k:k + 512], self.CW[:, j, ti, k:k + 1], acc, ALU.mult, ALU.add,
                             [xb, B("CW"), ab], [ab])
                self.ACT(self.QKV[:, ti, :], acc, AF.Silu, [ab], [B(f"QKV{ti}")])
            return f
        for c in range(6):
            self.proj_cols(wsrc, wbufs, c * 512, 512, ev_qkv(c * 512))
            if c == 2:
                for ti in range(12):
                    self.ACT(self.SQ[:], self.QKV[:, ti, :], AF.Square, [B(f"QKV{ti}")], [B("SQ")])
                    pp = self.next_ps()
                    self.MM(self.PS[pp][:], self.ONES[:], self.SQ[:], True, True, [B("ONES"), B("SQ")], [B(f"PS{pp}")])
                    self.rstd_from(self.RSTD[:], self.PS[pp][:], 1.0, [B(f"PS{pp}")], [B("RSTD")])
                    if ti < 6:
                        self.STT(V, self.QKV[:, ti, :], self.QKV[:, ti, :], ISQ, self.RSTD[:], ALU.mult, ALU.mult,
                                 [B(f"QKV{ti}"), B("RSTD")], [B(f"QKV{ti}")])
                    else:
                        self.TTo(V, self.QKV[:, ti, :], self.QKV[:, ti, :], self.RSTD[:], ALU.mult,
                                 [B(f"QKV{ti}"), B("RSTD")], [B(f"QKV{ti}")])

        def ev_gate(c):
            def f(i, ps, pbuf, m):
                self.ACT(self.SG[:, 4 * c + i, :], ps[:, :], AF.Silu, [pbuf], [B(f"SG{4 * c + i}")])
            return f
        for c in range(3):
            self.proj_cols(wsrc, wbufs, 3096 + c * 512, 512, ev_gate(c))
        self.proj_cols(wsrc, wbufs, 4632, 512,
                       lambda i, ps, pbuf, m: self.CP(V, self.QX[:, i, :], ps[:, :], [pbuf], [B(f"QX{i}")]))
        self.proj_cols(wsrc, wbufs, 5144, 512,
                       lambda i, ps, pbuf, m: self.ACT(self.SGX[:, i, :], ps[:, :], AF.Silu, [pbuf], [B(f"SGX{i}")]))
        self.xattn(l)
        side = []

        self.P.barrier()
        steps = 0
        pending = list(range(12))
        active = {}
        start_at = {0: 0, 1: STAGGER, 2: 2 * STAGGER, 3: 3 * STAGGER}
        while pending or active:
            for sl in range(4):
                if sl not in active and pending and steps >= start_at[sl]:
                    active[sl] = self.gdn_head(j, pending.pop(0), self.slots[sl])
            for sl in list(active.keys()):
                try:
                    next(active[sl])
                except StopIteration:
                    del active[sl]
            steps += 1
        while side:
            side.pop(0)()
        self.out_proj(l)

    for k_, v_ in list(locals().items()):
        if callable(v_):
            setattr(Builder, k_, v_)


_gdn_methods()


SEQ_FULL = 8192
N_CORES = 8


def kernel(**inputs):
    L = SEQ_FULL
    bld = Builder(L, layers=(0, 1, 2, 3), do_final=True, mix=True)
    nc = bld.build()
    shared = host_layout_mix(inputs)
    base = host_layout(inputs, 0, 0, L)
    in_maps = []
    for c in range(N_CORES):
        b = c % 4
        m = dict(base)
        m.update(shared)
        m["xT"] = np.ascontiguousarray(np.asarray(inputs["x"], dtype=np.float32)[b].T)
        m["memT"] = np.ascontiguousarray(np.asarray(inputs["mem"], dtype=np.float32)[b].T)
        in_maps.append(m)
    res = run_bass_kernel_spmd(nc, in_maps, core_ids=list(range(N_CORES)))
    out = np.stack([np.ascontiguousarray(res.results[b]["outT"].T) for b in range(4)], axis=0)
    return out.astype(np.float32)
```
